# Optimizing a Trainium2 kernel written in Bass

```python
import math
import jax
import jax.numpy as jnp
from jax import lax
import numpy as np

D_MODEL = 2048
BATCH = 8
SEQ = 2048
DEPTH = 2
DEC_BATCH = 16
DEC_SEQ = 2048
PAST_LEN = 128

HEAD_DIM = 128
PLE_DIM = 256
GRID_W = 64
EPS = 1e-6
NEG = -1e30
ROPE_THETA = 500000.0
ROT_DIM = HEAD_DIM // 4
D_FF = ((8 * D_MODEL + 3 * 256 - 1) // (3 * 256)) * 256
GLA_HEADS = 4
GLA_DV = D_MODEL // (2 * GLA_HEADS)
GLA_DK = GLA_DV // 2
GLA_RANK = 16
GLA_TAU = 16.0
GLA_CHUNK = 64
NA_HEADS = D_MODEL // (2 * HEAD_DIM)
NA_WIN_ROWS = 8
NA_WIN_COLS = 16
DIL_PAIRS = ((128, 1), (512, 4), (2048, 16))
DIL_HEADS = D_MODEL // (2 * HEAD_DIM)
DIL_BLOCK = 64
DIFF_DH = 64
DIFF_HEADS = D_MODEL // (4 * DIFF_DH)
Q_BLOCK = 128

N_EVEN = (DEPTH + 1) // 2
N_ODD = DEPTH // 2
EVEN_SPLIT = (GLA_HEADS * GLA_DK, GLA_HEADS * GLA_DK, GLA_HEADS * GLA_DV, GLA_HEADS * GLA_DV, 2 * GLA_RANK,
              NA_HEADS * HEAD_DIM, NA_HEADS * HEAD_DIM, NA_HEADS * HEAD_DIM)
EVEN_WIDTH = sum(EVEN_SPLIT)
EVEN_OUT = GLA_HEADS * GLA_DV + NA_HEADS * HEAD_DIM
ODD_SPLIT = (len(DIL_PAIRS) * DIL_HEADS * HEAD_DIM, DIL_HEADS * HEAD_DIM, DIL_HEADS * HEAD_DIM,
             DIFF_HEADS * 2 * DIFF_DH, DIFF_HEADS * 2 * DIFF_DH, DIFF_HEADS * 2 * DIFF_DH)
ODD_WIDTH = sum(ODD_SPLIT)
ODD_OUT = DIL_HEADS * HEAD_DIM + DIFF_HEADS * 2 * DIFF_DH

kernel_name = 'hybrid_bidir_encoder'


def _rmsnorm(x, g):
    xf = x.astype(jnp.float32)
    y = xf * lax.rsqrt(jnp.mean(xf * xf, axis=-1, keepdims=True) + EPS)
    return (y * g.astype(jnp.float32)).astype(x.dtype)


def _split(z, sizes):
    outs, off = [], 0
    for s in sizes:
        outs.append(z[..., off:off + s])
        off += s
    return outs


def _rope(x, rot):
    T = x.shape[1]
    half = rot // 2
    inv = ROPE_THETA ** (-jnp.arange(half, dtype=jnp.float32) / half)
    ang = jnp.arange(T, dtype=jnp.float32)[:, None] * inv[None, :]
    shape = (1, T) + (1,) * (x.ndim - 3) + (half,)
    cos = jnp.cos(ang).reshape(shape)
    sin = jnp.sin(ang).reshape(shape)
    xf = x.astype(jnp.float32)
    x1, x2, rest = xf[..., :half], xf[..., half:rot], xf[..., rot:]
    return jnp.concatenate([x1 * cos - x2 * sin, x2 * cos + x1 * sin, rest], axis=-1).astype(x.dtype)


def _gla_scan(q, k, v, g, strict):
    B, H, T, dk = q.shape
    dv = v.shape[-1]
    L = GLA_CHUNK
    N = T // L
    q, k, g = (t.reshape(B, H, N, L, dk) for t in (q, k, g))
    v = v.reshape(B, H, N, L, dv)
    b = jnp.cumsum(g, axis=3)
    b_last = b[:, :, :, -1:]
    qd = q * jnp.exp(b)
    kd = k * jnp.exp(-b)
    causal = jnp.tril(jnp.ones((L, L), dtype=bool), -1 if strict else 0)
    att = jnp.where(causal, jnp.einsum('bhnid,bhnjd->bhnij', qd, kd), 0.0)
    intra = jnp.einsum('bhnij,bhnjv->bhniv', att, v)
    u = jnp.einsum('bhnjd,bhnjv->nbhdv', k * jnp.exp(b_last - b), v)
    decay = jnp.moveaxis(jnp.exp(b_last[:, :, :, 0]), 2, 0)

    def step(s, inp):
        d, uc = inp
        return d[..., None] * s + uc, s

    _, s_in = lax.scan(step, jnp.zeros((B, H, dk, dv), jnp.float32), (decay, u))
    inter = jnp.einsum('bhnid,nbhdv->bhniv', qd, s_in)
    return (intra + inter).reshape(B, H, T, dv)


def _gla(q, k, v, lr, wg_f, bg_f, wg_b, bg_b, g_norm):
    B, T, H, dk = q.shape
    dv = v.shape[-1]
    dt = v.dtype
    hf = lambda t: jnp.transpose(t, (0, 2, 1, 3)).astype(jnp.float32)
    gf = jax.nn.log_sigmoid((lr[..., :GLA_RANK] @ wg_f + bg_f).astype(jnp.float32)) / GLA_TAU
    gb = jax.nn.log_sigmoid((lr[..., GLA_RANK:] @ wg_b + bg_b).astype(jnp.float32)) / GLA_TAU
    qh = hf(q) * dk ** -0.5
    kh, vh = hf(k), hf(v)
    gfh, gbh = hf(gf.reshape(B, T, H, dk)), hf(gb.reshape(B, T, H, dk))
    flip = lambda t: jnp.flip(t, axis=2)
    o_f = _gla_scan(qh, kh, vh, gfh, False)
    o_b = flip(_gla_scan(flip(qh), flip(kh), flip(vh), flip(gbh), True))
    o = jnp.transpose(o_f + o_b, (0, 2, 1, 3))
    o = _rmsnorm(o, g_norm)
    return o.reshape(B, T, H * dv).astype(dt)


def _neighbourhood_attn(q, k, v, rpb):
    B, T, H, dh = q.shape
    rows = T // GRID_W
    wr = min(NA_WIN_ROWS, rows)
    grid = lambda t: t.reshape(B, rows, GRID_W, H, dh)
    qg, kg, vg = grid(q), grid(k), grid(v)
    r = jnp.arange(rows)
    key_rows = jnp.clip(r - wr // 2, 0, rows - wr)[:, None] + jnp.arange(wr)[None, :]
    kw = kg[:, key_rows]
    vw = vg[:, key_rows]
    c = jnp.arange(GRID_W)
    c0 = jnp.clip(c - NA_WIN_COLS // 2, 0, GRID_W - NA_WIN_COLS)
    col_ok = (c[None, :] >= c0[:, None]) & (c[None, :] < c0[:, None] + NA_WIN_COLS)
    dr = key_rows - r[:, None] + NA_WIN_ROWS - 1
    dc = jnp.clip(c[None, :] - c[:, None] + NA_WIN_COLS - 1, 0, 2 * NA_WIN_COLS - 2)
    bias = rpb[:, dr[:, None, :, None], dc[None, :, None, :]]
    s = jnp.einsum('brqhd,brokhd->bhrqok', qg, kw).astype(jnp.float32) * dh ** -0.5
    s = s + bias.astype(jnp.float32)[None]
    s = jnp.where(col_ok[:, None, :], s, NEG)
    p = jax.nn.softmax(s.reshape(B, H, rows, GRID_W, wr * GRID_W), axis=-1).reshape(s.shape).astype(v.dtype)
    o = jnp.einsum('bhrqok,brokhd->brqhd', p, vw)
    return o.reshape(B, T, H * dh)


def _dilated_group(q, k, v, dil, radius):
    B, T, H, dh = q.shape
    n = T // dil
    nb = -(-n // DIL_BLOCK)
    n_pad = nb * DIL_BLOCK
    X = B * dil
    by_res = lambda t: t.reshape(B, n, dil, H, dh).transpose(0, 2, 1, 3, 4).reshape(X, n, H, dh)
    qr = jnp.pad(by_res(q), ((0, 0), (0, n_pad - n), (0, 0), (0, 0)))
    kr = jnp.pad(by_res(k), ((0, 0), (DIL_BLOCK, n_pad - n + DIL_BLOCK), (0, 0), (0, 0)))
    vr = jnp.pad(by_res(v), ((0, 0), (DIL_BLOCK, n_pad - n + DIL_BLOCK), (0, 0), (0, 0)))
    qb = qr.reshape(X, nb, DIL_BLOCK, H, dh)
    kb = kr.reshape(X, nb + 2, DIL_BLOCK, H, dh)
    vb = vr.reshape(X, nb + 2, DIL_BLOCK, H, dh)
    band = lambda t: jnp.concatenate([t[:, :-2], t[:, 1:-1], t[:, 2:]], axis=2)
    kband, vband = band(kb), band(vb)
    qpos = jnp.arange(n_pad).reshape(nb, DIL_BLOCK)
    kpos = qpos[:, :1] - DIL_BLOCK + jnp.arange(3 * DIL_BLOCK)[None, :]
    kp = kpos[:, None, :]
    valid = (jnp.abs(kp - qpos[:, :, None]) <= radius) & (kp >= 0) & (kp < n)
    s = jnp.einsum('xnqhd,xnkhd->xhnqk', qb, kband).astype(jnp.float32)
    s = jnp.where(valid, s, NEG)
    lse = jax.nn.logsumexp(s, axis=-1, keepdims=True)
    p = jnp.exp(s - lse).astype(v.dtype)
    o = jnp.einsum('xhnqk,xnkhd->xnqhd', p, vband).reshape(X, n_pad, H, dh)[:, :n]
    lse = jnp.transpose(lse[..., 0], (0, 2, 3, 1)).reshape(X, n_pad, H)[:, :n]
    undo = lambda y: y.reshape((B, dil, n) + y.shape[2:]).swapaxes(1, 2).reshape((B, T) + y.shape[2:])
    return undo(o), undo(lse)


def _dilated_attn(q, k, v):
    B, T, G, H, dh = q.shape
    q = _rope(q, ROT_DIM) * dh ** -0.5
    k = _rope(k, ROT_DIM)
    outs, lses = [], []
    for g, (win, dil) in enumerate(DIL_PAIRS):
        o, l = _dilated_group(q[:, :, g], k, v, dil, win // (2 * dil))
        outs.append(o)
        lses.append(l)
    w = jax.nn.softmax(jnp.stack(lses, axis=0), axis=0).astype(v.dtype)
    o = jnp.sum(w[..., None] * jnp.stack(outs, axis=0), axis=0)
    return o.reshape(B, T, H * dh)


def _diff_attn(q, k, v, lam, subln_g, lam_init):
    B, T, H, _, dh = q.shape
    q = _rope(q, dh // 4) * dh ** -0.5
    k = _rope(k, dh // 4)
    lamf = lam.astype(jnp.float32)
    lam_full = jnp.exp(jnp.sum(lamf[0] * lamf[1])) - jnp.exp(jnp.sum(lamf[2] * lamf[3])) + lam_init
    nq = T // Q_BLOCK
    qb = jnp.moveaxis(q.reshape(B, nq, Q_BLOCK, H, 2, dh), 1, 0)

    def block(qc):
        s = jnp.einsum('bqhcd,bkhcd->bhcqk', qc, k).astype(jnp.float32)
        p = jax.nn.softmax(s, axis=-1)
        a = (p[:, :, 0] - lam_full * p[:, :, 1]).astype(v.dtype)
        return jnp.einsum('bhqk,bkhe->bqhe', a, v)

    o = jnp.moveaxis(lax.map(block, qb), 0, 1).reshape(B, T, H, 2 * dh)
    o = _rmsnorm(o, subln_g) * (1.0 - lam_init)
    return o.reshape(B, T, H * 2 * dh)


def _mixer_even(a, w_in, w_out, wg_f, bg_f, wg_b, bg_b, gla_g, rpb):
    B, T, _ = a.shape
    qa, ka, va, ra, lr, qb, kb, vb = _split(a @ w_in, EVEN_SPLIT)
    ya = _gla(qa.reshape(B, T, GLA_HEADS, GLA_DK), ka.reshape(B, T, GLA_HEADS, GLA_DK),
              va.reshape(B, T, GLA_HEADS, GLA_DV), lr, wg_f, bg_f, wg_b, bg_b, gla_g)
    ya = ya * jax.nn.silu(ra)
    nh = lambda t: t.reshape(B, T, NA_HEADS, HEAD_DIM)
    yb = _neighbourhood_attn(nh(qb), nh(kb), nh(vb), rpb)
    return jnp.concatenate([ya, yb], axis=-1) @ w_out


def _mixer_odd(a, w_in, w_out, lam, subln_g, lam_init):
    B, T, _ = a.shape
    qc, kc, vc, qd, kd, vd = _split(a @ w_in, ODD_SPLIT)
    yc = _dilated_attn(qc.reshape(B, T, len(DIL_PAIRS), DIL_HEADS, HEAD_DIM),
                       kc.reshape(B, T, DIL_HEADS, HEAD_DIM), vc.reshape(B, T, DIL_HEADS, HEAD_DIM))
    yd = _diff_attn(qd.reshape(B, T, DIFF_HEADS, 2, DIFF_DH), kd.reshape(B, T, DIFF_HEADS, 2, DIFF_DH),
                    vd.reshape(B, T, DIFF_HEADS, 2 * DIFF_DH), lam, subln_g, lam_init)
    return jnp.concatenate([yc, yd], axis=-1) @ w_out


def _swiglu(a, wg, wu, wd):
    return (jax.nn.silu(a @ wg) * (a @ wu)) @ wd


def _trunk(x, p, norm_g, w_in_even, w_out_even, gla_wg_fwd, gla_bg_fwd, gla_wg_bwd, gla_bg_bwd,
           gla_norm_g, na_rpb, w_in_odd, w_out_odd, diff_lambda, diff_subln_g,
           w_ffn_gate, w_ffn_up, w_ffn_down, w_ple_proj, w_ple_gate):
    h = x
    for i in range(DEPTH):
        g = norm_g[i]
        a = _rmsnorm(h, g[0])
        j = i // 2
        if i % 2 == 0:
            m = _mixer_even(a, w_in_even[j], w_out_even[j], gla_wg_fwd[j], gla_bg_fwd[j],
                            gla_wg_bwd[j], gla_bg_bwd[j], gla_norm_g[j], na_rpb[j])
        else:
            m = _mixer_odd(a, w_in_odd[j], w_out_odd[j], diff_lambda[j], diff_subln_g[j],
                           0.8 - 0.6 * math.exp(-0.3 * i))
        h = h + _rmsnorm(m, g[1])
        f = _swiglu(_rmsnorm(h, g[2]), w_ffn_gate[i], w_ffn_up[i], w_ffn_down[i])
        h = h + _rmsnorm(f, g[3])
        e = _rmsnorm(p[i] @ w_ple_proj[i], g[4])
        h = h + e * jax.nn.sigmoid(h @ w_ple_gate[i])
    return h


def setup_inputs(seed: int = 0) -> dict:
    key = jax.random.key(seed)
    ks = jax.random.split(key, 24)
    nrm = lambda k, shape, scale: jax.random.normal(k, shape, jnp.float32) * scale
    D = D_MODEL
    return {
        'x_prompt': nrm(ks[0], (BATCH, SEQ, D), 1.0),
        'x_sample': nrm(ks[1], (DEC_BATCH, DEC_SEQ, D), 1.0),
        'p_prompt': nrm(ks[2], (DEPTH, BATCH, SEQ, PLE_DIM), 1.0),
        'p_sample': nrm(ks[3], (DEPTH, DEC_BATCH, DEC_SEQ, PLE_DIM), 1.0),
        'norm_g': 1.0 + nrm(ks[4], (DEPTH, 5, D), 0.05),
        'w_in_even': nrm(ks[5], (N_EVEN, D, EVEN_WIDTH), D ** -0.5),
        'w_out_even': nrm(ks[6], (N_EVEN, EVEN_OUT, D), EVEN_OUT ** -0.5),
        'gla_wg_fwd': nrm(ks[7], (N_EVEN, GLA_RANK, GLA_HEADS * GLA_DK), GLA_RANK ** -0.5),
        'gla_bg_fwd': nrm(ks[8], (N_EVEN, GLA_HEADS * GLA_DK), 0.1),
        'gla_wg_bwd': nrm(ks[9], (N_EVEN, GLA_RANK, GLA_HEADS * GLA_DK), GLA_RANK ** -0.5),
        'gla_bg_bwd': nrm(ks[10], (N_EVEN, GLA_HEADS * GLA_DK), 0.1),
        'gla_norm_g': 1.0 + nrm(ks[11], (N_EVEN, GLA_DV), 0.05),
        'na_rpb': nrm(ks[12], (N_EVEN, NA_HEADS, 2 * NA_WIN_ROWS - 1, 2 * NA_WIN_COLS - 1), 0.1),
        'w_in_odd': nrm(ks[13], (N_ODD, D, ODD_WIDTH), D ** -0.5),
        'w_out_odd': nrm(ks[14], (N_ODD, ODD_OUT, D), ODD_OUT ** -0.5),
        'diff_lambda': nrm(ks[15], (N_ODD, 4, DIFF_DH), 0.1),
        'diff_subln_g': 1.0 + nrm(ks[16], (N_ODD, 2 * DIFF_DH), 0.05),
        'w_ffn_gate': nrm(ks[17], (DEPTH, D, D_FF), D ** -0.5),
        'w_ffn_up': nrm(ks[18], (DEPTH, D, D_FF), D ** -0.5),
        'w_ffn_down': nrm(ks[19], (DEPTH, D_FF, D), D_FF ** -0.5),
        'w_ple_proj': nrm(ks[20], (DEPTH, PLE_DIM, D), PLE_DIM ** -0.5),
        'w_ple_gate': nrm(ks[21], (DEPTH, D, D), D ** -0.5),
    }


def reference(x_prompt, x_sample, p_prompt, p_sample, norm_g, w_in_even, w_out_even, gla_wg_fwd, gla_bg_fwd,
              gla_wg_bwd, gla_bg_bwd, gla_norm_g, na_rpb, w_in_odd, w_out_odd, diff_lambda, diff_subln_g,
              w_ffn_gate, w_ffn_up, w_ffn_down, w_ple_proj, w_ple_gate):
    y_prompt = _trunk(x_prompt, p_prompt, norm_g, w_in_even, w_out_even, gla_wg_fwd, gla_bg_fwd, gla_wg_bwd,
                      gla_bg_bwd, gla_norm_g, na_rpb, w_in_odd, w_out_odd, diff_lambda, diff_subln_g,
                      w_ffn_gate, w_ffn_up, w_ffn_down, w_ple_proj, w_ple_gate)
    y_sample = _trunk(x_sample, p_sample, norm_g, w_in_even, w_out_even, gla_wg_fwd, gla_bg_fwd, gla_wg_bwd,
                      gla_bg_bwd, gla_norm_g, na_rpb, w_in_odd, w_out_odd, diff_lambda, diff_subln_g,
                      w_ffn_gate, w_ffn_up, w_ffn_down, w_ple_proj, w_ple_gate)
    return (y_prompt, y_sample)
```

```python
import contextlib
import math
import numpy as np
import concourse.bass as bass
import concourse.mybir as mybir
from concourse.bass_utils import run_bass_kernel_spmd

F32 = mybir.dt.float32
BF16 = mybir.dt.bfloat16
AF = mybir.ActivationFunctionType
ALU = mybir.AluOpType

T = 2048
D = 2048
TT = 512
NTT = T // TT
KC = D // 128
DFF = 5632
FC = DFF // 128
PLE = 256
EPS = 1e-6
NEG = -30000.0
EVEN_W = 6176
ODD_W = 8192
ROPE_THETA = 500000.0


class Buf:
    __slots__ = ("ap", "w", "r", "name")

    def __init__(self, ap, name=""):
        self.ap = ap
        self.w = None
        self.r = {}
        self.name = name

    def __getitem__(self, idx):
        return self.ap[idx]


class Sched:
    ENGS = ("pe", "act", "dve", "pool", "sp")

    def __init__(self, nc, es, needed=None):
        self.nc = nc
        self.es = es
        self.dry = needed is None
        self.needed = needed if needed is not None else {n: set() for n in self.ENGS}
        self.eng = {"pe": nc.tensor, "act": nc.scalar, "dve": nc.vector, "pool": nc.gpsimd, "sp": nc.sync}
        self.sem = {n: es.enter_context(nc.semaphore("s_" + n)) for n in self.ENGS}
        self.cnt = {n: 0 for n in self.ENGS}
        self.inc = {n: 0 for n in self.ENGS}
        self.map = {n: {} for n in self.ENGS}
        self.seen = {n: {} for n in self.ENGS}
        self.dsem = {}
        self.dcnt = {}
        self.nwaits = 0

    def dkey(self, key):
        if key not in self.dsem:
            self.dsem[key] = self.es.enter_context(self.nc.semaphore("d_" + key))
            self.dcnt[key] = 0
        return key

    def _wait(self, en, ev):
        if ev[0] == "e":
            _, src, idx = ev
            if src == en and en in ("pe",):
                return
            key = ("e", src)
            if self.seen[en].get(key, 0) >= idx:
                return
            self.seen[en][key] = idx
            if self.dry:
                self.needed[src].add(idx)
            else:
                self.eng[en].wait_ge(self.sem[src], self.map[src][idx])
        else:
            _, k, val = ev
            val = self.dcnt[k]
            key = ("d", k)
            if self.seen[en].get(key, 0) >= val:
                return
            self.seen[en][key] = val
            if not self.dry:
                self.eng[en].wait_ge(self.dsem[k], val)
        self.nwaits += 1

    def _deps(self, en, reads, writes):
        for b in reads:
            if b.w is not None:
                self._wait(en, b.w)
        for b in writes:
            if b.w is not None:
                self._wait(en, b.w)
            for ev in b.r.values():
                self._wait(en, ev)

    def _mark(self, me, rk, reads, writes):
        for b in writes:
            b.w = me
            b.r = {}
        for b in reads:
            b.r[rk] = me

    def op(self, en, fn, reads=(), writes=()):
        self._deps(en, reads, writes)
        self.cnt[en] += 1
        idx = self.cnt[en]
        if not self.dry:
            ins = fn()
            if idx in self.needed[en]:
                self.inc[en] += 1
                self.map[en][idx] = self.inc[en]
                ins.then_inc(self.sem[en], 1)
        me = ("e", en, idx)
        self._mark(me, ("e", en), reads, writes)
        return me

    def dma(self, q, out, in_, key, reads=(), writes=(), **kw):
        self.dkey(key)
        self._deps(q, reads, writes)
        self.dcnt[key] += 16
        if not self.dry:
            self.eng[q].dma_start(out=out, in_=in_, **kw).then_inc(self.dsem[key], 16)
        me = ("d", key, self.dcnt[key])
        self._mark(me, ("d", key), reads, writes)
        return me

    def barrier(self):
        for en in self.ENGS:
            for src in self.ENGS:
                if src != en and self.cnt[src] > 0:
                    self._wait(en, ("e", src, self.cnt[src]))
            for k in self.dsem:
                if self.dcnt[k] > 0:
                    self._wait(en, ("d", k, self.dcnt[k]))

    def final_wait(self):
        for k in self.dsem:
            if self.dcnt[k] > 0:
                self._wait("sp", ("d", k, self.dcnt[k]))


def _rope_tables():
    def tab(rot, period):
        half = rot // 2
        inv = ROPE_THETA ** (-np.arange(half, dtype=np.float32) / half)
        ang = np.arange(T, dtype=np.float32)[None, :] * inv[:, None]
        c = np.ones((128, T), np.float32)
        s = np.zeros((128, T), np.float32)
        perm = np.zeros((128, 128), np.float32)
        for base in range(0, 128, period):
            c[base:base + half] = np.cos(ang)
            c[base + half:base + rot] = np.cos(ang)
            s[base:base + half] = -np.sin(ang)
            s[base + half:base + rot] = np.sin(ang)
            for i in range(half):
                perm[base + half + i, base + i] = 1.0
                perm[base + i, base + half + i] = 1.0
        return c, s, perm
    return tab(32, 128), tab(16, 64)


def _consts():
    cs = {}
    cs["ident"] = np.eye(128, dtype=np.float32)
    (c1, s1, p1), (c2, s2, p2) = _rope_tables()
    cs["rope_c1"], cs["rope_s1"], cs["rope_p1"] = c1, s1, p1
    cs["rope_c2"], cs["rope_s2"], cs["rope_p2"] = c2, s2, p2
    return cs


CONST_SHAPES = {
    "ident": [128, 128],
    "rope_c1": [128, T], "rope_s1": [128, T], "rope_p1": [128, 128],
    "rope_c2": [128, T], "rope_s2": [128, T], "rope_p2": [128, 128],
}

WEIGHT_SHAPES = {
    "norm_g": [2, 5, D], "w_in_even": [D, EVEN_W], "w_out_even": [D, D],
    "gla_wg_fwd": [16, 512], "gla_bg_fwd": [1, 512], "gla_wg_bwd": [16, 512], "gla_bg_bwd": [1, 512],
    "gla_norm_g": [1, 256], "na_rpb": [120, 31], "w_in_odd": [D, ODD_W], "w_out_odd": [D, D],
    "diff_lambda": [4, 64], "diff_subln_g": [1, 128],
    "w_ffn_gate": [2, D, DFF], "w_ffn_up": [2, D, DFF], "w_ffn_down": [2, DFF, D],
    "w_ple_proj": [2, PLE, D], "w_ple_gate": [2, D, D],
}


class Builder:
    def __init__(self, nseq, needed=None, dbg=(), stop_after=None):
        self.nseq = nseq
        self.ntok = nseq * T
        self.dbg = set(dbg)
        self.stop_after = stop_after
        self.nc = bass.Bass("TRN2", target_bir_lowering=False)
        self.es = contextlib.ExitStack()
        self.S = Sched(self.nc, self.es, needed)
        self.dram = {}
        self.uid = 0

    def din(self, name, shape, dt=F32):
        t = self.nc.dram_tensor(name, list(shape), dt, kind="ExternalInput").ap()
        self.dram[name] = t
        return t

    def dscr(self, name, shape, dt):
        kind = "ExternalOutput" if name in self.dbg else "Internal"
        t = self.nc.dram_tensor(name, list(shape), dt, kind=kind).ap()
        self.dram[name] = t
        return t

    def sb(self, es, name, shape, dt):
        self.uid += 1
        return es.enter_context(self.nc.sbuf_tensor("sb%d_%s" % (self.uid, name), list(shape), dt))

    def ps(self, es, name, shape, dt=F32):
        return es.enter_context(self.nc.psum_tensor("ps_" + name, list(shape), dt))

    def mm(self, out, lhsT, rhs, start, stop, reads, writes, **kw):
        nc = self.nc
        return self.S.op("pe", lambda: nc.tensor.matmul(out, lhsT, rhs, start=start, stop=stop, **kw),
                         reads=reads, writes=writes)

    def tr(self, out, in_, ident, reads, writes):
        nc = self.nc
        return self.S.op("pe", lambda: nc.tensor.transpose(out, in_, ident), reads=reads, writes=writes)

    def act(self, out, in_, func, reads, writes, scale=None, bias=None):
        nc = self.nc
        kw = {}
        if scale is not None:
            kw["scale"] = scale
        if bias is not None:
            kw["bias"] = bias
        return self.S.op("act", lambda: nc.scalar.activation(out, in_, func, **kw), reads=reads, writes=writes)

    def tt(self, en, out, in0, in1, op, reads, writes):
        e = self.nc.vector if en == "dve" else self.nc.gpsimd
        return self.S.op(en, lambda: e.tensor_tensor(out, in0, in1, op), reads=reads, writes=writes)

    def ts(self, en, out, in0, s1, op0, reads, writes, s2=None, op1=None):
        e = self.nc.vector if en == "dve" else self.nc.gpsimd
        if op1 is None:
            return self.S.op(en, lambda: e.tensor_scalar(out, in0, s1, None, op0), reads=reads, writes=writes)
        return self.S.op(en, lambda: e.tensor_scalar(out, in0, s1, s2, op0, op1), reads=reads, writes=writes)

    def stt(self, out, in0, scalar, in1, op0, op1, reads, writes):
        nc = self.nc
        return self.S.op("dve", lambda: nc.vector.scalar_tensor_tensor(out, in0, scalar, in1, op0, op1),
                         reads=reads, writes=writes)

    def cp(self, en, out, in_, reads, writes):
        nc = self.nc
        if en == "act":
            return self.S.op("act", lambda: nc.scalar.copy(out, in_), reads=reads, writes=writes)
        e = nc.vector if en == "dve" else nc.gpsimd
        return self.S.op(en, lambda: e.tensor_copy(out, in_), reads=reads, writes=writes)

    def memset(self, en, ap, val, writes):
        e = self.nc.vector if en == "dve" else self.nc.gpsimd
        return self.S.op(en, lambda: e.memset(ap, val), writes=writes)

    def recip(self, out, in_, reads, writes):
        nc = self.nc
        return self.S.op("dve", lambda: nc.vector.reciprocal(out, in_), reads=reads, writes=writes)

    def setup(self):
        nc, es, S = self.nc, self.es, self.S
        ntok = self.ntok
        self.x = self.din("x", [ntok, D])
        self.p = self.din("p", [2, ntok, PLE])
        self.W = {k: self.din(k, v) for k, v in WEIGHT_SHAPES.items()}
        self.C = {k: self.din(k, v) for k, v in CONST_SHAPES.items()}
        self.C.update({k: self.din(k, v) for k, v in MIX_CONST_SHAPES.items()})
        self.y = self.nc.dram_tensor("y", [ntok, D], F32, kind="ExternalOutput").ap()
        sc = {}
        sc["HS"] = self.dscr("HS", [D, ntok], F32)
        for l in range(2):
            sc["YT%d" % l] = self.dscr("YT%d" % l, [D, ntok], BF16)
        for nm, rows in (("QAT", 512), ("KAT", 512), ("QBT", 1024), ("KBT", 1024),
                         ("QCT", 3072), ("KCT", 1024), ("QDT", 1024), ("KDT", 1024)):
            sc[nm] = self.dscr(nm, [rows, ntok], BF16)
        for nm, cols in (("KA", 512), ("VA", 1024), ("RA", 1024), ("VB", 1024), ("VC", 1024), ("VD", 1024)):
            sc[nm] = self.dscr(nm, [ntok, cols], BF16)
        sc["LRT"] = self.dscr("LRT", [32, ntok], F32)
        self.sc = sc
        self.ident = self.sb(es, "ident", [128, 128], F32)
        self.ident_bf = self.sb(es, "ident_bf", [128, 128], BF16)
        self.ones_bf = self.sb(es, "ones_bf", [128, 128], BF16)
        self.eps_t = self.sb(es, "eps_t", [128, 1], F32)
        self.gT = self.sb(es, "gT", [128, 10, KC], F32)
        self.perm1 = self.sb(es, "perm1", [128, 128], BF16)
        self.perm2 = self.sb(es, "perm2", [128, 128], BF16)
        self.cb = Buf(None, "consts")
        ptmp = self.sb(es, "ptmp", [128, 256], F32)
        S.dma("sp", self.ident[:], self.C["ident"][:, :], "c0", writes=[self.cb])
        S.dma("sp", ptmp[:, 0:128], self.C["rope_p1"][:, :], "c0", writes=[self.cb])
        S.dma("sp", ptmp[:, 128:256], self.C["rope_p2"][:, :], "c0", writes=[self.cb])
        with nc.allow_non_contiguous_dma(reason="one-time gain vector gather"):
            for ln in range(10):
                S.dma("sp", self.gT[:, ln, :],
                      self.W["norm_g"][ln // 5, ln % 5, :].rearrange("(k p) -> p k", p=128), "c0",
                      writes=[self.cb])
        self.cp("dve", self.ident_bf[:], self.ident[:], [self.cb], [self.cb])
        self.cp("dve", self.perm1[:], ptmp[:, 0:128], [self.cb], [self.cb])
        self.cp("dve", self.perm2[:], ptmp[:, 128:256], [self.cb], [self.cb])
        self.memset("dve", self.ones_bf[:], 1.0, [self.cb])
        self.memset("dve", self.eps_t[:], EPS, [self.cb])
        self.pb = [Buf(self.ps(es, "pb%d" % i, [128, 512]), "pb%d" % i) for i in range(8)]
        self.mm_banks = self.pb[0:4]
        self.ss_bank = self.pb[4]
        self.tr_banks = self.pb[5:7]
        self.x_bank = self.pb[7]
        self.mm_i = 0
        self.tr_i = 0
        S.barrier()

    def next_mm(self):
        b = self.mm_banks[self.mm_i % len(self.mm_banks)]
        self.mm_i += 1
        return b

    def next_tr(self):
        b = self.tr_banks[self.tr_i % len(self.tr_banks)]
        self.tr_i += 1
        return b

    def alloc_tok(self, es):
        sbt = lambda n, s, d: self.sb(es, n, s, d)
        ht = sbt("hT", [128, KC, TT], F32)
        self.h = [Buf(ht[:, k, :], "h%d" % k) for k in range(KC)]
        xt = sbt("xT", [128, KC, TT], BF16)
        self.xt = [Buf(xt[:, k, :], "x%d" % k) for k in range(KC)]
        sct = sbt("scT", [128, KC, TT], F32)
        self.sct = [Buf(sct[:, k, :], "sc%d" % k) for k in range(KC)]
        self.xin = [Buf(sct[:, 4 * j:4 * j + 4, :].rearrange("p a b -> p (a b)"), "xin%d" % j) for j in range(4)]
        at = sbt("actT", [128, FC, TT], BF16)
        self.actt = [Buf(at[:, f, :], "a%d" % f) for f in range(FC)]
        self.ostg = []
        for j in range(2):
            v = at[:, 8 * j:8 * j + 8, :].rearrange("p a b -> p (a b)").bitcast(F32)
            self.ostg.append((Buf(v, "ostg%d" % j), self.actt[8 * j:8 * j + 8]))
        sq = sbt("sq", [128, 4, TT], BF16)
        self.sq = [Buf(sq[:, i, :], "sq%d" % i) for i in range(4)]
        self.sq_i = 0
        self.stats_pend = []
        tmp = sbt("tmp", [128, 6, TT], F32)
        self.tmp = [Buf(tmp[:, i, :], "tmp%d" % i) for i in range(6)]
        self.tmp_i = 0
        rs = sbt("rstd", [128, 3, TT], F32)
        self.rstd = [Buf(rs[:, i, :], "rstd%d" % i) for i in range(3)]
        rtm = sbt("rtm", [128, 2, 4], F32)
        self.rtm = [Buf(rtm[:, i, :], "rtm%d" % i) for i in range(2)]
        self.rtm_i = 0
        self.rstd_i = 0
        self.NW = 4
        wt = sbt("wslots", [128, self.NW, 4096], BF16)
        self.wslot = [Buf(wt[:, i, :], "w%d" % i) for i in range(self.NW)]
        stg = sbt("stg", [128, 6, TT], BF16)
        self.stg = [Buf(stg[:, i, :], "stg%d" % i) for i in range(6)]
        self.stg_i = 0
        qs_ = sbt("qs", [128, 4, TT], BF16)
        self.qs = [Buf(qs_[:, i, :], "qs%d" % i) for i in range(4)]
        self.qs_i = 0
        rp = sbt("rope", [128, 4, TT], F32)
        self.rope = [Buf(rp[:, i, :], "rope%d" % i) for i in range(4)]
        pt = sbt("pT", [128, 2, TT], BF16)
        self.pt = [Buf(pt[:, i, :], "pT%d" % i) for i in range(2)]
        pin = sbt("pin", [128, 4, PLE], F32)
        self.pin = Buf(pin, "pin")

    def nxt(self, name):
        lst = getattr(self, name)
        i = getattr(self, name + "_i")
        setattr(self, name + "_i", i + 1)
        return lst[i % len(lst)], i % len(lst)

    def wstream_begin(self, specs, ntiles, name):
        self.wspecs = specs
        self.w_n = len(specs)
        self.w_total = len(specs) * ntiles
        self.w_issued = 0
        self.w_used = 0
        self.wscr = self.dscr("WS_" + name, [len(specs), 128, 4096], BF16)
        self.wscr_b = [Buf(None, "ws%d" % i) for i in range(len(specs))]

    def wnext(self):
        S = self.S
        while self.w_issued < self.w_total and self.w_issued < self.w_used + self.NW - 1:
            bi = self.w_issued % self.w_n
            src, a, b = self.wspecs[bi]
            slot = self.wslot[self.w_issued % self.NW]
            key = "w%d" % (self.w_issued % self.NW)
            if self.w_issued < self.w_n:
                dst = slot[:, 0:a * b].rearrange("p (a b) -> p a b", b=b)
                S.dma("pool", dst, src, key, writes=[slot], max_dma_last_dim=4096)
                S.dma("sp", self.wscr[bi, :, 0:a * b], slot[:, 0:a * b], "wsb", reads=[slot], writes=[self.wscr_b[bi]])
            else:
                S.dma("pool", slot[:, 0:a * b], self.wscr[bi, :, 0:a * b], key, reads=[self.wscr_b[bi]], writes=[slot])
            self.w_issued += 1
        src, a, b = self.wspecs[self.w_used % self.w_n]
        slot = self.wslot[self.w_used % self.NW]
        self.w_used += 1
        return slot, slot[:, 0:a * b].rearrange("p (a b) -> p a b", b=b)

    def wspec_cols(self, w2d, c0, ncols):
        return (w2d[:, c0:c0 + ncols].rearrange("(k p) n -> p k n", p=128), w2d.shape[0] // 128, ncols)

    def stats_chunk(self, c, n, src_buf, src_ap, eng):
        sq, _ = self.nxt("sq")
        if eng == "act":
            self.act(sq[:, :], src_ap, AF.Square, [src_buf], [sq])
        else:
            self.tt(eng, sq[:, :], src_ap, src_ap, ALU.mult, [src_buf], [sq])
        self.stats_pend.append((sq, c == 0, c == n - 1))
        while len(self.stats_pend) > 2:
            self.stats_flush_one()

    def stats_flush_one(self):
        sq, first, last = self.stats_pend.pop(0)
        self.mm(self.ss_bank[:, :], self.ones_bf[:], sq[:, :], first, last, [sq, self.cb], [self.ss_bank])

    def stats_finish(self, nfeat):
        while self.stats_pend:
            self.stats_flush_one()
        r, _ = self.nxt("rstd")
        self.act(r[:, :], self.ss_bank[:, :], AF.Sqrt, [self.ss_bank, self.cb], [r],
                 scale=1.0 / nfeat, bias=self.eps_t[:, 0:1])
        self.recip(r[:, :], r[:, :], [r], [r])
        return r

    def g(self, l, n, k):
        return self.gT[:, l * 5 + n, k:k + 1]

    def win_plan(self, l):
        if l == 0:
            return [("QAT", 0, 512, "F", 128 ** -0.5, 0), ("KAT", 512, 512, "FT", 1.0, 0),
                    ("VA", 1024, 1024, "T", 1.0, 0), ("RA", 2048, 1024, "T", 1.0, 0),
                    ("LRT", 3072, 32, "L", 1.0, 0),
                    ("QBT", 3104, 1024, "F", 128 ** -0.5, 0), ("KBT", 4128, 1024, "F", 1.0, 0),
                    ("VB", 5152, 1024, "T", 1.0, 0)]
        return [("QCT", 0, 3072, "F", 128 ** -0.5, 1), ("KCT", 3072, 1024, "F", 1.0, 1),
                ("VC", 4096, 1024, "T", 1.0, 0),
                ("QDT", 5120, 1024, "F", 64 ** -0.5, 2), ("KDT", 6144, 1024, "F", 1.0, 2),
                ("VD", 7168, 1024, "T", 1.0, 0)]

    def win_wspecs(self, l):
        w = self.W["w_in_even" if l == 0 else "w_in_odd"]
        specs = []
        for (nm, c0, ncols, mode, scale, rope) in self.win_plan(l):
            if mode == "L":
                specs.append(self.wspec_cols(w, c0, 32))
            else:
                for b in range(ncols // 256):
                    specs.append(self.wspec_cols(w, c0 + 256 * b, 256))
        return specs

    def chain_wspecs(self, l):
        specs = []
        wo = self.W["w_out_even" if l == 0 else "w_out_odd"]
        for b in range(D // 256):
            specs.append(self.wspec_cols(wo, 256 * b, 256))
        wg, wu, wd = self.W["w_ffn_gate"][l], self.W["w_ffn_up"][l], self.W["w_ffn_down"][l]
        for b in range(DFF // 256):
            specs.append(self.wspec_cols(wg, 256 * b, 256))
            specs.append(self.wspec_cols(wu, 256 * b, 256))
        for c in range(KC):
            for hlf in range(2):
                src = wd[hlf * 2816:(hlf + 1) * 2816, c * 128:(c + 1) * 128].rearrange("(k p) n -> p k n", p=128)
                specs.append((src, 22, 128))
        wp = self.W["w_ple_proj"][l]
        for hlf in range(2):
            specs.append((wp[:, hlf * 1024:(hlf + 1) * 1024].rearrange("(k p) n -> p k n", p=128), 2, 1024))
        wpg = self.W["w_ple_gate"][l]
        for b in range(D // 256):
            specs.append(self.wspec_cols(wpg, 256 * b, 256))
        return specs

    def prenorm_chunk(self, l, n, c):
        self.stats_chunk(c, KC, self.h[c], self.h[c][:, :], "act")
        self.act(self.xt[c][:, :], self.h[c][:, :], AF.Identity, [self.h[c], self.cb], [self.xt[c]],
                 scale=self.g(l, n, c))

    def rstd_tokmajor(self, r):
        trb = self.next_tr()
        for j in range(4):
            self.tr(trb[:, j * 128:(j + 1) * 128], r[:, j * 128:(j + 1) * 128], self.ident[:], [r, self.cb], [trb])
        rt, _ = self.nxt("rtm")
        self.cp("dve", rt[:, 0:4], trb[:, :].rearrange("p (j t) -> p j t", j=4)[:, :, 0], [trb], [rt])
        return rt

    def win_stage(self, l, tok0, pos0):
        S = self.S
        sc = self.sc
        rr = {}

        def get_r():
            if "r" not in rr:
                rr["r"] = self.stats_finish(D)
                rr["rt"] = self.rstd_tokmajor(rr["r"])
            return rr["r"], rr["rt"]
        if l == 1:
            for i, nm in enumerate(("rope_c1", "rope_s1", "rope_c2", "rope_s2")):
                S.dma("sp", self.rope[i][:, :], self.C[nm][:, pos0:pos0 + TT], "rope", writes=[self.rope[i]])
        rope_pend = []
        for (nm, c0, ncols, mode, scale, rope) in self.win_plan(l):
            dst = sc[nm]
            if rope == 0:
                while rope_pend:
                    rope_pend.pop(0)()
            if mode == "L":
                wb, wv = self.wnext()
                pbk = self.next_mm()
                for k in range(KC):
                    self.mm(pbk[0:32, :], wv[:, k, 0:32], self.xt[k][:, :], k == 0, k == KC - 1,
                            [wb, self.xt[k]], [pbk])
                t, _ = self.nxt("tmp")
                r, rt = get_r()
                self.tt("dve", t[0:32, :], pbk[0:32, :], r[0:32, :], ALU.mult, [pbk, r], [t])
                S.dma("sp", dst[:, tok0:tok0 + TT], t[0:32, :], "tmp_st", reads=[t])
                continue
            for b in range(ncols // 256):
                wb, wv = self.wnext()
                if "F" in mode:
                    for sub in range(2):
                        f0 = 256 * b + 128 * sub
                        pbk = self.next_mm()
                        for k in range(KC):
                            self.mm(pbk[:, :], wv[:, k, sub * 128:(sub + 1) * 128], self.xt[k][:, :],
                                    k == 0, k == KC - 1, [wb, self.xt[k]], [pbk])
                        st, si = self.nxt("stg")
                        r, rt = get_r()
                        if rope == 0:
                            self.stt(st[:, :], pbk[:, :], scale, r[:, :], ALU.mult, ALU.mult, [pbk, r], [st])
                        else:
                            perm = self.perm1 if rope == 1 else self.perm2
                            rc, rs = (self.rope[0], self.rope[1]) if rope == 1 else (self.rope[2], self.rope[3])
                            qs, _ = self.nxt("qs")
                            self.stt(qs[:, :], pbk[:, :], scale, r[:, :], ALU.mult, ALU.mult, [pbk, r], [qs])

                            def fin(qs=qs, perm=perm, rc=rc, rs=rs, st=st, si=si, f0=f0, dst=dst):
                                trb = self.next_tr()
                                self.mm(trb[:, :], perm[:], qs[:, :], True, True, [qs, self.cb], [trb])
                                t2, _ = self.nxt("tmp")
                                self.tt("pool", t2[:, :], qs[:, :], rc[:, :], ALU.mult, [qs, rc], [t2])
                                t3, _ = self.nxt("tmp")
                                self.tt("dve", t3[:, :], trb[:, :], rs[:, :], ALU.mult, [trb, rs], [t3])
                                self.tt("pool", st[:, :], t2[:, :], t3[:, :], ALU.add, [t2, t3], [st])
                                S.dma("sp", dst[f0:f0 + 128, tok0:tok0 + TT], st[:, :], "stg%d" % si, reads=[st])
                            rope_pend.append(fin)
                            while len(rope_pend) > 1:
                                rope_pend.pop(0)()
                            continue
                        S.dma("sp", dst[f0:f0 + 128, tok0:tok0 + TT], st[:, :], "stg%d" % si, reads=[st])
                if "T" in mode:
                    dstT = sc["KA"] if mode == "FT" else dst
                    for jp in range(2):
                        pbk = self.next_mm()
                        for jj in range(2):
                            j = 2 * jp + jj
                            for k in range(KC):
                                self.mm(pbk[:, jj * 256:(jj + 1) * 256], self.xt[k][:, j * 128:(j + 1) * 128],
                                        wv[:, k, :], k == 0, k == KC - 1, [wb, self.xt[k]], [pbk])
                        st, si = self.nxt("stg")
                        r, rt = get_r()
                        assert scale == 1.0
                        for jj in range(2):
                            j = 2 * jp + jj
                            self.act(st[:, jj * 256:(jj + 1) * 256], pbk[:, jj * 256:(jj + 1) * 256], AF.Identity,
                                     [pbk, rt], [st], scale=rt[:, j:j + 1])
                        r0 = tok0 + jp * 256
                        S.dma("sp", dstT[r0:r0 + 256, 256 * b:256 * b + 256].rearrange("(j p) c -> p j c", p=128),
                              st[:, :].rearrange("p (j c) -> p j c", c=256), "stg%d" % si, reads=[st])

        while rope_pend:
            rope_pend.pop(0)()

    def phase_A0(self):
        S = self.S
        tiles = [(s, t) for s in range(self.nseq) for t in range(NTT)]
        self.wstream_begin(self.win_wspecs(0), len(tiles), "A0")
        def load_x(tok0):
            for j in range(4):
                S.dma("sp", self.xin[j][:, :], self.x[tok0 + j * 128:tok0 + (j + 1) * 128, :], "xin%d" % j,
                      writes=[self.xin[j]])

        load_x(0)
        for tix, (s, t) in enumerate(tiles):
            tok0 = s * T + t * TT
            for k in range(KC):
                trb = self.next_tr()
                for j in range(4):
                    self.tr(trb[:, j * 128:(j + 1) * 128], self.xin[j][:, k * 128:(k + 1) * 128], self.ident[:],
                            [self.xin[j], self.cb], [trb])
                self.cp("act" if k % 2 == 0 else "dve", self.h[k][:, :], trb[:, :], [trb], [self.h[k]])
                S.dma("sp", self.sc["HS"][k * 128:(k + 1) * 128, tok0:tok0 + TT], self.h[k][:, :], "hst",
                      reads=[self.h[k]])
                self.prenorm_chunk(0, 0, k)
            if tix + 1 < len(tiles):
                s2, t2 = tiles[tix + 1]
                load_x(s2 * T + t2 * TT)
            self.win_stage(0, tok0, t * TT)
        S.barrier()

    def proj_chunk(self, wb, wv, sub, pbk, xbufs=None, nk=KC, k0=0, first=True, last=True):
        xb = self.xt if xbufs is None else xbufs
        for k in range(nk):
            self.mm(pbk[:, :], wv[:, k, sub * 128:(sub + 1) * 128], xb[k0 + k][:, :],
                    first and k == 0, last and k == nk - 1, [wb, xb[k0 + k]], [pbk])

    def postnorm_residual(self, l, n, cast_xt=False, pre=None):
        r = self.stats_finish(D)
        for c in range(KC):
            t, _ = self.nxt("tmp")
            self.tt("pool", t[:, :], self.sct[c][:, :], r[:, :], ALU.mult, [self.sct[c], r], [t])
            self.stt(self.h[c][:, :], t[:, :], self.g(l, n, c), self.h[c][:, :], ALU.mult, ALU.add,
                     [t, self.h[c], self.cb], [self.h[c]])
            if cast_xt:
                self.cp("act", self.xt[c][:, :], self.h[c][:, :], [self.h[c]], [self.xt[c]])
            if pre is not None:
                self.prenorm_chunk(pre[0], pre[1], c)

    def phase_C(self, l):
        S = self.S
        last = (l == 1)
        tiles = [(s, ti) for s in range(self.nseq) for ti in range(NTT)]
        specs = self.chain_wspecs(l)
        if not last:
            specs += self.win_wspecs(l + 1)
        self.wstream_begin(specs, len(tiles), "C%d" % l)
        YT = self.sc["YT%d" % l]
        ybuf = self.actt[16:32]

        def load_h(tok0):
            for k in range(KC):
                S.dma("sp", self.h[k][:, :], self.sc["HS"][k * 128:(k + 1) * 128, tok0:tok0 + TT], "hld",
                      writes=[self.h[k]])

        def load_y(tok0):
            for k in range(KC):
                S.dma("sp", ybuf[k][:, :], YT[k * 128:(k + 1) * 128, tok0:tok0 + TT], "yld", writes=[ybuf[k]])

        def load_p(tok0):
            S.dma("sp", self.pin.ap[:, :, :],
                  self.p[l, tok0:tok0 + TT, :].rearrange("(j p) c -> p j c", p=128), "pld", writes=[self.pin])

        toks = [s_ * T + ti * TT for (s_, ti) in tiles]
        load_y(toks[0])
        load_h(toks[0])
        load_p(toks[0])
        for tix, (s, ti) in enumerate(tiles):
            tok0 = toks[tix]
            nxt_tok = toks[tix + 1] if tix + 1 < len(tiles) else None
            for b in range(D // 256):
                wb, wv = self.wnext()
                for sub in range(2):
                    c = 2 * b + sub
                    pbk = self.next_mm()
                    self.proj_chunk(wb, wv, sub, pbk, xbufs=ybuf)
                    self.cp("act", self.sct[c][:, :], pbk[:, :], [pbk], [self.sct[c]])
                    self.stats_chunk(c, KC, pbk, pbk[:, :], "act")
            self.postnorm_residual(l, 1, pre=(l, 2))
            r2 = None
            for b in range(DFF // 256):
                wbg, wvg = self.wnext()
                wbu, wvu = self.wnext()
                for sub in range(2):
                    f = 2 * b + sub
                    pg = self.next_mm()
                    self.proj_chunk(wbg, wvg, sub, pg)
                    pu = self.next_mm()
                    self.proj_chunk(wbu, wvu, sub, pu)
                    if r2 is None:
                        r2 = self.stats_finish(D)
                    t, _ = self.nxt("tmp")
                    t2, _ = self.nxt("tmp")
                    self.tt("dve", t[:, :], pg[:, :], r2[:, :], ALU.mult, [pg, r2], [t])
                    self.act(t[:, :], t[:, :], AF.Silu, [t], [t])
                    self.tt("dve", t2[:, :], pu[:, :], r2[:, :], ALU.mult, [pu, r2], [t2])
                    self.tt("dve", self.actt[f][:, :], t[:, :], t2[:, :], ALU.mult, [t, t2], [self.actt[f]])
            for c in range(KC):
                pbk = self.next_mm()
                for hlf in range(2):
                    wb, wv = self.wnext()
                    self.proj_chunk(wb, wv, 0, pbk, xbufs=self.actt, nk=22, k0=22 * hlf,
                                    first=(hlf == 0), last=(hlf == 1))
                self.cp("act", self.sct[c][:, :], pbk[:, :], [pbk], [self.sct[c]])
                self.stats_chunk(c, KC, pbk, pbk[:, :], "act")
            if nxt_tok is not None:
                load_y(nxt_tok)
            self.postnorm_residual(l, 3, cast_xt=True)
            for kk in range(2):
                trb = self.next_tr()
                for j in range(4):
                    self.tr(trb[:, j * 128:(j + 1) * 128], self.pin.ap[:, j, kk * 128:(kk + 1) * 128], self.ident[:],
                            [self.pin, self.cb], [trb])
                self.cp("dve", self.pt[kk][:, :], trb[:, :], [trb], [self.pt[kk]])
            if nxt_tok is not None:
                load_p(nxt_tok)
            for hlf in range(2):
                wb, wv = self.wnext()
                for sub in range(8):
                    c = 8 * hlf + sub
                    pbk = self.next_mm()
                    self.proj_chunk(wb, wv, sub, pbk, xbufs=self.pt, nk=2)
                    self.cp("act", self.sct[c][:, :], pbk[:, :], [pbk], [self.sct[c]])
                    self.stats_chunk(c, KC, pbk, pbk[:, :], "act")
            r4 = self.stats_finish(D)
            for b in range(D // 256):
                wb, wv = self.wnext()
                for sub in range(2):
                    c = 2 * b + sub
                    pbk = self.next_mm()
                    self.proj_chunk(wb, wv, sub, pbk)
                    sg, _ = self.nxt("tmp")
                    self.act(sg[:, :], pbk[:, :], AF.Sigmoid, [pbk], [sg])
                    t1, _ = self.nxt("tmp")
                    self.tt("pool", t1[:, :], self.sct[c][:, :], r4[:, :], ALU.mult, [self.sct[c], r4], [t1])
                    self.stt(t1[:, :], t1[:, :], self.g(l, 4, c), sg[:, :], ALU.mult, ALU.mult,
                             [t1, sg, self.cb], [t1])
                    self.tt("dve", self.h[c][:, :], self.h[c][:, :], t1[:, :], ALU.add, [self.h[c], t1], [self.h[c]])
                    if not last:
                        self.stats_chunk(c, KC, self.h[c], self.h[c][:, :], "act")
            if last:
                for j in range(4):
                    (ob, alias) = self.ostg[j % 2]
                    for k4 in range(4):
                        trb = self.next_tr()
                        for kk in range(4):
                            k = 4 * k4 + kk
                            self.tr(trb[:, kk * 128:(kk + 1) * 128], self.h[k][:, j * 128:(j + 1) * 128], self.ident[:],
                                    [self.h[k], self.cb], [trb])
                        self.cp("act" if k4 % 2 == 0 else "dve", ob[:, k4 * 512:(k4 + 1) * 512], trb[:, :],
                                [trb], [ob] + alias)
                    S.dma("sp", self.y[tok0 + j * 128:tok0 + (j + 1) * 128, :], ob[:, :], "ost%d" % (j % 2),
                          reads=[ob] + alias)
                if nxt_tok is not None:
                    load_h(nxt_tok)
            else:
                for k in range(KC):
                    S.dma("sp", self.sc["HS"][k * 128:(k + 1) * 128, tok0:tok0 + TT], self.h[k][:, :], "hst",
                          reads=[self.h[k]])
                for c in range(KC):
                    self.act(self.xt[c][:, :], self.h[c][:, :], AF.Identity, [self.h[c], self.cb], [self.xt[c]],
                             scale=self.g(l + 1, 0, c))
                if nxt_tok is not None:
                    load_h(nxt_tok)
                self.win_stage(l + 1, tok0, ti * TT)
        S.barrier()


    def phase_B1(self):
        with contextlib.ExitStack() as es:
            self.mix_common(es)
            self.dilated_all(es)
        self.S.barrier()
        with contextlib.ExitStack() as es:
            self.mix_common(es)
            self.diff_all(es)
        self.S.barrier()

    def mix_common(self, es):
        sbt = lambda n, s, d: self.sb(es, n, s, d)
        pt = sbt("mx_pt", [128, 6, 512], BF16)
        self.mpt = [Buf(pt[:, i, :], "mpt%d" % i) for i in range(6)]
        self.mpt_i = 0
        t32 = sbt("mx_t32", [128, 4, 512], F32)
        self.m32 = [Buf(t32[:, i, :], "m32_%d" % i) for i in range(4)]
        self.m32_i = 0
        st = sbt("mx_stg", [128, 4, 512], BF16)
        self.mstg = [Buf(st[:, i, :], "mstg%d" % i) for i in range(4)]
        self.mstg_i = 0


    def pipe(self, n, A, B, C, L=3):
        ctx = {}
        for i in range(n + L):
            if i < n:
                c = A(i)
                ctx[i] = B(i, c)
            if i >= L:
                C(i - L, ctx.pop(i - L))
            if self.later:
                due = [f for (d, f) in self.later if d <= i]
                self.later = [(d, f) for (d, f) in self.later if d > i]
                for f in due:
                    f()
        for (d, f) in self.later:
            f()
        self.later = []

    def diff_all(self, es):
        S, nc = self.S, self.nc
        sbt = lambda n, s, d: self.sb(es, n, s, d)
        lam_init = 0.8 - 0.6 * math.exp(-0.3 * 1)
        lam = sbt("df_lam", [128, 256], F32)
        lamb = Buf(lam, "lam")
        sc8 = sbt("df_sc", [128, 8], F32)
        scb = Buf(sc8, "dfsc")
        S.dma("sp", lam[:, :], self.W["diff_lambda"].rearrange("a b -> (a b)").partition_broadcast(128), "c0",
              writes=[lamb])
        with nc.allow_non_contiguous_dma(reason="tiny per-partition gain load"):
            S.dma("sp", sc8[:, 4:5], self.W["diff_subln_g"].rearrange("a p -> p a"), "c0", writes=[scb])
        prod = sbt("df_prod", [128, 128], F32)
        pb_ = Buf(prod, "prod")
        self.tt("dve", prod[:, 0:64], lam[:, 0:64], lam[:, 64:128], ALU.mult, [lamb], [pb_])
        self.tt("dve", prod[:, 64:128], lam[:, 128:192], lam[:, 192:256], ALU.mult, [lamb], [pb_])
        self.S.op("dve", lambda: nc.vector.tensor_reduce(sc8[:, 0:1], prod[:, 0:64], mybir.AxisListType.X, ALU.add),
                  reads=[pb_], writes=[scb])
        self.S.op("dve", lambda: nc.vector.tensor_reduce(sc8[:, 1:2], prod[:, 64:128], mybir.AxisListType.X, ALU.add),
                  reads=[pb_], writes=[scb])
        self.act(sc8[:, 0:2], sc8[:, 0:2], AF.Exp, [scb], [scb])
        self.tt("dve", sc8[:, 2:3], sc8[:, 1:2], sc8[:, 0:1], ALU.subtract, [scb], [scb])
        self.ts("dve", sc8[:, 3:4], sc8[:, 2:3], -lam_init, ALU.add, [scb], [scb])
        self.ts("dve", sc8[:, 5:6], sc8[:, 4:5], 1.0 - lam_init, ALU.mult, [scb], [scb])
        neglam = sc8[:, 3:4]
        gsub = sc8[:, 5:6]
        vt = sbt("df_v", [128, 16, 1024], BF16)
        vb = Buf(vt, "dfv")
        qk = sbt("df_qk", [128, 2, 2, T], BF16)
        qkb = [(Buf(qk[:, i, 0, :], "dfq%d" % i), Buf(qk[:, i, 1, :], "dfk%d" % i)) for i in range(2)]
        kz = sbt("df_kz", [128, 2, 2, T], BF16)
        kzb = [[Buf(kz[:, i, c, :], "dfkz%d_%d" % (i, c)) for c in range(2)] for i in range(2)]
        for i in range(2):
            for c in range(2):
                self.memset("pool", kz[:, i, c, :], 0.0, [kzb[i][c]])
        YT = self.sc["YT1"]
        self.later = []
        dsq = sbt("df_sq", [128, 2, 512], BF16)
        self.dfsq = [Buf(dsq[:, i, :], "dfsq%d" % i) for i in range(2)]
        self.dfsq_i = 0
        for s in range(self.nseq):
            S.dma("sp", vt[:, :, :], self.sc["VD"][s * T:(s + 1) * T, :].rearrange("(k p) c -> p k c", p=128), "dfv",
                  writes=[vb])
            steps = [(h, qblk, c, kt) for h in range(8) for qblk in range(4) for c in range(2) for kt in range(16)]
            sbanks = self.pb[4:7]
            state = {"res": {}}

            def load_head(h):
                qb, kb = qkb[h % 2]
                S.dma("sp", qb[:, :], self.sc["QDT"][h * 128:(h + 1) * 128, s * T:(s + 1) * T], "dfq%d" % (h % 2),
                      writes=[qb])
                S.dma("sp", kb[:, :], self.sc["KDT"][h * 128:(h + 1) * 128, s * T:(s + 1) * T], "dfk%d" % (h % 2),
                      writes=[kb])
                self.cp("pool", kz[0:64, h % 2, 0, :], kb[0:64, :], [kb], [kzb[h % 2][0]])
                self.cp("pool", kz[64:128, h % 2, 1, :], kb[64:128, :], [kb], [kzb[h % 2][1]])

            load_head(0)

            def A(i):
                h, qblk, c, kt = steps[i]
                if qblk == 0 and c == 0 and kt == 0 and h + 1 < 8:
                    load_head(h + 1)
                qb, kb = qkb[h % 2]
                sb_ = sbanks[i % 3]
                q0 = qblk * 512
                kzc = kzb[h % 2][c]
                self.mm(sb_[:, :], kzc[:, kt * 128:(kt + 1) * 128], qb[:, q0:q0 + 512], True, True, [kzc, qb], [sb_])
                return sb_

            def B(i, sb_):
                p_, _ = self.nxt("mpt")
                self.act(p_[:, :], sb_[:, :], AF.Exp, [sb_], [p_])
                return p_

            def C(i, p_):
                h, qblk, c, kt = steps[i]
                ob, zb = self.pb[c], self.pb[2 + c]
                self.mm(ob[:, :], vt[:, kt, h * 128:(h + 1) * 128], p_[:, :], kt == 0, kt == 15, [vb, p_], [ob])
                self.mm(zb[:, :], self.ones_bf[:], p_[:, :], kt == 0, kt == 15, [self.cb, p_], [zb])
                if kt == 15:
                    r, _ = self.nxt("m32")
                    self.recip(r[:, :], zb[:, :], [zb], [r])
                    self.tt("dve", r[:, :], ob[:, :], r[:, :], ALU.mult, [ob, r], [r])
                    state["res"][c] = r
                    if c == 1:
                        r0, r1 = state["res"][0], state["res"][1]
                        q0 = qblk * 512
                        self.stt(r0[:, :], r1[:, :], neglam, r0[:, :], ALU.mult, ALU.add, [r0, r1, scb], [r0])
                        sq, _ = self.nxt("dfsq")
                        self.act(sq[:, :], r0[:, :], AF.Square, [r0], [sq])

                        def fin(r0=r0, r1=r1, sq=sq, h=h, q0=q0):
                            ssb = self.x_bank_df
                            self.mm(ssb[:, :], self.ones_bf[:], sq[:, :], True, True, [self.cb, sq], [ssb])
                            self.act(r1[:, :], ssb[:, :], AF.Sqrt, [ssb, self.cb], [r1], scale=1.0 / 128,
                                     bias=self.eps_t[:, 0:1])
                            self.recip(r1[:, :], r1[:, :], [r1], [r1])
                            st, sti = self.nxt("mstg")
                            self.stt(st[:, :], r0[:, :], gsub, r1[:, :], ALU.mult, ALU.mult, [r0, r1, scb], [st])
                            S.dma("sp", YT[1024 + h * 128:1024 + (h + 1) * 128, s * T + q0:s * T + q0 + 512], st[:, :],
                                  "mstg%d" % sti, reads=[st])
                        self.later.append((i + 3 + 6, fin))

            self.x_bank_df = self.pb[7]
            self.pipe(len(steps), A, B, C, L=3)

    def dilated_all(self, es):
        S, nc = self.S, self.nc
        sbt = lambda n, s, d: self.sb(es, n, s, d)
        mk32 = sbt("dl_mk32", [128, 256], F32)
        mk = sbt("dl_mk", [128, 256], BF16)
        mkb = Buf(mk, "dlmask")
        S.dma("sp", mk32[:, :], self.C["dil_mask"][:, :], "c0", writes=[mkb])
        self.ts("dve", mk[:, :], mk32[:, :], -1.0, ALU.add, [mkb], [mkb], s2=-NEG, op1=ALU.mult)
        vts = [sbt("dl_v%d" % g, [128, 16, 1024], BF16) for g in range(3)]
        vbs = [Buf(vts[g], "dlv%d" % g) for g in range(3)]
        kt_ = sbt("dl_k", [128, 2, T], BF16)
        kbuf = [Buf(kt_[:, i, :], "dlk%d" % i) for i in range(2)]
        kp_ = sbt("dl_kp", [128, 2, 2, T], BF16)
        kpb = [[Buf(kp_[:, j, i, :], "dlkp%d_%d" % (j, i)) for i in range(2)] for j in range(2)]
        qt_ = sbt("dl_q", [128, 2, 3, T], BF16)
        qbuf = [[Buf(qt_[:, j, i, :], "dlq%d_%d" % (j, i)) for i in range(3)] for j in range(2)]
        qp_ = sbt("dl_qp", [128, 2, 2, T], BF16)
        qpb = [[Buf(qp_[:, j, i, :], "dlqp%d_%d" % (j, i)) for i in range(2)] for j in range(2)]
        acc = sbt("dl_acc", [128, 2, T], F32)
        acco, accz = Buf(acc[:, 0, :], "acco"), Buf(acc[:, 1, :], "accz")
        yst = sbt("dl_y", [128, T], BF16)
        ystb = Buf(yst, "dly")
        YT = self.sc["YT1"]
        DIL = (1, 4, 16)
        self.later = []
        for s in range(self.nseq):
            vsrc = self.sc["VC"][s * T:(s + 1) * T, :]
            S.dma("sp", vts[0][:, :, :], vsrc.rearrange("(k p) c -> p k c", p=128), "dlv0", writes=[vbs[0]])
            for b in range(4):
                S.dma("sp", vts[1][:, :, :].rearrange("p (r b) c -> p r b c", r=4)[:, :, b, :],
                      vsrc.rearrange("(b p r) c -> p r b c", p=128, r=4)[:, :, b, :], "dlv1", writes=[vbs[1]])
            S.dma("sp", vts[2][:, :, :], vsrc.rearrange("(p r) c -> p r c", r=16), "dlv2", writes=[vbs[2]])

            def prep_head(h):
                j = h % 2
                kb = kbuf[j]
                S.dma("sp", kb[:, :], self.sc["KCT"][h * 128:(h + 1) * 128, s * T:(s + 1) * T], "dlk%d" % j, writes=[kb])
                for g in range(3):
                    S.dma("sp", qbuf[j][g][:, :],
                          self.sc["QCT"][g * 1024 + h * 128:g * 1024 + (h + 1) * 128, s * T:(s + 1) * T],
                          "dlq%d_%d" % (j, g), writes=[qbuf[j][g]])
                for gi, d in ((0, 4), (1, 16)):
                    self.cp("pool", kpb[j][gi][:, :].rearrange("p (r i) -> p r i", r=d),
                            kb[:, :].rearrange("p (i r) -> p r i", r=d), [kb], [kpb[j][gi]])
                    self.cp("act", qpb[j][gi][:, :].rearrange("p (r i) -> p r i", r=d),
                            qbuf[j][gi + 1][:, :].rearrange("p (i r) -> p r i", r=d), [qbuf[j][gi + 1]], [qpb[j][gi]])

            steps = []
            blk = 0
            for h in range(8):
                for g in range(3):
                    d = DIL[g]
                    n = T // d
                    for rho in range(d):
                        for Q0 in range(0, n, 512):
                            Q1 = min(Q0 + 512, n)
                            b_lo = max(0, (Q0 - 64) // 128)
                            b_hi = min(n // 128 - 1, (Q1 - 1 + 64) // 128)
                            bl = []
                            for b in range(b_lo, b_hi + 1):
                                qs = max(Q0, 128 * b - 64)
                                qe = min(Q1, 128 * b + 192)
                                if qe - qs > 0:
                                    bl.append((b, qs, qe))
                            for bi, (b, qs, qe) in enumerate(bl):
                                steps.append(dict(h=h, g=g, d=d, n=n, rho=rho, Q0=Q0, Q1=Q1, b=b, qs=qs, qe=qe,
                                                  first=(bi == 0), last=(bi == len(bl) - 1), blk=blk,
                                                  hfirst=(g == 0 and Q0 == 0 and bi == 0),
                                                  hlast=(g == 2 and rho == d - 1 and bi == len(bl) - 1)))
                            blk += 1
            prep_head(0)
            sbanks = self.pb[4:8]

            def A(i):
                st = steps[i]
                h, g, n, rho, b = st["h"], st["g"], st["n"], st["rho"], st["b"]
                if st["hfirst"] and h + 1 < 8:
                    prep_head(h + 1)
                j = h % 2
                ksrc = kbuf[j] if g == 0 else kpb[j][g - 1]
                qsrc = qbuf[j][0] if g == 0 else qpb[j][g - 1]
                N = st["qe"] - st["qs"]
                sbk = sbanks[i % 4]
                kbase = rho * n + 128 * b
                qoff = st["qs"] - (128 * b - 64)
                self.mm(sbk[:, 0:N], self.ident_bf[:], mk[:, qoff:qoff + N], True, False, [self.cb, mkb], [sbk])
                self.mm(sbk[:, 0:N], ksrc[:, kbase:kbase + 128], qsrc[:, rho * n + st["qs"]:rho * n + st["qe"]],
                        False, True, [ksrc, qsrc], [sbk])
                return sbk

            def B(i, sbk):
                st = steps[i]
                N = st["qe"] - st["qs"]
                qoff = st["qs"] - (128 * st["b"] - 64)
                p_, _ = self.nxt("mpt")
                self.act(p_[:, 0:N], sbk[:, 0:N], AF.Exp, [sbk], [p_])
                return p_

            def C(i, p_):
                st = steps[i]
                h, g, d, n, rho, b, Q0, Q1 = st["h"], st["g"], st["d"], st["n"], st["rho"], st["b"], st["Q0"], st["Q1"]
                N = st["qe"] - st["qs"]
                ob = self.pb[st["blk"] % 2]
                zb = self.pb[2 + st["blk"] % 2]
                tile_i = b if g == 0 else (rho * 4 + b if g == 1 else rho)
                self.mm(ob[:, st["qs"] - Q0:st["qe"] - Q0], vts[g][:, tile_i, h * 128:(h + 1) * 128], p_[:, 0:N],
                        st["first"], False, [vbs[g], p_], [ob], skip_group_check=True)
                self.mm(zb[:, st["qs"] - Q0:st["qe"] - Q0], self.ones_bf[:], p_[:, 0:N],
                        st["first"], False, [self.cb, p_], [zb], skip_group_check=True)
                if st["last"]:
                    L_ = Q1 - Q0
                    ov = acco[:, :].rearrange("p (i r) -> p r i", r=d)[:, rho, Q0:Q1]
                    zv = accz[:, :].rearrange("p (i r) -> p r i", r=d)[:, rho, Q0:Q1]
                    if g == 0:
                        self.cp("act", ov, ob[:, 0:L_], [ob], [acco])
                        self.cp("dve", zv, zb[:, 0:L_], [zb], [accz])
                    else:
                        self.tt("dve", ov, ov, ob[:, 0:L_], ALU.add, [ob, acco], [acco])
                        self.tt("dve", zv, zv, zb[:, 0:L_], ALU.add, [zb, accz], [accz])
                if st["hlast"]:
                    self.recip(accz[:, :], accz[:, :], [accz], [accz])
                    self.tt("dve", yst[:, :], acco[:, :], accz[:, :], ALU.mult, [acco, accz], [ystb])
                    S.dma("sp", YT[h * 128:(h + 1) * 128, s * T:(s + 1) * T], yst[:, :], "dly", reads=[ystb])

            self.pipe(len(steps), A, B, C, L=3)

    def phase_B0(self):
        with contextlib.ExitStack() as es:
            self.mix_common(es)
            self.na_tables(es)
        self.S.barrier()
        with contextlib.ExitStack() as es:
            self.mix_common(es)
            self.na_all(es)
        self.S.barrier()
        with contextlib.ExitStack() as es:
            self.gla_all(es)
        self.S.barrier()

    @staticmethod
    def na_row(r):
        r0 = min(max(r - 4, 0), 24)
        return r0, r0 - r + 7

    def na_tables(self, es):
        S, nc = self.S, self.nc
        sbt = lambda n, s, d: self.sb(es, n, s, d)
        self.MB = self.dscr("MB", [8, 5, 128, 512], BF16)
        R1D = self.dscr("R1D", [120, 64, 64], F32)
        rp = sbt("na_rp", [128, 128], F32)
        rpb_ = Buf(rp, "rp")
        self.memset("dve", rp[:, :], 0.0, [rpb_])
        S.dma("sp", rp[0:120, 0:31], self.W["na_rpb"][:, :], "c0", writes=[rpb_])
        trb = self.next_tr()
        self.tr(trb[:, 0:128], rp[:, :], self.ident[:], [rpb_, self.cb], [trb])
        rpT = sbt("na_rpT", [32, 128], F32)
        rpTb = Buf(rpT, "rpT")
        self.cp("dve", rpT[0:32, 0:128], trb[0:32, 0:128], [trb], [rpTb])
        toe = sbt("na_toe", [32, 4096], F32)
        toeb = Buf(toe, "toe")
        self.memset("dve", toe[:, :], 0.0, [toeb])
        S.dma("sp", toe[0:31, :], self.C["na_toe"][:, :], "c0", writes=[toeb])
        r1 = sbt("na_r1", [128, 4096], F32)
        r1b = Buf(r1, "r1")
        for c in range(8):
            pbk = self.next_mm()
            self.mm(pbk[:, :], rpT[0:32, 0:128], toe[0:32, c * 512:(c + 1) * 512], True, True, [rpTb, toeb], [pbk])
            self.cp("act" if c % 2 else "dve", r1[0:120, c * 512:(c + 1) * 512], pbk[0:120, :], [pbk], [r1b])
        S.dma("sp", R1D.rearrange("a b c -> a (b c)"), r1[0:120, :], "na_r1", reads=[r1b])
        S.barrier()
        cm = sbt("na_cm", [128, 64], F32)
        cmb = Buf(cm, "cm")
        S.dma("sp", cm[:, :], self.C["na_colmask"][:, :], "c0", writes=[cmb])
        NTB = 8
        tb = sbt("na_tb", [128, NTB, 512], F32)
        tbb = [[Buf(tb[64 * (q % 2):64 * (q % 2) + 64, i, (q // 2) * 256:(q // 2 + 1) * 256], "natb%d_%d" % (i, q))
                for q in range(4)] for i in range(NTB)]
        tbo_ = sbt("na_tbo", [128, NTB, 512], BF16)
        tbo = [Buf(tbo_[:, i, :], "natbo%d" % i) for i in range(NTB)]
        PAIRS = ((0, 1), (2, 3), (4, 5), (28, 29), (30, 31))
        cnt = 0
        for h in range(8):
            for pc, (ra, rb) in enumerate(PAIRS):
                tq = tbb[cnt % NTB]
                t = tb[:, cnt % NTB, :]
                cnt += 1
                for slot, r in enumerate((ra, rb)):
                    r0, dr0 = self.na_row(r)
                    for par in range(2):
                        row0 = h * 15 + dr0 + par
                        src = R1D[row0:row0 + 7:2, :, :].rearrange("t k c -> k t c")
                        qb_ = tq[2 * slot + par]
                        dst = qb_[:, :].rearrange("k (t c) -> k t c", c=64)
                        S.dma(("sp", "pool")[(slot + par) % 2], dst, src, "natbq%d" % (2 * slot + par), writes=[qb_])
                to = tbo[cnt % NTB]
                self.tt("dve", to[:, :].rearrange("p (a c) -> p a c", c=64), t.rearrange("p (a c) -> p a c", c=64),
                        cm[:, :].unsqueeze(1).broadcast_to([128, 8, 64]), ALU.add, list(tq) + [cmb], [to])
                S.dma("sp", self.MB[h, pc, :, :], to[:, :], "natbo", reads=[to])

    def na_all(self, es):
        S, nc = self.S, self.nc
        sbt = lambda n, s, d: self.sb(es, n, s, d)
        v0 = sbt("na_v0", [128, 16, 1024], BF16)
        v1 = sbt("na_v1", [128, 15, 1024], BF16)
        v0b, v1b = Buf(v0, "nav0"), Buf(v1, "nav1")
        qk = sbt("na_qk", [128, 2, 2, T], BF16)
        qkb = [(Buf(qk[:, i, 0, :], "naq%d" % i), Buf(qk[:, i, 1, :], "nak%d" % i)) for i in range(2)]
        mb = sbt("na_mb", [128, 2, 5, 512], BF16)
        mbb = [Buf(mb[:, i], "namb%d" % i) for i in range(2)]
        YT = self.sc["YT0"]
        PCLS = {0: 0, 2: 1, 28: 3, 30: 4}
        self.later = []
        for s in range(self.nseq):
            vsrc = self.sc["VB"][s * T:(s + 1) * T, :]
            S.dma("sp", v0[:, :, :], vsrc.rearrange("(k p) c -> p k c", p=128), "nav0", writes=[v0b])
            S.dma("sp", v1[:, :, :], vsrc[64:64 + 15 * 128, :].rearrange("(k p) c -> p k c", p=128), "nav1", writes=[v1b])

            def load_head(h):
                qb, kb = qkb[h % 2]
                mt = mbb[h % 2]
                S.dma("sp", qb[:, :], self.sc["QBT"][h * 128:(h + 1) * 128, s * T:(s + 1) * T], "naq%d" % (h % 2), writes=[qb])
                S.dma("sp", kb[:, :], self.sc["KBT"][h * 128:(h + 1) * 128, s * T:(s + 1) * T], "nak%d" % (h % 2), writes=[kb])
                S.dma("sp", mt[:, :, :], self.MB[h].rearrange("a p c -> p a c"), "namb%d" % (h % 2), writes=[mt])

            steps = [(h, blk, pr) for h in range(8) for blk in range(4) for pr in range(4)]
            load_head(0)
            sbanks = self.pb[4:8]

            def A(i):
                h, blk, pr = steps[i]
                if blk == 0 and pr == 0 and h + 1 < 8:
                    load_head(h + 1)
                qb, kb = qkb[h % 2]
                ra = blk * 8 + pr * 2
                sbk = sbanks[i % 4]
                mt = mbb[h % 2]
                pc = PCLS.get(ra, 2)
                self.mm(sbk[:, :], self.ident_bf[:], mt[:, pc, :], True, False, [self.cb, mt], [sbk])
                for slot in range(2):
                    r = ra + slot
                    r0, _ = self.na_row(r)
                    for tp in range(4):
                        k0 = 64 * r0 + 128 * tp
                        self.mm(sbk[:, slot * 256 + tp * 64:slot * 256 + (tp + 1) * 64], kb[:, k0:k0 + 128],
                                qb[:, r * 64:(r + 1) * 64], False, slot == 1 and tp == 3, [kb, qb], [sbk],
                                skip_group_check=True)
                return sbk

            def B(i, sbk):
                h, blk, pr = steps[i]
                mt = mbb[h % 2]
                pc = PCLS.get(blk * 8 + pr * 2, 2)
                p_, _ = self.nxt("mpt")
                self.act(p_[:, :], sbk[:, :], AF.Exp, [sbk], [p_])
                return p_

            def C(i, p_):
                h, blk, pr = steps[i]
                ra = blk * 8 + pr * 2
                ob = self.pb[(h * 4 + blk) % 2]
                zb = self.pb[2 + (h * 4 + blk) % 2]
                for slot in range(2):
                    r = ra + slot
                    r0, _ = self.na_row(r)
                    c0 = (r - blk * 8) * 64
                    for tp in range(4):
                        k0 = 64 * r0 + 128 * tp
                        if k0 % 128 == 0:
                            vl, vbuf = v0[:, k0 // 128, h * 128:(h + 1) * 128], v0b
                        else:
                            vl, vbuf = v1[:, (k0 - 64) // 128, h * 128:(h + 1) * 128], v1b
                        pr_ = p_[:, slot * 256 + tp * 64:slot * 256 + (tp + 1) * 64]
                        self.mm(ob[:, c0:c0 + 64], vl, pr_, tp == 0, tp == 3, [vbuf, p_], [ob], skip_group_check=True)
                        self.mm(zb[:, c0:c0 + 64], self.ones_bf[:], pr_, tp == 0, tp == 3, [self.cb, p_], [zb],
                                skip_group_check=True)
                if pr == 3:
                    rz, _ = self.nxt("m32")
                    self.recip(rz[:, :], zb[:, :], [zb], [rz])
                    st, sti = self.nxt("mstg")
                    self.tt("dve", st[:, :], ob[:, :], rz[:, :], ALU.mult, [ob, rz], [st])
                    S.dma("sp", YT[1024 + h * 128:1024 + (h + 1) * 128, s * T + blk * 512:s * T + (blk + 1) * 512], st[:, :],
                          "mstg%d" % sti, reads=[st])

            self.pipe(len(steps), A, B, C, L=3)

    def nb(self):
        b = self.pb[self.nb_i % 8]
        self.nb_i += 1
        return b

    def gla_all(self, es):
        S, nc = self.S, self.nc
        sbt = lambda n, s, d: self.sb(es, n, s, d)
        self.nb_i = 0
        NCH = T // 128
        gm = sbt("gl_m", [128, 6, 128], F32)
        gmb = Buf(gm, "glm")
        S.dma("sp", gm[:, :, :], self.C["gla_m"].rearrange("a p c -> p a c"), "c0", writes=[gmb])
        M1, M2, M3, M4, MF, MB_ = [gm[:, i, :] for i in range(6)]
        wg = sbt("gl_wg", [32, 2, 512], F32)
        wgb = Buf(wg, "glwg")
        self.memset("dve", wg[:, :, :], 0.0, [wgb])
        for i, (wn, bn) in enumerate((("gla_wg_fwd", "gla_bg_fwd"), ("gla_wg_bwd", "gla_bg_bwd"))):
            S.dma("sp", wg[0:16, i, :], self.W[wn][:, :], "c0", writes=[wgb])
            S.dma("sp", wg[16:17, i, :], self.W[bn][:, :], "c0", writes=[wgb])
        lr = sbt("gl_lr", [32, 2, T], F32)
        lrb = Buf(lr, "gllr")
        self.memset("dve", lr[:, :, :], 0.0, [lrb])
        for i in range(2):
            S.dma("sp", lr[16:17, i, :], self.C["ones_row"][:, :], "c0", writes=[lrb])
        gn = sbt("gl_gn", [128, 256], F32)
        gnb = Buf(gn, "glgn")
        S.dma("sp", gn[:, :], self.W["gla_norm_g"].rearrange("a b -> (a b)").partition_broadcast(128), "c0", writes=[gnb])
        one_t = sbt("gl_one", [128, 1], F32)
        oneb = Buf(one_t, "glone")
        self.memset("dve", one_t[:, :], 1.0, [oneb])
        qa = sbt("gl_q", [128, 4, T], BF16)
        ka = sbt("gl_k", [128, 4, T], BF16)
        kt = sbt("gl_kt", [128, NCH, 512], BF16)
        va = sbt("gl_v", [128, NCH, 1024], BF16)
        qab, kab, ktb, vab = Buf(qa, "glq"), Buf(ka, "glk"), Buf(kt, "glkt"), Buf(va, "glv")
        ra = sbt("gl_ra", [128, 2, 1024], BF16)
        rab = [Buf(ra[:, i, :], "glra%d" % i) for i in range(2)]
        sbst = sbt("gl_sb", [128, NCH, 4, 256], BF16)
        sbb = [Buf(sbst[:, n], "glsb%d" % n) for n in range(NCH)]
        s32 = sbt("gl_s32", [128, 4, 256], F32)
        s32b = [Buf(s32[:, h, :], "gls32_%d" % h) for h in range(4)]
        sf = sbt("gl_sf", [128, 2, 4, 256], BF16)
        sfb = [[Buf(sf[:, i, h, :], "glsf%d_%d" % (i, h)) for h in range(4)] for i in range(2)]
        f32t = sbt("gl_f32", [128, 8, 512], F32)
        f32b = [Buf(f32t[:, i, :], "glf%d" % i) for i in range(8)]
        self.glf = f32b
        self.glf_i = 0
        b16t = sbt("gl_b16", [128, 16, 512], BF16)
        b16b = [Buf(b16t[:, i, :], "glh%d" % i) for i in range(16)]
        self.glh = b16b
        self.glh_i = 0
        sm = sbt("gl_sm", [128, 4, 8], F32)
        smb = [Buf(sm[:, i, :], "glsm%d" % i) for i in range(4)]
        self.glsm = smb
        self.glsm_i = 0
        gs = sbt("gl_gs", [128, 1024], F32)
        gsb = Buf(gs, "glgs")
        ya = sbt("gl_ya", [128, 1024], F32)
        yab = Buf(ya, "glya")
        yst = sbt("gl_yst", [128, 2, 8, 512], BF16)
        ystb = [Buf(yst[:, i], "glyst%d" % i) for i in range(2)]
        YT = self.sc["YT0"]

        def gate(n, di):
            pb = self.nb()
            self.mm(pb[:, :], lr[0:32, di, n * 128:(n + 1) * 128], wg[0:32, di, :], True, True, [lrb, wgb], [pb])
            e, _ = self.nxt("glf")
            self.act(e[:, :], pb[:, :], AF.Exp, [pb], [e], scale=-1.0)
            self.act(e[:, :], e[:, :], AF.Ln, [e, oneb], [e], bias=one_t[:, 0:1])
            return e

        for s in range(self.nseq):
            c0, c1 = s * T, (s + 1) * T
            S.dma("sp", qa[:, :, :], self.sc["QAT"][:, c0:c1].rearrange("(h p) t -> p h t", p=128), "glq", writes=[qab])
            S.dma("sp", ka[:, :, :], self.sc["KAT"][:, c0:c1].rearrange("(h p) t -> p h t", p=128), "glk", writes=[kab])
            S.dma("sp", kt[:, :, :], self.sc["KA"][c0:c1, :].rearrange("(c p) n -> p c n", p=128), "glkt", writes=[ktb])
            S.dma("sp", va[:, :, :], self.sc["VA"][c0:c1, :].rearrange("(c p) n -> p c n", p=128), "glv", writes=[vab])
            for i in range(2):
                S.dma("sp", lr[0:16, i, :], self.sc["LRT"][16 * i:16 * i + 16, c0:c1], "gllr", writes=[lrb])
            for h in range(4):
                self.memset("pool", s32[:, h, :], 0.0, [s32b[h]])
            self.memset("pool", sbst[:, NCH - 1], 0.0, [sbb[NCH - 1]])
            for n in range(NCH - 1, 0, -1):
                Gb = gate(n, 1)
                pe_ = self.nb()
                self.mm(pe_[:, :], M4, Gb[:, :], True, True, [gmb, Gb], [pe_])
                pd = self.nb()
                for h in range(4):
                    self.mm(pd[:, 2 * h:2 * h + 2], Gb[:, h * 128:(h + 1) * 128], M2[:, 0:2], True, True, [Gb, gmb], [pd])
                Ee, _ = self.nxt("glf")
                self.act(Ee[:, :], pe_[:, :], AF.Exp, [pe_], [Ee])
                dec, _ = self.nxt("glsm")
                self.act(dec[:, 0:8], pd[:, 0:8], AF.Exp, [pd], [dec])
                ku, _ = self.nxt("glh")
                self.tt("dve", ku[:, :], kt[:, n, :], Ee[:, :], ALU.mult, [ktb, Ee], [ku])
                for hp in range(2):
                    pu = self.nb()
                    for hh in range(2):
                        h = 2 * hp + hh
                        self.mm(pu[:, hh * 256:(hh + 1) * 256], ku[:, h * 128:(h + 1) * 128],
                                va[:, n, h * 256:(h + 1) * 256], True, True, [ku, vab], [pu])
                    for hh in range(2):
                        h = 2 * hp + hh
                        self.stt(s32[:, h, :], s32[:, h, :], dec[:, 2 * h:2 * h + 1], pu[:, hh * 256:(hh + 1) * 256],
                                 ALU.mult, ALU.add, [s32b[h], dec, pu], [s32b[h]])
                        self.cp("pool", sbst[:, n - 1, h, :], s32[:, h, :], [s32b[h]], [sbb[n - 1]])
            for h in range(4):
                self.memset("pool", s32[:, h, :], 0.0, [s32b[h]])
                self.memset("pool", sf[:, 0, h, :], 0.0, [sfb[0][h]])
            v3 = lambda ap: ap.rearrange("p (h t) -> p h t", h=4)
            v2 = lambda ap: ap.rearrange("p (h t) -> p h t", h=2)

            def stage1(n):
                rb = rab[n % 2]
                S.dma("sp", rb[:, :], self.sc["RA"][c0 + n * 128:c0 + (n + 1) * 128, :], "glra%d" % (n % 2), writes=[rb])
                Gf = gate(n, 0)
                Gb = gate(n, 1)
                pcf, pcb, pef = self.nb(), self.nb(), self.nb()
                for h in range(4):
                    self.mm(pcf[:, h * 128:(h + 1) * 128], Gf[:, h * 128:(h + 1) * 128], M1, True, True, [Gf, gmb], [pcf])
                for h in range(4):
                    self.mm(pcb[:, h * 128:(h + 1) * 128], Gb[:, h * 128:(h + 1) * 128], M2, True, True, [Gb, gmb], [pcb])
                self.mm(pef[:, :], M3, Gf[:, :], True, True, [gmb, Gf], [pef])
                Ecf, _ = self.nxt("glf")
                Encf, _ = self.nxt("glf")
                Ecb, _ = self.nxt("glf")
                Encb, _ = self.nxt("glf")
                Eef, _ = self.nxt("glf")
                self.act(Ecf[:, :], pcf[:, :], AF.Exp, [pcf], [Ecf])
                self.act(Encf[:, :], pcf[:, :], AF.Exp, [pcf], [Encf], scale=-1.0)
                self.act(Ecb[:, :], pcb[:, :], AF.Exp, [pcb], [Ecb])
                self.act(Encb[:, :], pcb[:, :], AF.Exp, [pcb], [Encb], scale=-1.0)
                self.act(Eef[:, :], pef[:, :], AF.Exp, [pef], [Eef])
                dec, _ = self.nxt("glsm")
                self.cp("pool", dec[:, 0:4], Ecf[:, :].rearrange("p (h t) -> p h t", h=4)[:, :, 127], [Ecf], [dec])
                qn = qa[:, :, n * 128:(n + 1) * 128]
                kn = ka[:, :, n * 128:(n + 1) * 128]
                qdf, _ = self.nxt("glh")
                kdf, _ = self.nxt("glh")
                qdb, _ = self.nxt("glh")
                kdb, _ = self.nxt("glh")
                kuf, _ = self.nxt("glh")
                self.tt("dve", v3(qdf[:, :]), qn, v3(Ecf[:, :]), ALU.mult, [qab, Ecf], [qdf])
                self.tt("pool", v3(kdf[:, :]), kn, v3(Encf[:, :]), ALU.mult, [kab, Encf], [kdf])
                self.tt("dve", v3(qdb[:, :]), qn, v3(Ecb[:, :]), ALU.mult, [qab, Ecb], [qdb])
                self.tt("pool", v3(kdb[:, :]), kn, v3(Encb[:, :]), ALU.mult, [kab, Encb], [kdb])
                self.tt("dve", kuf[:, :], kt[:, n, :], Eef[:, :], ALU.mult, [ktb, Eef], [kuf])
                return dict(rb=rb, dec=dec, qdf=qdf, kdf=kdf, qdb=qdb, kdb=kdb, kuf=kuf)

            def stage2(n, c):
                rb, dec, qdf, kdf, qdb, kdb, kuf = c["rb"], c["dec"], c["qdf"], c["kdf"], c["qdb"], c["kdb"], c["kuf"]
                pA, pB = self.nb(), self.nb()
                for h in range(4):
                    hs = slice(h * 128, (h + 1) * 128)
                    self.mm(pA[:, hs], kdf[:, hs], qdf[:, hs], True, True, [kdf, qdf], [pA])
                for h in range(4):
                    hs = slice(h * 128, (h + 1) * 128)
                    self.mm(pB[:, hs], kdb[:, hs], qdb[:, hs], True, True, [kdb, qdb], [pB])
                atf, _ = self.nxt("glh")
                atb, _ = self.nxt("glh")
                self.tt("dve", v3(atf[:, :]), v3(pA[:, :]), MF.unsqueeze(1).broadcast_to([128, 4, 128]), ALU.mult,
                        [pA, gmb], [atf])
                self.tt("dve", v3(atb[:, :]), v3(pB[:, :]), MB_.unsqueeze(1).broadcast_to([128, 4, 128]), ALU.mult,
                        [pB, gmb], [atb])
                sfc = sfb[n % 2]
                sfn = sfb[(n + 1) % 2]
                po = [self.nb(), self.nb()]
                for h in range(4):
                    hs = slice(h * 128, (h + 1) * 128)
                    o_ = po[h // 2][:, (h % 2) * 256:(h % 2 + 1) * 256]
                    vv = va[:, n, h * 256:(h + 1) * 256]
                    self.mm(o_, atf[:, hs], vv, True, False, [atf, vab], [po[h // 2]])
                    self.mm(o_, atb[:, hs], vv, False, False, [atb, vab], [po[h // 2]])
                    self.mm(o_, qdf[:, hs], sf[:, n % 2, h, :], False, False, [qdf, sfc[h]], [po[h // 2]])
                    self.mm(o_, qdb[:, hs], sbst[:, n, h, :], False, True, [qdb, sbb[n]], [po[h // 2]])
                if n < NCH - 1:
                    for hp in range(2):
                        pu = self.nb()
                        for hh in range(2):
                            h = 2 * hp + hh
                            self.mm(pu[:, hh * 256:(hh + 1) * 256], kuf[:, h * 128:(h + 1) * 128],
                                    va[:, n, h * 256:(h + 1) * 256], True, True, [kuf, vab], [pu])
                        for hh in range(2):
                            h = 2 * hp + hh
                            self.stt(s32[:, h, :], s32[:, h, :], dec[:, h:h + 1],
                                     pu[:, hh * 256:(hh + 1) * 256], ALU.mult, ALU.add, [s32b[h], dec, pu], [s32b[h]])
                            self.cp("pool", sf[:, (n + 1) % 2, h, :], s32[:, h, :], [s32b[h]], [sfn[h]])
                ss, _ = self.nxt("glsm")
                junk, _ = self.nxt("glf")
                for h in range(4):
                    o_ = po[h // 2][:, (h % 2) * 256:(h % 2 + 1) * 256]
                    self.S.op("act", (lambda o_=o_, h=h, junk=junk, ss=ss: nc.scalar.activation(
                        junk[:, 0:256], o_, AF.Square, accum_out=ss[:, h:h + 1])),
                        reads=[po[h // 2]], writes=[junk, ss])
                self.act(ss[:, 4:8], ss[:, 0:4], AF.Sqrt, [ss, self.cb], [ss], scale=1.0 / 256, bias=self.eps_t[:, 0:1])
                self.recip(ss[:, 4:8], ss[:, 4:8], [ss], [ss])
                sr, _ = self.nxt("glf")
                sr2, _ = self.nxt("glf")
                self.act(sr[:, :], rb[:, 0:512], AF.Silu, [rb], [sr])
                self.act(sr2[:, :], rb[:, 512:1024], AF.Silu, [rb], [sr2])
                g2 = gn[:, :].unsqueeze(1).broadcast_to([128, 2, 256])
                self.tt("pool", v2(gs[:, 0:512]), v2(sr[:, :]), g2, ALU.mult, [sr, gnb], [gsb])
                self.tt("pool", v2(gs[:, 512:1024]), v2(sr2[:, :]), g2, ALU.mult, [sr2, gnb], [gsb])
                for h in range(4):
                    o_ = po[h // 2][:, (h % 2) * 256:(h % 2 + 1) * 256]
                    self.stt(ya[:, h * 256:(h + 1) * 256], o_, ss[:, 4 + h:5 + h], gs[:, h * 256:(h + 1) * 256],
                             ALU.mult, ALU.mult, [po[h // 2], ss, gsb], [yab])
                ysb = ystb[(n // 4) % 2]
                ysv = yst[:, (n // 4) % 2]
                for half in range(2):
                    ptb = self.nb()
                    for f in range(4):
                        ft = half * 4 + f
                        self.tr(ptb[:, f * 128:(f + 1) * 128], ya[:, ft * 128:(ft + 1) * 128], self.ident[:],
                                [yab, self.cb], [ptb])
                    self.cp("act" if half == 0 else "dve",
                            ysv[:, half * 4:half * 4 + 4, (n % 4) * 128:(n % 4 + 1) * 128],
                            ptb[:, :].rearrange("p (f t) -> p f t", f=4), [ptb], [ysb])
                if n % 4 == 3:
                    t0 = c0 + (n // 4) * 512
                    S.dma("sp", YT[0:1024, t0:t0 + 512].rearrange("(f p) t -> p f t", p=128), ysv[:, :, :],
                          "glyst%d" % ((n // 4) % 2), reads=[ysb])

            ctxs = {0: stage1(0)}
            for n in range(NCH):
                if n + 1 < NCH:
                    ctxs[n + 1] = stage1(n + 1)
                stage2(n, ctxs.pop(n))

    def build(self, phases=("A0", "B0", "C0", "B1", "C1")):
        self.setup()
        with contextlib.ExitStack() as tes:
            if "A0" in phases:
                self.alloc_tok(tes)
                self.phase_A0()
        if "B0" in phases:
            self.phase_B0()
        if "C0" in phases:
            with contextlib.ExitStack() as tes:
                self.alloc_tok(tes)
                self.phase_C(0)
        if "B1" in phases:
            self.phase_B1()
        if "C1" in phases:
            with contextlib.ExitStack() as tes:
                self.alloc_tok(tes)
                self.phase_C(1)
        self.S.barrier()
        self.S.final_wait()
        return self.nc


MIX_CONST_SHAPES = {"dil_mask": [128, 256], "na_toe": [31, 4096], "na_colmask": [128, 64],
                    "gla_m": [6, 128, 128], "ones_row": [1, T]}


def _mix_consts():
    cs = {}
    k = np.arange(128)[:, None]
    q = np.arange(256)[None, :]
    cs["dil_mask"] = ((q >= k) & (q <= k + 128)).astype(np.float32)
    kc = np.arange(64)[:, None]
    c = np.arange(64)[None, :]
    dc = kc - c + 15
    toe = np.zeros((31, 64, 64), np.float32)
    for i in range(31):
        toe[i] = (dc == i)
    cs["na_toe"] = toe.reshape(31, 4096)
    c0 = np.clip(c - 8, 0, 48)
    ok = (kc >= c0) & (kc < c0 + 16)
    cm = np.where(ok, 0.0, NEG).astype(np.float32)
    cs["na_colmask"] = np.concatenate([cm, cm], 0)
    r = np.arange(128)[:, None]
    i = np.arange(128)[None, :]
    sc = -1.0 / 16.0
    gm = np.stack([(r <= i) * sc, (r >= i) * sc, (r > i) * sc, (r < i) * sc, (r <= i) * 1.0, (r > i) * 1.0]).astype(np.float32)
    cs["gla_m"] = gm
    cs["ones_row"] = np.ones((1, T), np.float32)
    return cs


def build_program(nseq, dbg=(), phases=("A0", "B0", "C0", "B1", "C1")):
    b1 = Builder(nseq, None, dbg)
    b1.build(phases)
    needed = b1.S.needed
    b1.es.close()
    b2 = Builder(nseq, needed, dbg)
    nc = b2.build(phases)
    return nc, b2


_PROG = {}


def kernel(x_prompt, x_sample, p_prompt, p_sample, norm_g, w_in_even, w_out_even, gla_wg_fwd, gla_bg_fwd,
           gla_wg_bwd, gla_bg_bwd, gla_norm_g, na_rpb, w_in_odd, w_out_odd, diff_lambda, diff_subln_g,
           w_ffn_gate, w_ffn_up, w_ffn_down, w_ple_proj, w_ple_gate):
    ncores, nseq = 8, 3
    f32 = lambda a: np.ascontiguousarray(np.asarray(a), dtype=np.float32)
    x_prompt, x_sample, p_prompt, p_sample = f32(x_prompt), f32(x_sample), f32(p_prompt), f32(p_sample)
    wts = {
        "norm_g": f32(norm_g), "w_in_even": f32(w_in_even)[0], "w_out_even": f32(w_out_even)[0],
        "gla_wg_fwd": f32(gla_wg_fwd)[0], "gla_bg_fwd": f32(gla_bg_fwd), "gla_wg_bwd": f32(gla_wg_bwd)[0],
        "gla_bg_bwd": f32(gla_bg_bwd), "gla_norm_g": f32(gla_norm_g), "na_rpb": f32(na_rpb).reshape(120, 31),
        "w_in_odd": f32(w_in_odd)[0], "w_out_odd": f32(w_out_odd)[0], "diff_lambda": f32(diff_lambda)[0],
        "diff_subln_g": f32(diff_subln_g), "w_ffn_gate": f32(w_ffn_gate), "w_ffn_up": f32(w_ffn_up),
        "w_ffn_down": f32(w_ffn_down), "w_ple_proj": f32(w_ple_proj), "w_ple_gate": f32(w_ple_gate),
    }
    for k, shp in WEIGHT_SHAPES.items():
        wts[k] = np.ascontiguousarray(wts[k].reshape(shp))
    consts = _consts()
    consts.update(_mix_consts())
    if "nc" not in _PROG:
        _PROG["nc"] = build_program(nseq)[0]
    nc = _PROG["nc"]
    in_maps = []
    for c in range(ncores):
        xs = np.concatenate([x_prompt[c], x_sample[2 * c], x_sample[2 * c + 1]], axis=0)
        ps = np.concatenate([p_prompt[:, c], p_sample[:, 2 * c], p_sample[:, 2 * c + 1]], axis=1)
        m = {"x": np.ascontiguousarray(xs), "p": np.ascontiguousarray(ps)}
        m.update(wts)
        m.update(consts)
        in_maps.append(m)
    res = run_bass_kernel_spmd(nc, in_maps, core_ids=list(range(ncores)))
    y_prompt = np.empty((8, T, D), np.float32)
    y_sample = np.empty((16, T, D), np.float32)
    for c in range(ncores):
        y = np.asarray(res.results[c]["y"], dtype=np.float32)
        y_prompt[c] = y[0:T]
        y_sample[2 * c] = y[T:2 * T]
        y_sample[2 * c + 1] = y[2 * T:3 * T]
    return (y_prompt, y_sample)
```

```python
import contextlib
import math
import numpy as np
import concourse.bass as bass
import concourse.mybir as mybir
from concourse.bass_utils import run_bass_kernel_spmd

F32 = mybir.dt.float32
BF16 = mybir.dt.bfloat16
AF = mybir.ActivationFunctionType
ALU = mybir.AluOpType

T = 2048
D = 2048
TT = 512
NTT = T // TT
KC = D // 128
DFF = 5632
FC = DFF // 128
PLE = 256
EPS = 1e-6
NEG = -30000.0
EVEN_W = 6176
ODD_W = 8192
ROPE_THETA = 500000.0


class Buf:
    __slots__ = ("ap", "w", "r", "name")

    def __init__(self, ap, name=""):
        self.ap = ap
        self.w = None
        self.r = {}
        self.name = name

    def __getitem__(self, idx):
        return self.ap[idx]


class Sched:
    ENGS = ("pe", "act", "dve", "pool", "sp")

    def __init__(self, nc, es, needed=None):
        self.nc = nc
        self.es = es
        self.dry = needed is None
        self.needed = needed if needed is not None else {n: set() for n in self.ENGS}
        self.eng = {"pe": nc.tensor, "act": nc.scalar, "dve": nc.vector, "pool": nc.gpsimd, "sp": nc.sync}
        self.sem = {n: es.enter_context(nc.semaphore("s_" + n)) for n in self.ENGS}
        self.cnt = {n: 0 for n in self.ENGS}
        self.inc = {n: 0 for n in self.ENGS}
        self.map = {n: {} for n in self.ENGS}
        self.seen = {n: {} for n in self.ENGS}
        self.dsem = {}
        self.dcnt = {}
        self.nwaits = 0

    def dkey(self, key):
        if key not in self.dsem:
            self.dsem[key] = self.es.enter_context(self.nc.semaphore("d_" + key))
            self.dcnt[key] = 0
        return key

    def _wait(self, en, ev):
        if ev[0] == "e":
            _, src, idx = ev
            if src == en and en in ("pe",):
                return
            key = ("e", src)
            if self.seen[en].get(key, 0) >= idx:
                return
            self.seen[en][key] = idx
            if self.dry:
                self.needed[src].add(idx)
            else:
                self.eng[en].wait_ge(self.sem[src], self.map[src][idx])
        else:
            _, k, val = ev
            val = self.dcnt[k]
            key = ("d", k)
            if self.seen[en].get(key, 0) >= val:
                return
            self.seen[en][key] = val
            if not self.dry:
                self.eng[en].wait_ge(self.dsem[k], val)
        self.nwaits += 1

    def _deps(self, en, reads, writes):
        for b in reads:
            if b.w is not None:
                self._wait(en, b.w)
        for b in writes:
            if b.w is not None:
                self._wait(en, b.w)
            for ev in b.r.values():
                self._wait(en, ev)

    def _mark(self, me, rk, reads, writes):
        for b in writes:
            b.w = me
            b.r = {}
        for b in reads:
            b.r[rk] = me

    def op(self, en, fn, reads=(), writes=()):
        self._deps(en, reads, writes)
        self.cnt[en] += 1
        idx = self.cnt[en]
        if not self.dry:
            ins = fn()
            if idx in self.needed[en]:
                self.inc[en] += 1
                self.map[en][idx] = self.inc[en]
                ins.then_inc(self.sem[en], 1)
        me = ("e", en, idx)
        self._mark(me, ("e", en), reads, writes)
        return me

    def dma(self, q, out, in_, key, reads=(), writes=(), **kw):
        self.dkey(key)
        self._deps(q, reads, writes)
        self.dcnt[key] += 16
        if not self.dry:
            self.eng[q].dma_start(out=out, in_=in_, **kw).then_inc(self.dsem[key], 16)
        me = ("d", key, self.dcnt[key])
        self._mark(me, ("d", key), reads, writes)
        return me

    def barrier(self):
        for en in self.ENGS:
            for src in self.ENGS:
                if src != en and self.cnt[src] > 0:
                    self._wait(en, ("e", src, self.cnt[src]))
            for k in self.dsem:
                if self.dcnt[k] > 0:
                    self._wait(en, ("d", k, self.dcnt[k]))

    def final_wait(self):
        for k in self.dsem:
            if self.dcnt[k] > 0:
                self._wait("sp", ("d", k, self.dcnt[k]))


def _rope_tables():
    def tab(rot, period):
        half = rot // 2
        inv = ROPE_THETA ** (-np.arange(half, dtype=np.float32) / half)
        ang = np.arange(T, dtype=np.float32)[None, :] * inv[:, None]
        c = np.ones((128, T), np.float32)
        s = np.zeros((128, T), np.float32)
        perm = np.zeros((128, 128), np.float32)
        for base in range(0, 128, period):
            c[base:base + half] = np.cos(ang)
            c[base + half:base + rot] = np.cos(ang)
            s[base:base + half] = -np.sin(ang)
            s[base + half:base + rot] = np.sin(ang)
            for i in range(half):
                perm[base + half + i, base + i] = 1.0
                perm[base + i, base + half + i] = 1.0
        return c, s, perm
    return tab(32, 128), tab(16, 64)


def _consts():
    cs = {}
    cs["ident"] = np.eye(128, dtype=np.float32)
    (c1, s1, p1), (c2, s2, p2) = _rope_tables()
    cs["rope_c1"], cs["rope_s1"], cs["rope_p1"] = c1, s1, p1
    cs["rope_c2"], cs["rope_s2"], cs["rope_p2"] = c2, s2, p2
    return cs


CONST_SHAPES = {
    "ident": [128, 128],
    "rope_c1": [128, T], "rope_s1": [128, T], "rope_p1": [128, 128],
    "rope_c2": [128, T], "rope_s2": [128, T], "rope_p2": [128, 128],
}

WEIGHT_SHAPES = {
    "norm_g": [2, 5, D], "w_in_even": [D, EVEN_W], "w_out_even": [D, D],
    "gla_wg_fwd": [16, 512], "gla_bg_fwd": [1, 512], "gla_wg_bwd": [16, 512], "gla_bg_bwd": [1, 512],
    "gla_norm_g": [1, 256], "na_rpb": [120, 31], "w_in_odd": [D, ODD_W], "w_out_odd": [D, D],
    "diff_lambda": [4, 64], "diff_subln_g": [1, 128],
    "w_ffn_gate": [2, D, DFF], "w_ffn_up": [2, D, DFF], "w_ffn_down": [2, DFF, D],
    "w_ple_proj": [2, PLE, D], "w_ple_gate": [2, D, D],
}


class Builder:
    def __init__(self, nseq, needed=None, dbg=(), stop_after=None):
        self.nseq = nseq
        self.ntok = nseq * T
        self.dbg = set(dbg)
        self.stop_after = stop_after
        self.nc = bass.Bass("TRN2", target_bir_lowering=False)
        self.es = contextlib.ExitStack()
        self.S = Sched(self.nc, self.es, needed)
        self.dram = {}
        self.uid = 0

    def din(self, name, shape, dt=F32):
        t = self.nc.dram_tensor(name, list(shape), dt, kind="ExternalInput").ap()
        self.dram[name] = t
        return t

    def dscr(self, name, shape, dt):
        kind = "ExternalOutput" if name in self.dbg else "Internal"
        t = self.nc.dram_tensor(name, list(shape), dt, kind=kind).ap()
        self.dram[name] = t
        return t

    def sb(self, es, name, shape, dt):
        self.uid += 1
        return es.enter_context(self.nc.sbuf_tensor("sb%d_%s" % (self.uid, name), list(shape), dt))

    def ps(self, es, name, shape, dt=F32):
        return es.enter_context(self.nc.psum_tensor("ps_" + name, list(shape), dt))

    def mm(self, out, lhsT, rhs, start, stop, reads, writes, **kw):
        nc = self.nc
        return self.S.op("pe", lambda: nc.tensor.matmul(out, lhsT, rhs, start=start, stop=stop, **kw),
                         reads=reads, writes=writes)

    def tr(self, out, in_, ident, reads, writes):
        nc = self.nc
        return self.S.op("pe", lambda: nc.tensor.transpose(out, in_, ident), reads=reads, writes=writes)

    def act(self, out, in_, func, reads, writes, scale=None, bias=None):
        nc = self.nc
        kw = {}
        if scale is not None:
            kw["scale"] = scale
        if bias is not None:
            kw["bias"] = bias
        return self.S.op("act", lambda: nc.scalar.activation(out, in_, func, **kw), reads=reads, writes=writes)

    def tt(self, en, out, in0, in1, op, reads, writes):
        e = self.nc.vector if en == "dve" else self.nc.gpsimd
        return self.S.op(en, lambda: e.tensor_tensor(out, in0, in1, op), reads=reads, writes=writes)

    def ts(self, en, out, in0, s1, op0, reads, writes, s2=None, op1=None):
        e = self.nc.vector if en == "dve" else self.nc.gpsimd
        if op1 is None:
            return self.S.op(en, lambda: e.tensor_scalar(out, in0, s1, None, op0), reads=reads, writes=writes)
        return self.S.op(en, lambda: e.tensor_scalar(out, in0, s1, s2, op0, op1), reads=reads, writes=writes)

    def stt(self, out, in0, scalar, in1, op0, op1, reads, writes):
        nc = self.nc
        return self.S.op("dve", lambda: nc.vector.scalar_tensor_tensor(out, in0, scalar, in1, op0, op1),
                         reads=reads, writes=writes)

    def cp(self, en, out, in_, reads, writes):
        nc = self.nc
        if en == "act":
            return self.S.op("act", lambda: nc.scalar.copy(out, in_), reads=reads, writes=writes)
        e = nc.vector if en == "dve" else nc.gpsimd
        return self.S.op(en, lambda: e.tensor_copy(out, in_), reads=reads, writes=writes)

    def memset(self, en, ap, val, writes):
        e = self.nc.vector if en == "dve" else self.nc.gpsimd
        return self.S.op(en, lambda: e.memset(ap, val), writes=writes)

    def recip(self, out, in_, reads, writes):
        nc = self.nc
        return self.S.op("dve", lambda: nc.vector.reciprocal(out, in_), reads=reads, writes=writes)

    def setup(self):
        nc, es, S = self.nc, self.es, self.S
        ntok = self.ntok
        self.x = self.din("x", [ntok, D])
        self.p = self.din("p", [2, ntok, PLE])
        self.W = {k: self.din(k, v) for k, v in WEIGHT_SHAPES.items()}
        self.C = {k: self.din(k, v) for k, v in CONST_SHAPES.items()}
        self.C.update({k: self.din(k, v) for k, v in MIX_CONST_SHAPES.items()})
        self.y = self.nc.dram_tensor("y", [ntok, D], F32, kind="ExternalOutput").ap()
        sc = {}
        sc["HS"] = self.dscr("HS", [D, ntok], F32)
        for l in range(2):
            sc["YT%d" % l] = self.dscr("YT%d" % l, [D, ntok], BF16)
        for nm, rows in (("QAT", 512), ("KAT", 512), ("QBT", 1024), ("KBT", 1024),
                         ("QCT", 3072), ("KCT", 1024), ("QDT", 1024), ("KDT", 1024)):
            sc[nm] = self.dscr(nm, [rows, ntok], BF16)
        for nm, cols in (("KA", 512), ("VA", 1024), ("RA", 1024), ("VB", 1024), ("VC", 1024), ("VD", 1024)):
            sc[nm] = self.dscr(nm, [ntok, cols], BF16)
        sc["LRT"] = self.dscr("LRT", [32, ntok], F32)
        self.sc = sc
        self.ident = self.sb(es, "ident", [128, 128], F32)
        self.ident_bf = self.sb(es, "ident_bf", [128, 128], BF16)
        self.ones_bf = self.sb(es, "ones_bf", [128, 128], BF16)
        self.eps_t = self.sb(es, "eps_t", [128, 1], F32)
        self.gT = self.sb(es, "gT", [128, 10, KC], F32)
        self.perm1 = self.sb(es, "perm1", [128, 128], BF16)
        self.perm2 = self.sb(es, "perm2", [128, 128], BF16)
        self.cb = Buf(None, "consts")
        ptmp = self.sb(es, "ptmp", [128, 256], F32)
        S.dma("sp", self.ident[:], self.C["ident"][:, :], "c0", writes=[self.cb])
        S.dma("sp", ptmp[:, 0:128], self.C["rope_p1"][:, :], "c0", writes=[self.cb])
        S.dma("sp", ptmp[:, 128:256], self.C["rope_p2"][:, :], "c0", writes=[self.cb])
        with nc.allow_non_contiguous_dma(reason="one-time gain vector gather"):
            for ln in range(10):
                S.dma("sp", self.gT[:, ln, :],
                      self.W["norm_g"][ln // 5, ln % 5, :].rearrange("(k p) -> p k", p=128), "c0",
                      writes=[self.cb])
        self.cp("dve", self.ident_bf[:], self.ident[:], [self.cb], [self.cb])
        self.cp("dve", self.perm1[:], ptmp[:, 0:128], [self.cb], [self.cb])
        self.cp("dve", self.perm2[:], ptmp[:, 128:256], [self.cb], [self.cb])
        self.memset("dve", self.ones_bf[:], 1.0, [self.cb])
        self.memset("dve", self.eps_t[:], EPS, [self.cb])
        self.pb = [Buf(self.ps(es, "pb%d" % i, [128, 512]), "pb%d" % i) for i in range(8)]
        self.mm_banks = self.pb[0:4]
        self.ss_bank = self.pb[4]
        self.tr_banks = self.pb[5:7]
        self.x_bank = self.pb[7]
        self.mm_i = 0
        self.tr_i = 0
        S.barrier()

    def next_mm(self):
        b = self.mm_banks[self.mm_i % len(self.mm_banks)]
        self.mm_i += 1
        return b

    def next_tr(self):
        b = self.tr_banks[self.tr_i % len(self.tr_banks)]
        self.tr_i += 1
        return b

    def alloc_tok(self, es):
        sbt = lambda n, s, d: self.sb(es, n, s, d)
        ht = sbt("hT", [128, KC, TT], F32)
        self.h = [Buf(ht[:, k, :], "h%d" % k) for k in range(KC)]
        xt = sbt("xT", [128, KC, TT], BF16)
        self.xt = [Buf(xt[:, k, :], "x%d" % k) for k in range(KC)]
        sct = sbt("scT", [128, KC, TT], F32)
        self.sct = [Buf(sct[:, k, :], "sc%d" % k) for k in range(KC)]
        self.xin = [Buf(sct[:, 4 * j:4 * j + 4, :].rearrange("p a b -> p (a b)"), "xin%d" % j) for j in range(4)]
        at = sbt("actT", [128, FC, TT], BF16)
        self.actt = [Buf(at[:, f, :], "a%d" % f) for f in range(FC)]
        self.ostg = []
        for j in range(2):
            v = at[:, 8 * j:8 * j + 8, :].rearrange("p a b -> p (a b)").bitcast(F32)
            self.ostg.append((Buf(v, "ostg%d" % j), self.actt[8 * j:8 * j + 8]))
        sq = sbt("sq", [128, 4, TT], BF16)
        self.sq = [Buf(sq[:, i, :], "sq%d" % i) for i in range(4)]
        self.sq_i = 0
        self.stats_pend = []
        tmp = sbt("tmp", [128, 6, TT], F32)
        self.tmp = [Buf(tmp[:, i, :], "tmp%d" % i) for i in range(6)]
        self.tmp_i = 0
        rs = sbt("rstd", [128, 3, TT], F32)
        self.rstd = [Buf(rs[:, i, :], "rstd%d" % i) for i in range(3)]
        rtm = sbt("rtm", [128, 2, 4], F32)
        self.rtm = [Buf(rtm[:, i, :], "rtm%d" % i) for i in range(2)]
        self.rtm_i = 0
        self.rstd_i = 0
        self.NW = 4
        wt = sbt("wslots", [128, self.NW, 4096], BF16)
        self.wslot = [Buf(wt[:, i, :], "w%d" % i) for i in range(self.NW)]
        stg = sbt("stg", [128, 6, TT], BF16)
        self.stg = [Buf(stg[:, i, :], "stg%d" % i) for i in range(6)]
        self.stg_i = 0
        qs_ = sbt("qs", [128, 4, TT], BF16)
        self.qs = [Buf(qs_[:, i, :], "qs%d" % i) for i in range(4)]
        self.qs_i = 0
        rp = sbt("rope", [128, 4, TT], F32)
        self.rope = [Buf(rp[:, i, :], "rope%d" % i) for i in range(4)]
        pt = sbt("pT", [128, 2, TT], BF16)
        self.pt = [Buf(pt[:, i, :], "pT%d" % i) for i in range(2)]
        pin = sbt("pin", [128, 4, PLE], F32)
        self.pin = Buf(pin, "pin")

    def nxt(self, name):
        lst = getattr(self, name)
        i = getattr(self, name + "_i")
        setattr(self, name + "_i", i + 1)
        return lst[i % len(lst)], i % len(lst)

    def wstream_begin(self, specs, ntiles, name):
        self.wspecs = specs
        self.w_n = len(specs)
        self.w_total = len(specs) * ntiles
        self.w_issued = 0
        self.w_used = 0
        self.wscr = self.dscr("WS_" + name, [len(specs), 128, 4096], BF16)
        self.wscr_b = [Buf(None, "ws%d" % i) for i in range(len(specs))]

    def wnext(self):
        S = self.S
        while self.w_issued < self.w_total and self.w_issued < self.w_used + self.NW - 1:
            bi = self.w_issued % self.w_n
            src, a, b = self.wspecs[bi]
            slot = self.wslot[self.w_issued % self.NW]
            key = "w%d" % (self.w_issued % self.NW)
            if self.w_issued < self.w_n:
                dst = slot[:, 0:a * b].rearrange("p (a b) -> p a b", b=b)
                S.dma("pool", dst, src, key, writes=[slot], max_dma_last_dim=4096)
                S.dma("sp", self.wscr[bi, :, 0:a * b], slot[:, 0:a * b], "wsb", reads=[slot], writes=[self.wscr_b[bi]])
            else:
                S.dma("pool", slot[:, 0:a * b], self.wscr[bi, :, 0:a * b], key, reads=[self.wscr_b[bi]], writes=[slot])
            self.w_issued += 1
        src, a, b = self.wspecs[self.w_used % self.w_n]
        slot = self.wslot[self.w_used % self.NW]
        self.w_used += 1
        return slot, slot[:, 0:a * b].rearrange("p (a b) -> p a b", b=b)

    def wspec_cols(self, w2d, c0, ncols):
        return (w2d[:, c0:c0 + ncols].rearrange("(k p) n -> p k n", p=128), w2d.shape[0] // 128, ncols)

    def stats_chunk(self, c, n, src_buf, src_ap, eng):
        sq, _ = self.nxt("sq")
        if eng == "act":
            self.act(sq[:, :], src_ap, AF.Square, [src_buf], [sq])
        else:
            self.tt(eng, sq[:, :], src_ap, src_ap, ALU.mult, [src_buf], [sq])
        self.stats_pend.append((sq, c == 0, c == n - 1))
        while len(self.stats_pend) > 2:
            self.stats_flush_one()

    def stats_flush_one(self):
        sq, first, last = self.stats_pend.pop(0)
        self.mm(self.ss_bank[:, :], self.ones_bf[:], sq[:, :], first, last, [sq, self.cb], [self.ss_bank])

    def stats_finish(self, nfeat):
        while self.stats_pend:
            self.stats_flush_one()
        r, _ = self.nxt("rstd")
        self.act(r[:, :], self.ss_bank[:, :], AF.Sqrt, [self.ss_bank, self.cb], [r],
                 scale=1.0 / nfeat, bias=self.eps_t[:, 0:1])
        self.recip(r[:, :], r[:, :], [r], [r])
        return r

    def g(self, l, n, k):
        return self.gT[:, l * 5 + n, k:k + 1]

    def win_plan(self, l):
        if l == 0:
            return [("QAT", 0, 512, "F", 128 ** -0.5, 0), ("KAT", 512, 512, "FT", 1.0, 0),
                    ("VA", 1024, 1024, "T", 1.0, 0), ("RA", 2048, 1024, "T", 1.0, 0),
                    ("LRT", 3072, 32, "L", 1.0, 0),
                    ("QBT", 3104, 1024, "F", 128 ** -0.5, 0), ("KBT", 4128, 1024, "F", 1.0, 0),
                    ("VB", 5152, 1024, "T", 1.0, 0)]
        return [("QCT", 0, 3072, "F", 128 ** -0.5, 1), ("KCT", 3072, 1024, "F", 1.0, 1),
                ("VC", 4096, 1024, "T", 1.0, 0),
                ("QDT", 5120, 1024, "F", 64 ** -0.5, 2), ("KDT", 6144, 1024, "F", 1.0, 2),
                ("VD", 7168, 1024, "T", 1.0, 0)]

    def win_wspecs(self, l):
        w = self.W["w_in_even" if l == 0 else "w_in_odd"]
        specs = []
        for (nm, c0, ncols, mode, scale, rope) in self.win_plan(l):
            if mode == "L":
                specs.append(self.wspec_cols(w, c0, 32))
            else:
                for b in range(ncols // 256):
                    specs.append(self.wspec_cols(w, c0 + 256 * b, 256))
        return specs

    def chain_wspecs(self, l):
        specs = []
        wo = self.W["w_out_even" if l == 0 else "w_out_odd"]
        for b in range(D // 256):
            specs.append(self.wspec_cols(wo, 256 * b, 256))
        wg, wu, wd = self.W["w_ffn_gate"][l], self.W["w_ffn_up"][l], self.W["w_ffn_down"][l]
        for b in range(DFF // 256):
            specs.append(self.wspec_cols(wg, 256 * b, 256))
            specs.append(self.wspec_cols(wu, 256 * b, 256))
        for c in range(KC):
            for hlf in range(2):
                src = wd[hlf * 2816:(hlf + 1) * 2816, c * 128:(c + 1) * 128].rearrange("(k p) n -> p k n", p=128)
                specs.append((src, 22, 128))
        wp = self.W["w_ple_proj"][l]
        for hlf in range(2):
            specs.append((wp[:, hlf * 1024:(hlf + 1) * 1024].rearrange("(k p) n -> p k n", p=128), 2, 1024))
        wpg = self.W["w_ple_gate"][l]
        for b in range(D // 256):
            specs.append(self.wspec_cols(wpg, 256 * b, 256))
        return specs

    def prenorm_chunk(self, l, n, c):
        self.stats_chunk(c, KC, self.h[c], self.h[c][:, :], "act")
        self.act(self.xt[c][:, :], self.h[c][:, :], AF.Identity, [self.h[c], self.cb], [self.xt[c]],
                 scale=self.g(l, n, c))

    def rstd_tokmajor(self, r):
        trb = self.next_tr()
        for j in range(4):
            self.tr(trb[:, j * 128:(j + 1) * 128], r[:, j * 128:(j + 1) * 128], self.ident[:], [r, self.cb], [trb])
        rt, _ = self.nxt("rtm")
        self.cp("dve", rt[:, 0:4], trb[:, :].rearrange("p (j t) -> p j t", j=4)[:, :, 0], [trb], [rt])
        return rt

    def win_stage(self, l, tok0, pos0):
        S = self.S
        sc = self.sc
        rr = {}

        def get_r():
            if "r" not in rr:
                rr["r"] = self.stats_finish(D)
                rr["rt"] = self.rstd_tokmajor(rr["r"])
            return rr["r"], rr["rt"]
        if l == 1:
            for i, nm in enumerate(("rope_c1", "rope_s1", "rope_c2", "rope_s2")):
                S.dma("sp", self.rope[i][:, :], self.C[nm][:, pos0:pos0 + TT], "rope", writes=[self.rope[i]])
        rope_pend = []
        for (nm, c0, ncols, mode, scale, rope) in self.win_plan(l):
            dst = sc[nm]
            if rope == 0:
                while rope_pend:
                    rope_pend.pop(0)()
            if mode == "L":
                wb, wv = self.wnext()
                pbk = self.next_mm()
                for k in range(KC):
                    self.mm(pbk[0:32, :], wv[:, k, 0:32], self.xt[k][:, :], k == 0, k == KC - 1,
                            [wb, self.xt[k]], [pbk])
                t, _ = self.nxt("tmp")
                r, rt = get_r()
                self.tt("dve", t[0:32, :], pbk[0:32, :], r[0:32, :], ALU.mult, [pbk, r], [t])
                S.dma("sp", dst[:, tok0:tok0 + TT], t[0:32, :], "tmp_st", reads=[t])
                continue
            for b in range(ncols // 256):
                wb, wv = self.wnext()
                if "F" in mode:
                    for sub in range(2):
                        f0 = 256 * b + 128 * sub
                        pbk = self.next_mm()
                        for k in range(KC):
                            self.mm(pbk[:, :], wv[:, k, sub * 128:(sub + 1) * 128], self.xt[k][:, :],
                                    k == 0, k == KC - 1, [wb, self.xt[k]], [pbk])
                        st, si = self.nxt("stg")
                        r, rt = get_r()
                        if rope == 0:
                            self.stt(st[:, :], pbk[:, :], scale, r[:, :], ALU.mult, ALU.mult, [pbk, r], [st])
                        else:
                            perm = self.perm1 if rope == 1 else self.perm2
                            rc, rs = (self.rope[0], self.rope[1]) if rope == 1 else (self.rope[2], self.rope[3])
                            qs, _ = self.nxt("qs")
                            self.stt(qs[:, :], pbk[:, :], scale, r[:, :], ALU.mult, ALU.mult, [pbk, r], [qs])

                            def fin(qs=qs, perm=perm, rc=rc, rs=rs, st=st, si=si, f0=f0, dst=dst):
                                trb = self.next_tr()
                                self.mm(trb[:, :], perm[:], qs[:, :], True, True, [qs, self.cb], [trb])
                                t2, _ = self.nxt("tmp")
                                self.tt("pool", t2[:, :], qs[:, :], rc[:, :], ALU.mult, [qs, rc], [t2])
                                t3, _ = self.nxt("tmp")
                                self.tt("dve", t3[:, :], trb[:, :], rs[:, :], ALU.mult, [trb, rs], [t3])
                                self.tt("pool", st[:, :], t2[:, :], t3[:, :], ALU.add, [t2, t3], [st])
                                S.dma("sp", dst[f0:f0 + 128, tok0:tok0 + TT], st[:, :], "stg%d" % si, reads=[st])
                            rope_pend.append(fin)
                            while len(rope_pend) > 1:
                                rope_pend.pop(0)()
                            continue
                        S.dma("sp", dst[f0:f0 + 128, tok0:tok0 + TT], st[:, :], "stg%d" % si, reads=[st])
                if "T" in mode:
                    dstT = sc["KA"] if mode == "FT" else dst
                    for jp in range(2):
                        pbk = self.next_mm()
                        for jj in range(2):
                            j = 2 * jp + jj
                            for k in range(KC):
                                self.mm(pbk[:, jj * 256:(jj + 1) * 256], self.xt[k][:, j * 128:(j + 1) * 128],
                                        wv[:, k, :], k == 0, k == KC - 1, [wb, self.xt[k]], [pbk])
                        st, si = self.nxt("stg")
                        r, rt = get_r()
                        assert scale == 1.0
                        for jj in range(2):
                            j = 2 * jp + jj
                            self.act(st[:, jj * 256:(jj + 1) * 256], pbk[:, jj * 256:(jj + 1) * 256], AF.Identity,
                                     [pbk, rt], [st], scale=rt[:, j:j + 1])
                        r0 = tok0 + jp * 256
                        S.dma("sp", dstT[r0:r0 + 256, 256 * b:256 * b + 256].rearrange("(j p) c -> p j c", p=128),
                              st[:, :].rearrange("p (j c) -> p j c", c=256), "stg%d" % si, reads=[st])

        while rope_pend:
            rope_pend.pop(0)()

    def phase_A0(self):
        S = self.S
        tiles = [(s, t) for s in range(self.nseq) for t in range(NTT)]
        self.wstream_begin(self.win_wspecs(0), len(tiles), "A0")
        def load_x(tok0):
            for j in range(4):
                S.dma("sp", self.xin[j][:, :], self.x[tok0 + j * 128:tok0 + (j + 1) * 128, :], "xin%d" % j,
                      writes=[self.xin[j]])

        load_x(0)
        for tix, (s, t) in enumerate(tiles):
            tok0 = s * T + t * TT
            for k in range(KC):
                trb = self.next_tr()
                for j in range(4):
                    self.tr(trb[:, j * 128:(j + 1) * 128], self.xin[j][:, k * 128:(k + 1) * 128], self.ident[:],
                            [self.xin[j], self.cb], [trb])
                self.cp("act" if k % 2 == 0 else "dve", self.h[k][:, :], trb[:, :], [trb], [self.h[k]])
                S.dma("sp", self.sc["HS"][k * 128:(k + 1) * 128, tok0:tok0 + TT], self.h[k][:, :], "hst",
                      reads=[self.h[k]])
                self.prenorm_chunk(0, 0, k)
            if tix + 1 < len(tiles):
                s2, t2 = tiles[tix + 1]
                load_x(s2 * T + t2 * TT)
            self.win_stage(0, tok0, t * TT)
        S.barrier()

    def proj_chunk(self, wb, wv, sub, pbk, xbufs=None, nk=KC, k0=0, first=True, last=True):
        xb = self.xt if xbufs is None else xbufs
        for k in range(nk):
            self.mm(pbk[:, :], wv[:, k, sub * 128:(sub + 1) * 128], xb[k0 + k][:, :],
                    first and k == 0, last and k == nk - 1, [wb, xb[k0 + k]], [pbk])

    def postnorm_residual(self, l, n, cast_xt=False, pre=None):
        r = self.stats_finish(D)
        for c in range(KC):
            t, _ = self.nxt("tmp")
            self.tt("pool", t[:, :], self.sct[c][:, :], r[:, :], ALU.mult, [self.sct[c], r], [t])
            self.stt(self.h[c][:, :], t[:, :], self.g(l, n, c), self.h[c][:, :], ALU.mult, ALU.add,
                     [t, self.h[c], self.cb], [self.h[c]])
            if cast_xt:
                self.cp("act", self.xt[c][:, :], self.h[c][:, :], [self.h[c]], [self.xt[c]])
            if pre is not None:
                self.prenorm_chunk(pre[0], pre[1], c)

    def phase_C(self, l):
        S = self.S
        last = (l == 1)
        tiles = [(s, ti) for s in range(self.nseq) for ti in range(NTT)]
        specs = self.chain_wspecs(l)
        if not last:
            specs += self.win_wspecs(l + 1)
        self.wstream_begin(specs, len(tiles), "C%d" % l)
        YT = self.sc["YT%d" % l]
        ybuf = self.actt[16:32]

        def load_h(tok0):
            for k in range(KC):
                S.dma("sp", self.h[k][:, :], self.sc["HS"][k * 128:(k + 1) * 128, tok0:tok0 + TT], "hld",
                      writes=[self.h[k]])

        def load_y(tok0):
            for k in range(KC):
                S.dma("sp", ybuf[k][:, :], YT[k * 128:(k + 1) * 128, tok0:tok0 + TT], "yld", writes=[ybuf[k]])

        def load_p(tok0):
            S.dma("sp", self.pin.ap[:, :, :],
                  self.p[l, tok0:tok0 + TT, :].rearrange("(j p) c -> p j c", p=128), "pld", writes=[self.pin])

        toks = [s_ * T + ti * TT for (s_, ti) in tiles]
        load_y(toks[0])
        load_h(toks[0])
        load_p(toks[0])
        for tix, (s, ti) in enumerate(tiles):
            tok0 = toks[tix]
            nxt_tok = toks[tix + 1] if tix + 1 < len(tiles) else None
            for b in range(D // 256):
                wb, wv = self.wnext()
                for sub in range(2):
                    c = 2 * b + sub
                    pbk = self.next_mm()
                    self.proj_chunk(wb, wv, sub, pbk, xbufs=ybuf)
                    self.cp("act", self.sct[c][:, :], pbk[:, :], [pbk], [self.sct[c]])
                    self.stats_chunk(c, KC, pbk, pbk[:, :], "act")
            self.postnorm_residual(l, 1, pre=(l, 2))
            r2 = None
            for b in range(DFF // 256):
                wbg, wvg = self.wnext()
                wbu, wvu = self.wnext()
                for sub in range(2):
                    f = 2 * b + sub
                    pg = self.next_mm()
                    self.proj_chunk(wbg, wvg, sub, pg)
                    pu = self.next_mm()
                    self.proj_chunk(wbu, wvu, sub, pu)
                    if r2 is None:
                        r2 = self.stats_finish(D)
                    t, _ = self.nxt("tmp")
                    t2, _ = self.nxt("tmp")
                    self.tt("dve", t[:, :], pg[:, :], r2[:, :], ALU.mult, [pg, r2], [t])
                    self.act(t[:, :], t[:, :], AF.Silu, [t], [t])
                    self.tt("dve", t2[:, :], pu[:, :], r2[:, :], ALU.mult, [pu, r2], [t2])
                    self.tt("dve", self.actt[f][:, :], t[:, :], t2[:, :], ALU.mult, [t, t2], [self.actt[f]])
            for c in range(KC):
                pbk = self.next_mm()
                for hlf in range(2):
                    wb, wv = self.wnext()
                    self.proj_chunk(wb, wv, 0, pbk, xbufs=self.actt, nk=22, k0=22 * hlf,
                                    first=(hlf == 0), last=(hlf == 1))
                self.cp("act", self.sct[c][:, :], pbk[:, :], [pbk], [self.sct[c]])
                self.stats_chunk(c, KC, pbk, pbk[:, :], "act")
            if nxt_tok is not None:
                load_y(nxt_tok)
            self.postnorm_residual(l, 3, cast_xt=True)
            for kk in range(2):
                trb = self.next_tr()
                for j in range(4):
                    self.tr(trb[:, j * 128:(j + 1) * 128], self.pin.ap[:, j, kk * 128:(kk + 1) * 128], self.ident[:],
                            [self.pin, self.cb], [trb])
                self.cp("dve", self.pt[kk][:, :], trb[:, :], [trb], [self.pt[kk]])
            if nxt_tok is not None:
                load_p(nxt_tok)
            for hlf in range(2):
                wb, wv = self.wnext()
                for sub in range(8):
                    c = 8 * hlf + sub
                    pbk = self.next_mm()
                    self.proj_chunk(wb, wv, sub, pbk, xbufs=self.pt, nk=2)
                    self.cp("act", self.sct[c][:, :], pbk[:, :], [pbk], [self.sct[c]])
                    self.stats_chunk(c, KC, pbk, pbk[:, :], "act")
            r4 = self.stats_finish(D)
            for b in range(D // 256):
                wb, wv = self.wnext()
                for sub in range(2):
                    c = 2 * b + sub
                    pbk = self.next_mm()
                    self.proj_chunk(wb, wv, sub, pbk)
                    sg, _ = self.nxt("tmp")
                    self.act(sg[:, :], pbk[:, :], AF.Sigmoid, [pbk], [sg])
                    t1, _ = self.nxt("tmp")
                    self.tt("pool", t1[:, :], self.sct[c][:, :], r4[:, :], ALU.mult, [self.sct[c], r4], [t1])
                    self.stt(t1[:, :], t1[:, :], self.g(l, 4, c), sg[:, :], ALU.mult, ALU.mult,
                             [t1, sg, self.cb], [t1])
                    self.tt("dve", self.h[c][:, :], self.h[c][:, :], t1[:, :], ALU.add, [self.h[c], t1], [self.h[c]])
                    if not last:
                        self.stats_chunk(c, KC, self.h[c], self.h[c][:, :], "act")
            if last:
                for j in range(4):
                    (ob, alias) = self.ostg[j % 2]
                    for k4 in range(4):
                        trb = self.next_tr()
                        for kk in range(4):
                            k = 4 * k4 + kk
                            self.tr(trb[:, kk * 128:(kk + 1) * 128], self.h[k][:, j * 128:(j + 1) * 128], self.ident[:],
                                    [self.h[k], self.cb], [trb])
                        self.cp("act" if k4 % 2 == 0 else "dve", ob[:, k4 * 512:(k4 + 1) * 512], trb[:, :],
                                [trb], [ob] + alias)
                    S.dma("sp", self.y[tok0 + j * 128:tok0 + (j + 1) * 128, :], ob[:, :], "ost%d" % (j % 2),
                          reads=[ob] + alias)
                if nxt_tok is not None:
                    load_h(nxt_tok)
            else:
                for k in range(KC):
                    S.dma("sp", self.sc["HS"][k * 128:(k + 1) * 128, tok0:tok0 + TT], self.h[k][:, :], "hst",
                          reads=[self.h[k]])
                for c in range(KC):
                    self.act(self.xt[c][:, :], self.h[c][:, :], AF.Identity, [self.h[c], self.cb], [self.xt[c]],
                             scale=self.g(l + 1, 0, c))
                if nxt_tok is not None:
                    load_h(nxt_tok)
                self.win_stage(l + 1, tok0, ti * TT)
        S.barrier()


    def phase_B1(self):
        with contextlib.ExitStack() as es:
            self.mix_common(es)
            self.dilated_all(es)
        self.S.barrier()
        with contextlib.ExitStack() as es:
            self.mix_common(es)
            self.diff_all(es)
        self.S.barrier()

    def mix_common(self, es):
        sbt = lambda n, s, d: self.sb(es, n, s, d)
        pt = sbt("mx_pt", [128, 6, 512], BF16)
        self.mpt = [Buf(pt[:, i, :], "mpt%d" % i) for i in range(6)]
        self.mpt_i = 0
        t32 = sbt("mx_t32", [128, 4, 512], F32)
        self.m32 = [Buf(t32[:, i, :], "m32_%d" % i) for i in range(4)]
        self.m32_i = 0
        st = sbt("mx_stg", [128, 4, 512], BF16)
        self.mstg = [Buf(st[:, i, :], "mstg%d" % i) for i in range(4)]
        self.mstg_i = 0


    def pipe(self, n, A, B, C, L=3):
        ctx = {}
        for i in range(n + L):
            if i < n:
                c = A(i)
                ctx[i] = B(i, c)
            if i >= L:
                C(i - L, ctx.pop(i - L))
            if self.later:
                due = [f for (d, f) in self.later if d <= i]
                self.later = [(d, f) for (d, f) in self.later if d > i]
                for f in due:
                    f()
        for (d, f) in self.later:
            f()
        self.later = []

    def diff_all(self, es):
        S, nc = self.S, self.nc
        sbt = lambda n, s, d: self.sb(es, n, s, d)
        lam_init = 0.8 - 0.6 * math.exp(-0.3 * 1)
        lam = sbt("df_lam", [128, 256], F32)
        lamb = Buf(lam, "lam")
        sc8 = sbt("df_sc", [128, 8], F32)
        scb = Buf(sc8, "dfsc")
        S.dma("sp", lam[:, :], self.W["diff_lambda"].rearrange("a b -> (a b)").partition_broadcast(128), "c0",
              writes=[lamb])
        with nc.allow_non_contiguous_dma(reason="tiny per-partition gain load"):
            S.dma("sp", sc8[:, 4:5], self.W["diff_subln_g"].rearrange("a p -> p a"), "c0", writes=[scb])
        prod = sbt("df_prod", [128, 128], F32)
        pb_ = Buf(prod, "prod")
        self.tt("dve", prod[:, 0:64], lam[:, 0:64], lam[:, 64:128], ALU.mult, [lamb], [pb_])
        self.tt("dve", prod[:, 64:128], lam[:, 128:192], lam[:, 192:256], ALU.mult, [lamb], [pb_])
        self.S.op("dve", lambda: nc.vector.tensor_reduce(sc8[:, 0:1], prod[:, 0:64], mybir.AxisListType.X, ALU.add),
                  reads=[pb_], writes=[scb])
        self.S.op("dve", lambda: nc.vector.tensor_reduce(sc8[:, 1:2], prod[:, 64:128], mybir.AxisListType.X, ALU.add),
                  reads=[pb_], writes=[scb])
        self.act(sc8[:, 0:2], sc8[:, 0:2], AF.Exp, [scb], [scb])
        self.tt("dve", sc8[:, 2:3], sc8[:, 1:2], sc8[:, 0:1], ALU.subtract, [scb], [scb])
        self.ts("dve", sc8[:, 3:4], sc8[:, 2:3], -lam_init, ALU.add, [scb], [scb])
        self.ts("dve", sc8[:, 5:6], sc8[:, 4:5], 1.0 - lam_init, ALU.mult, [scb], [scb])
        neglam = sc8[:, 3:4]
        gsub = sc8[:, 5:6]
        vt = sbt("df_v", [128, 16, 1024], BF16)
        vb = Buf(vt, "dfv")
        qk = sbt("df_qk", [128, 2, 2, T], BF16)
        qkb = [(Buf(qk[:, i, 0, :], "dfq%d" % i), Buf(qk[:, i, 1, :], "dfk%d" % i)) for i in range(2)]
        kz = sbt("df_kz", [128, 2, 2, T], BF16)
        kzb = [[Buf(kz[:, i, c, :], "dfkz%d_%d" % (i, c)) for c in range(2)] for i in range(2)]
        for i in range(2):
            for c in range(2):
                self.memset("pool", kz[:, i, c, :], 0.0, [kzb[i][c]])
        YT = self.sc["YT1"]
        self.later = []
        dsq = sbt("df_sq", [128, 2, 512], BF16)
        self.dfsq = [Buf(dsq[:, i, :], "dfsq%d" % i) for i in range(2)]
        self.dfsq_i = 0
        for s in range(self.nseq):
            S.dma("sp", vt[:, :, :], self.sc["VD"][s * T:(s + 1) * T, :].rearrange("(k p) c -> p k c", p=128), "dfv",
                  writes=[vb])
            steps = [(h, qblk, c, kt) for h in range(8) for qblk in range(4) for c in range(2) for kt in range(16)]
            sbanks = self.pb[4:7]
            state = {"res": {}}

            def load_head(h):
                qb, kb = qkb[h % 2]
                S.dma("sp", qb[:, :], self.sc["QDT"][h * 128:(h + 1) * 128, s * T:(s + 1) * T], "dfq%d" % (h % 2),
                      writes=[qb])
                S.dma("sp", kb[:, :], self.sc["KDT"][h * 128:(h + 1) * 128, s * T:(s + 1) * T], "dfk%d" % (h % 2),
                      writes=[kb])
                self.cp("pool", kz[0:64, h % 2, 0, :], kb[0:64, :], [kb], [kzb[h % 2][0]])
                self.cp("pool", kz[64:128, h % 2, 1, :], kb[64:128, :], [kb], [kzb[h % 2][1]])

            load_head(0)

            def A(i):
                h, qblk, c, kt = steps[i]
                if qblk == 0 and c == 0 and kt == 0 and h + 1 < 8:
                    load_head(h + 1)
                qb, kb = qkb[h % 2]
                sb_ = sbanks[i % 3]
                q0 = qblk * 512
                kzc = kzb[h % 2][c]
                self.mm(sb_[:, :], kzc[:, kt * 128:(kt + 1) * 128], qb[:, q0:q0 + 512], True, True, [kzc, qb], [sb_])
                return sb_

            def B(i, sb_):
                p_, _ = self.nxt("mpt")
                self.act(p_[:, :], sb_[:, :], AF.Exp, [sb_], [p_])
                return p_

            def C(i, p_):
                h, qblk, c, kt = steps[i]
                ob, zb = self.pb[c], self.pb[2 + c]
                self.mm(ob[:, :], vt[:, kt, h * 128:(h + 1) * 128], p_[:, :], kt == 0, kt == 15, [vb, p_], [ob])
                self.mm(zb[:, :], self.ones_bf[:], p_[:, :], kt == 0, kt == 15, [self.cb, p_], [zb])
                if kt == 15:
                    r, _ = self.nxt("m32")
                    self.recip(r[:, :], zb[:, :], [zb], [r])
                    self.tt("dve", r[:, :], ob[:, :], r[:, :], ALU.mult, [ob, r], [r])
                    state["res"][c] = r
                    if c == 1:
                        r0, r1 = state["res"][0], state["res"][1]
                        q0 = qblk * 512
                        self.stt(r0[:, :], r1[:, :], neglam, r0[:, :], ALU.mult, ALU.add, [r0, r1, scb], [r0])
                        sq, _ = self.nxt("dfsq")
                        self.act(sq[:, :], r0[:, :], AF.Square, [r0], [sq])

                        def fin(r0=r0, r1=r1, sq=sq, h=h, q0=q0):
                            ssb = self.x_bank_df
                            self.mm(ssb[:, :], self.ones_bf[:], sq[:, :], True, True, [self.cb, sq], [ssb])
                            self.act(r1[:, :], ssb[:, :], AF.Sqrt, [ssb, self.cb], [r1], scale=1.0 / 128,
                                     bias=self.eps_t[:, 0:1])
                            self.recip(r1[:, :], r1[:, :], [r1], [r1])
                            st, sti = self.nxt("mstg")
                            self.stt(st[:, :], r0[:, :], gsub, r1[:, :], ALU.mult, ALU.mult, [r0, r1, scb], [st])
                            S.dma("sp", YT[1024 + h * 128:1024 + (h + 1) * 128, s * T + q0:s * T + q0 + 512], st[:, :],
                                  "mstg%d" % sti, reads=[st])
                        self.later.append((i + 3 + 6, fin))

            self.x_bank_df = self.pb[7]
            self.pipe(len(steps), A, B, C, L=3)

    def dilated_all(self, es):
        S, nc = self.S, self.nc
        sbt = lambda n, s, d: self.sb(es, n, s, d)
        mk32 = sbt("dl_mk32", [128, 256], F32)
        mk = sbt("dl_mk", [128, 256], BF16)
        mkb = Buf(mk, "dlmask")
        S.dma("sp", mk32[:, :], self.C["dil_mask"][:, :], "c0", writes=[mkb])
        self.ts("dve", mk[:, :], mk32[:, :], -1.0, ALU.add, [mkb], [mkb], s2=-NEG, op1=ALU.mult)
        vts = [sbt("dl_v%d" % g, [128, 16, 1024], BF16) for g in range(3)]
        vbs = [Buf(vts[g], "dlv%d" % g) for g in range(3)]
        kt_ = sbt("dl_k", [128, 2, T], BF16)
        kbuf = [Buf(kt_[:, i, :], "dlk%d" % i) for i in range(2)]
        kp_ = sbt("dl_kp", [128, 2, 2, T], BF16)
        kpb = [[Buf(kp_[:, j, i, :], "dlkp%d_%d" % (j, i)) for i in range(2)] for j in range(2)]
        qt_ = sbt("dl_q", [128, 2, 3, T], BF16)
        qbuf = [[Buf(qt_[:, j, i, :], "dlq%d_%d" % (j, i)) for i in range(3)] for j in range(2)]
        qp_ = sbt("dl_qp", [128, 2, 2, T], BF16)
        qpb = [[Buf(qp_[:, j, i, :], "dlqp%d_%d" % (j, i)) for i in range(2)] for j in range(2)]
        acc = sbt("dl_acc", [128, 2, T], F32)
        acco, accz = Buf(acc[:, 0, :], "acco"), Buf(acc[:, 1, :], "accz")
        yst = sbt("dl_y", [128, T], BF16)
        ystb = Buf(yst, "dly")
        YT = self.sc["YT1"]
        DIL = (1, 4, 16)
        self.later = []
        for s in range(self.nseq):
            vsrc = self.sc["VC"][s * T:(s + 1) * T, :]
            S.dma("sp", vts[0][:, :, :], vsrc.rearrange("(k p) c -> p k c", p=128), "dlv0", writes=[vbs[0]])
            for b in range(4):
                S.dma("sp", vts[1][:, :, :].rearrange("p (r b) c -> p r b c", r=4)[:, :, b, :],
                      vsrc.rearrange("(b p r) c -> p r b c", p=128, r=4)[:, :, b, :], "dlv1", writes=[vbs[1]])
            S.dma("sp", vts[2][:, :, :], vsrc.rearrange("(p r) c -> p r c", r=16), "dlv2", writes=[vbs[2]])

            def prep_head(h):
                j = h % 2
                kb = kbuf[j]
                S.dma("sp", kb[:, :], self.sc["KCT"][h * 128:(h + 1) * 128, s * T:(s + 1) * T], "dlk%d" % j, writes=[kb])
                for g in range(3):
                    S.dma("sp", qbuf[j][g][:, :],
                          self.sc["QCT"][g * 1024 + h * 128:g * 1024 + (h + 1) * 128, s * T:(s + 1) * T],
                          "dlq%d_%d" % (j, g), writes=[qbuf[j][g]])
                for gi, d in ((0, 4), (1, 16)):
                    self.cp("pool", kpb[j][gi][:, :].rearrange("p (r i) -> p r i", r=d),
                            kb[:, :].rearrange("p (i r) -> p r i", r=d), [kb], [kpb[j][gi]])
                    self.cp("act", qpb[j][gi][:, :].rearrange("p (r i) -> p r i", r=d),
                            qbuf[j][gi + 1][:, :].rearrange("p (i r) -> p r i", r=d), [qbuf[j][gi + 1]], [qpb[j][gi]])

            steps = []
            blk = 0
            for h in range(8):
                for g in range(3):
                    d = DIL[g]
                    n = T // d
                    for rho in range(d):
                        for Q0 in range(0, n, 512):
                            Q1 = min(Q0 + 512, n)
                            b_lo = max(0, (Q0 - 64) // 128)
                            b_hi = min(n // 128 - 1, (Q1 - 1 + 64) // 128)
                            bl = []
                            for b in range(b_lo, b_hi + 1):
                                qs = max(Q0, 128 * b - 64)
                                qe = min(Q1, 128 * b + 192)
                                if qe - qs > 0:
                                    bl.append((b, qs, qe))
                            for bi, (b, qs, qe) in enumerate(bl):
                                steps.append(dict(h=h, g=g, d=d, n=n, rho=rho, Q0=Q0, Q1=Q1, b=b, qs=qs, qe=qe,
                                                  first=(bi == 0), last=(bi == len(bl) - 1), blk=blk,
                                                  hfirst=(g == 0 and Q0 == 0 and bi == 0),
                                                  hlast=(g == 2 and rho == d - 1 and bi == len(bl) - 1)))
                            blk += 1
            prep_head(0)
            sbanks = self.pb[4:8]

            def A(i):
                st = steps[i]
                h, g, n, rho, b = st["h"], st["g"], st["n"], st["rho"], st["b"]
                if st["hfirst"] and h + 1 < 8:
                    prep_head(h + 1)
                j = h % 2
                ksrc = kbuf[j] if g == 0 else kpb[j][g - 1]
                qsrc = qbuf[j][0] if g == 0 else qpb[j][g - 1]
                N = st["qe"] - st["qs"]
                sbk = sbanks[i % 4]
                kbase = rho * n + 128 * b
                qoff = st["qs"] - (128 * b - 64)
                self.mm(sbk[:, 0:N], self.ident_bf[:], mk[:, qoff:qoff + N], True, False, [self.cb, mkb], [sbk])
                self.mm(sbk[:, 0:N], ksrc[:, kbase:kbase + 128], qsrc[:, rho * n + st["qs"]:rho * n + st["qe"]],
                        False, True, [ksrc, qsrc], [sbk])
                return sbk

            def B(i, sbk):
                st = steps[i]
                N = st["qe"] - st["qs"]
                qoff = st["qs"] - (128 * st["b"] - 64)
                p_, _ = self.nxt("mpt")
                self.act(p_[:, 0:N], sbk[:, 0:N], AF.Exp, [sbk], [p_])
                return p_

            def C(i, p_):
                st = steps[i]
                h, g, d, n, rho, b, Q0, Q1 = st["h"], st["g"], st["d"], st["n"], st["rho"], st["b"], st["Q0"], st["Q1"]
                N = st["qe"] - st["qs"]
                ob = self.pb[st["blk"] % 2]
                zb = self.pb[2 + st["blk"] % 2]
                tile_i = b if g == 0 else (rho * 4 + b if g == 1 else rho)
                self.mm(ob[:, st["qs"] - Q0:st["qe"] - Q0], vts[g][:, tile_i, h * 128:(h + 1) * 128], p_[:, 0:N],
                        st["first"], False, [vbs[g], p_], [ob], skip_group_check=True)
                self.mm(zb[:, st["qs"] - Q0:st["qe"] - Q0], self.ones_bf[:], p_[:, 0:N],
                        st["first"], False, [self.cb, p_], [zb], skip_group_check=True)
                if st["last"]:
                    L_ = Q1 - Q0
                    ov = acco[:, :].rearrange("p (i r) -> p r i", r=d)[:, rho, Q0:Q1]
                    zv = accz[:, :].rearrange("p (i r) -> p r i", r=d)[:, rho, Q0:Q1]
                    if g == 0:
                        self.cp("act", ov, ob[:, 0:L_], [ob], [acco])
                        self.cp("dve", zv, zb[:, 0:L_], [zb], [accz])
                    else:
                        self.tt("dve", ov, ov, ob[:, 0:L_], ALU.add, [ob, acco], [acco])
                        self.tt("dve", zv, zv, zb[:, 0:L_], ALU.add, [zb, accz], [accz])
                if st["hlast"]:
                    self.recip(accz[:, :], accz[:, :], [accz], [accz])
                    self.tt("dve", yst[:, :], acco[:, :], accz[:, :], ALU.mult, [acco, accz], [ystb])
                    S.dma("sp", YT[h * 128:(h + 1) * 128, s * T:(s + 1) * T], yst[:, :], "dly", reads=[ystb])

            self.pipe(len(steps), A, B, C, L=3)

    def phase_B0(self):
        with contextlib.ExitStack() as es:
            self.mix_common(es)
            self.na_tables(es)
        self.S.barrier()
        with contextlib.ExitStack() as es:
            self.mix_common(es)
            self.na_all(es)
        self.S.barrier()
        with contextlib.ExitStack() as es:
            self.gla_all(es)
        self.S.barrier()

    @staticmethod
    def na_row(r):
        r0 = min(max(r - 4, 0), 24)
        return r0, r0 - r + 7

    def na_tables(self, es):
        S, nc = self.S, self.nc
        sbt = lambda n, s, d: self.sb(es, n, s, d)
        self.MB = self.dscr("MB", [8, 5, 128, 512], BF16)
        R1D = self.dscr("R1D", [120, 64, 64], F32)
        rp = sbt("na_rp", [128, 128], F32)
        rpb_ = Buf(rp, "rp")
        self.memset("dve", rp[:, :], 0.0, [rpb_])
        S.dma("sp", rp[0:120, 0:31], self.W["na_rpb"][:, :], "c0", writes=[rpb_])
        trb = self.next_tr()
        self.tr(trb[:, 0:128], rp[:, :], self.ident[:], [rpb_, self.cb], [trb])
        rpT = sbt("na_rpT", [32, 128], F32)
        rpTb = Buf(rpT, "rpT")
        self.cp("dve", rpT[0:32, 0:128], trb[0:32, 0:128], [trb], [rpTb])
        toe = sbt("na_toe", [32, 4096], F32)
        toeb = Buf(toe, "toe")
        self.memset("dve", toe[:, :], 0.0, [toeb])
        S.dma("sp", toe[0:31, :], self.C["na_toe"][:, :], "c0", writes=[toeb])
        r1 = sbt("na_r1", [128, 4096], F32)
        r1b = Buf(r1, "r1")
        for c in range(8):
            pbk = self.next_mm()
            self.mm(pbk[:, :], rpT[0:32, 0:128], toe[0:32, c * 512:(c + 1) * 512], True, True, [rpTb, toeb], [pbk])
            self.cp("act" if c % 2 else "dve", r1[0:120, c * 512:(c + 1) * 512], pbk[0:120, :], [pbk], [r1b])
        S.dma("sp", R1D.rearrange("a b c -> a (b c)"), r1[0:120, :], "na_r1", reads=[r1b])
        S.barrier()
        cm = sbt("na_cm", [128, 64], F32)
        cmb = Buf(cm, "cm")
        S.dma("sp", cm[:, :], self.C["na_colmask"][:, :], "c0", writes=[cmb])
        NTB = 8
        tb = sbt("na_tb", [128, NTB, 512], F32)
        tbb = [[Buf(tb[64 * (q % 2):64 * (q % 2) + 64, i, (q // 2) * 256:(q // 2 + 1) * 256], "natb%d_%d" % (i, q))
                for q in range(4)] for i in range(NTB)]
        tbo_ = sbt("na_tbo", [128, NTB, 512], BF16)
        tbo = [Buf(tbo_[:, i, :], "natbo%d" % i) for i in range(NTB)]
        PAIRS = ((0, 1), (2, 3), (4, 5), (28, 29), (30, 31))
        cnt = 0
        for h in range(8):
            for pc, (ra, rb) in enumerate(PAIRS):
                tq = tbb[cnt % NTB]
                t = tb[:, cnt % NTB, :]
                cnt += 1
                for slot, r in enumerate((ra, rb)):
                    r0, dr0 = self.na_row(r)
                    for par in range(2):
                        row0 = h * 15 + dr0 + par
                        src = R1D[row0:row0 + 7:2, :, :].rearrange("t k c -> k t c")
                        qb_ = tq[2 * slot + par]
                        dst = qb_[:, :].rearrange("k (t c) -> k t c", c=64)
                        S.dma(("sp", "pool")[(slot + par) % 2], dst, src, "natbq%d" % (2 * slot + par), writes=[qb_])
                to = tbo[cnt % NTB]
                self.tt("dve", to[:, :].rearrange("p (a c) -> p a c", c=64), t.rearrange("p (a c) -> p a c", c=64),
                        cm[:, :].unsqueeze(1).broadcast_to([128, 8, 64]), ALU.add, list(tq) + [cmb], [to])
                S.dma("sp", self.MB[h, pc, :, :], to[:, :], "natbo", reads=[to])

    def na_all(self, es):
        S, nc = self.S, self.nc
        sbt = lambda n, s, d: self.sb(es, n, s, d)
        v0 = sbt("na_v0", [128, 16, 1024], BF16)
        v1 = sbt("na_v1", [128, 15, 1024], BF16)
        v0b, v1b = Buf(v0, "nav0"), Buf(v1, "nav1")
        qk = sbt("na_qk", [128, 2, 2, T], BF16)
        qkb = [(Buf(qk[:, i, 0, :], "naq%d" % i), Buf(qk[:, i, 1, :], "nak%d" % i)) for i in range(2)]
        mb = sbt("na_mb", [128, 2, 5, 512], BF16)
        mbb = [Buf(mb[:, i], "namb%d" % i) for i in range(2)]
        YT = self.sc["YT0"]
        PCLS = {0: 0, 2: 1, 28: 3, 30: 4}
        self.later = []
        for s in range(self.nseq):
            vsrc = self.sc["VB"][s * T:(s + 1) * T, :]
            S.dma("sp", v0[:, :, :], vsrc.rearrange("(k p) c -> p k c", p=128), "nav0", writes=[v0b])
            S.dma("sp", v1[:, :, :], vsrc[64:64 + 15 * 128, :].rearrange("(k p) c -> p k c", p=128), "nav1", writes=[v1b])

            def load_head(h):
                qb, kb = qkb[h % 2]
                mt = mbb[h % 2]
                S.dma("sp", qb[:, :], self.sc["QBT"][h * 128:(h + 1) * 128, s * T:(s + 1) * T], "naq%d" % (h % 2), writes=[qb])
                S.dma("sp", kb[:, :], self.sc["KBT"][h * 128:(h + 1) * 128, s * T:(s + 1) * T], "nak%d" % (h % 2), writes=[kb])
                S.dma("sp", mt[:, :, :], self.MB[h].rearrange("a p c -> p a c"), "namb%d" % (h % 2), writes=[mt])

            steps = [(h, blk, pr) for h in range(8) for blk in range(4) for pr in range(4)]
            load_head(0)
            sbanks = self.pb[4:8]

            def A(i):
                h, blk, pr = steps[i]
                if blk == 0 and pr == 0 and h + 1 < 8:
                    load_head(h + 1)
                qb, kb = qkb[h % 2]
                ra = blk * 8 + pr * 2
                sbk = sbanks[i % 4]
                mt = mbb[h % 2]
                pc = PCLS.get(ra, 2)
                self.mm(sbk[:, :], self.ident_bf[:], mt[:, pc, :], True, False, [self.cb, mt], [sbk])
                for slot in range(2):
                    r = ra + slot
                    r0, _ = self.na_row(r)
                    for tp in range(4):
                        k0 = 64 * r0 + 128 * tp
                        self.mm(sbk[:, slot * 256 + tp * 64:slot * 256 + (tp + 1) * 64], kb[:, k0:k0 + 128],
                                qb[:, r * 64:(r + 1) * 64], False, slot == 1 and tp == 3, [kb, qb], [sbk],
                                skip_group_check=True)
                return sbk

            def B(i, sbk):
                h, blk, pr = steps[i]
                mt = mbb[h % 2]
                pc = PCLS.get(blk * 8 + pr * 2, 2)
                p_, _ = self.nxt("mpt")
                self.act(p_[:, :], sbk[:, :], AF.Exp, [sbk], [p_])
                return p_

            def C(i, p_):
                h, blk, pr = steps[i]
                ra = blk * 8 + pr * 2
                ob = self.pb[(h * 4 + blk) % 2]
                zb = self.pb[2 + (h * 4 + blk) % 2]
                for slot in range(2):
                    r = ra + slot
                    r0, _ = self.na_row(r)
                    c0 = (r - blk * 8) * 64
                    for tp in range(4):
                        k0 = 64 * r0 + 128 * tp
                        if k0 % 128 == 0:
                            vl, vbuf = v0[:, k0 // 128, h * 128:(h + 1) * 128], v0b
                        else:
                            vl, vbuf = v1[:, (k0 - 64) // 128, h * 128:(h + 1) * 128], v1b
                        pr_ = p_[:, slot * 256 + tp * 64:slot * 256 + (tp + 1) * 64]
                        self.mm(ob[:, c0:c0 + 64], vl, pr_, tp == 0, tp == 3, [vbuf, p_], [ob], skip_group_check=True)
                        self.mm(zb[:, c0:c0 + 64], self.ones_bf[:], pr_, tp == 0, tp == 3, [self.cb, p_], [zb],
                                skip_group_check=True)
                if pr == 3:
                    rz, _ = self.nxt("m32")
                    self.recip(rz[:, :], zb[:, :], [zb], [rz])
                    st, sti = self.nxt("mstg")
                    self.tt("dve", st[:, :], ob[:, :], rz[:, :], ALU.mult, [ob, rz], [st])
                    S.dma("sp", YT[1024 + h * 128:1024 + (h + 1) * 128, s * T + blk * 512:s * T + (blk + 1) * 512], st[:, :],
                          "mstg%d" % sti, reads=[st])

            self.pipe(len(steps), A, B, C, L=3)

    def nb(self):
        b = self.pb[self.nb_i % 8]
        self.nb_i += 1
        return b

    def gla_all(self, es):
        S, nc = self.S, self.nc
        sbt = lambda n, s, d: self.sb(es, n, s, d)
        self.nb_i = 0
        NCH = T // 128
        gm = sbt("gl_m", [128, 6, 128], F32)
        gmb = Buf(gm, "glm")
        S.dma("sp", gm[:, :, :], self.C["gla_m"].rearrange("a p c -> p a c"), "c0", writes=[gmb])
        M1, M2, M3, M4, MF, MB_ = [gm[:, i, :] for i in range(6)]
        wg = sbt("gl_wg", [32, 2, 512], F32)
        wgb = Buf(wg, "glwg")
        self.memset("dve", wg[:, :, :], 0.0, [wgb])
        for i, (wn, bn) in enumerate((("gla_wg_fwd", "gla_bg_fwd"), ("gla_wg_bwd", "gla_bg_bwd"))):
            S.dma("sp", wg[0:16, i, :], self.W[wn][:, :], "c0", writes=[wgb])
            S.dma("sp", wg[16:17, i, :], self.W[bn][:, :], "c0", writes=[wgb])
        lr = sbt("gl_lr", [32, 2, T], F32)
        lrb = Buf(lr, "gllr")
        self.memset("dve", lr[:, :, :], 0.0, [lrb])
        for i in range(2):
            S.dma("sp", lr[16:17, i, :], self.C["ones_row"][:, :], "c0", writes=[lrb])
        gn = sbt("gl_gn", [128, 256], F32)
        gnb = Buf(gn, "glgn")
        S.dma("sp", gn[:, :], self.W["gla_norm_g"].rearrange("a b -> (a b)").partition_broadcast(128), "c0", writes=[gnb])
        one_t = sbt("gl_one", [128, 1], F32)
        oneb = Buf(one_t, "glone")
        self.memset("dve", one_t[:, :], 1.0, [oneb])
        qa = sbt("gl_q", [128, 4, T], BF16)
        ka = sbt("gl_k", [128, 4, T], BF16)
        kt = sbt("gl_kt", [128, NCH, 512], BF16)
        va = sbt("gl_v", [128, NCH, 1024], BF16)
        qab, kab, ktb, vab = Buf(qa, "glq"), Buf(ka, "glk"), Buf(kt, "glkt"), Buf(va, "glv")
        ra = sbt("gl_ra", [128, 2, 1024], BF16)
        rab = [Buf(ra[:, i, :], "glra%d" % i) for i in range(2)]
        sbst = sbt("gl_sb", [128, NCH, 4, 256], BF16)
        sbb = [Buf(sbst[:, n], "glsb%d" % n) for n in range(NCH)]
        s32 = sbt("gl_s32", [128, 4, 256], F32)
        s32b = [Buf(s32[:, h, :], "gls32_%d" % h) for h in range(4)]
        sf = sbt("gl_sf", [128, 2, 4, 256], BF16)
        sfb = [[Buf(sf[:, i, h, :], "glsf%d_%d" % (i, h)) for h in range(4)] for i in range(2)]
        f32t = sbt("gl_f32", [128, 6, 512], F32)
        f32b = [Buf(f32t[:, i, :], "glf%d" % i) for i in range(6)]
        self.glf = f32b
        self.glf_i = 0
        g32t = sbt("gl_g32", [128, 6, 512], F32)
        self.glg = [Buf(g32t[:, i, :], "glg%d" % i) for i in range(6)]
        self.glg_i = 0
        b16t = sbt("gl_b16", [128, 16, 512], BF16)
        b16b = [Buf(b16t[:, i, :], "glh%d" % i) for i in range(16)]
        self.glh = b16b
        self.glh_i = 0
        sm = sbt("gl_sm", [128, 4, 8], F32)
        smb = [Buf(sm[:, i, :], "glsm%d" % i) for i in range(4)]
        self.glsm = smb
        self.glsm_i = 0
        ya2 = sbt("gl_ya", [128, 2, 1024], F32)
        yab2 = [Buf(ya2[:, i, :], "glya%d" % i) for i in range(2)]
        yst = sbt("gl_yst", [128, 1, 8, 512], BF16)
        ystb = [Buf(yst[:, 0], "glyst0")] * 2
        YT = self.sc["YT0"]

        def gate(n, di):
            pb = self.nb()
            self.mm(pb[:, :], lr[0:32, di, n * 128:(n + 1) * 128], wg[0:32, di, :], True, True, [lrb, wgb], [pb])
            e, _ = self.nxt("glg")
            self.act(e[:, :], pb[:, :], AF.Exp, [pb], [e], scale=-1.0)
            self.act(e[:, :], e[:, :], AF.Ln, [e, oneb], [e], bias=one_t[:, 0:1])
            return e

        for s in range(self.nseq):
            c0, c1 = s * T, (s + 1) * T
            S.dma("sp", qa[:, :, :], self.sc["QAT"][:, c0:c1].rearrange("(h p) t -> p h t", p=128), "glq", writes=[qab])
            S.dma("sp", ka[:, :, :], self.sc["KAT"][:, c0:c1].rearrange("(h p) t -> p h t", p=128), "glk", writes=[kab])
            S.dma("sp", kt[:, :, :], self.sc["KA"][c0:c1, :].rearrange("(c p) n -> p c n", p=128), "glkt", writes=[ktb])
            S.dma("sp", va[:, :, :], self.sc["VA"][c0:c1, :].rearrange("(c p) n -> p c n", p=128), "glv", writes=[vab])
            for i in range(2):
                S.dma("sp", lr[0:16, i, :], self.sc["LRT"][16 * i:16 * i + 16, c0:c1], "gllr", writes=[lrb])
            for h in range(4):
                self.memset("pool", s32[:, h, :], 0.0, [s32b[h]])
            self.memset("pool", sbst[:, NCH - 1], 0.0, [sbb[NCH - 1]])
            gnext = gate(NCH - 1, 1)
            for n in range(NCH - 1, 0, -1):
                Gb = gnext
                if n - 1 > 0:
                    gnext = gate(n - 1, 1)
                pe_ = self.nb()
                self.mm(pe_[:, :], M4, Gb[:, :], True, True, [gmb, Gb], [pe_])
                pd = self.nb()
                for h in range(4):
                    self.mm(pd[:, 2 * h:2 * h + 2], Gb[:, h * 128:(h + 1) * 128], M2[:, 0:2], True, True, [Gb, gmb], [pd])
                Ee, _ = self.nxt("glf")
                self.act(Ee[:, :], pe_[:, :], AF.Exp, [pe_], [Ee])
                dec, _ = self.nxt("glsm")
                self.act(dec[:, 0:8], pd[:, 0:8], AF.Exp, [pd], [dec])
                ku, _ = self.nxt("glh")
                self.tt("dve", ku[:, :], kt[:, n, :], Ee[:, :], ALU.mult, [ktb, Ee], [ku])
                for hp in range(2):
                    pu = self.nb()
                    for hh in range(2):
                        h = 2 * hp + hh
                        self.mm(pu[:, hh * 256:(hh + 1) * 256], ku[:, h * 128:(h + 1) * 128],
                                va[:, n, h * 256:(h + 1) * 256], True, True, [ku, vab], [pu])
                    for hh in range(2):
                        h = 2 * hp + hh
                        self.stt(s32[:, h, :], s32[:, h, :], dec[:, 2 * h:2 * h + 1], pu[:, hh * 256:(hh + 1) * 256],
                                 ALU.mult, ALU.add, [s32b[h], dec, pu], [s32b[h]])
                        self.cp("pool", sbst[:, n - 1, h, :], s32[:, h, :], [s32b[h]], [sbb[n - 1]])
            for h in range(4):
                self.memset("pool", s32[:, h, :], 0.0, [s32b[h]])
                self.memset("pool", sf[:, 0, h, :], 0.0, [sfb[0][h]])
            v3 = lambda ap: ap.rearrange("p (h t) -> p h t", h=4)
            v2 = lambda ap: ap.rearrange("p (h t) -> p h t", h=2)

            def stage1a(n):
                return gate(n, 0), gate(n, 1)

            def stage1(n, gg):
                rb = rab[n % 2]
                S.dma("sp", rb[:, :], self.sc["RA"][c0 + n * 128:c0 + (n + 1) * 128, :], "glra%d" % (n % 2), writes=[rb])
                Gf, Gb = gg
                pcf, pcb, pef = self.nb(), self.nb(), self.nb()
                for h in range(4):
                    self.mm(pcf[:, h * 128:(h + 1) * 128], Gf[:, h * 128:(h + 1) * 128], M1, True, True, [Gf, gmb], [pcf])
                for h in range(4):
                    self.mm(pcb[:, h * 128:(h + 1) * 128], Gb[:, h * 128:(h + 1) * 128], M2, True, True, [Gb, gmb], [pcb])
                self.mm(pef[:, :], M3, Gf[:, :], True, True, [gmb, Gf], [pef])
                Ecf, _ = self.nxt("glf")
                Encf, _ = self.nxt("glf")
                Ecb, _ = self.nxt("glf")
                Encb, _ = self.nxt("glf")
                Eef, _ = self.nxt("glf")
                self.act(Ecf[:, :], pcf[:, :], AF.Exp, [pcf], [Ecf])
                self.act(Encf[:, :], pcf[:, :], AF.Exp, [pcf], [Encf], scale=-1.0)
                self.act(Ecb[:, :], pcb[:, :], AF.Exp, [pcb], [Ecb])
                self.act(Encb[:, :], pcb[:, :], AF.Exp, [pcb], [Encb], scale=-1.0)
                self.act(Eef[:, :], pef[:, :], AF.Exp, [pef], [Eef])
                dec, _ = self.nxt("glsm")
                self.cp("pool", dec[:, 0:4], Ecf[:, :].rearrange("p (h t) -> p h t", h=4)[:, :, 127], [Ecf], [dec])
                qn = qa[:, :, n * 128:(n + 1) * 128]
                kn = ka[:, :, n * 128:(n + 1) * 128]
                qdf, _ = self.nxt("glh")
                kdf, _ = self.nxt("glh")
                qdb, _ = self.nxt("glh")
                kdb, _ = self.nxt("glh")
                kuf, _ = self.nxt("glh")
                self.tt("dve", v3(qdf[:, :]), qn, v3(Ecf[:, :]), ALU.mult, [qab, Ecf], [qdf])
                self.tt("pool", v3(kdf[:, :]), kn, v3(Encf[:, :]), ALU.mult, [kab, Encf], [kdf])
                self.tt("dve", v3(qdb[:, :]), qn, v3(Ecb[:, :]), ALU.mult, [qab, Ecb], [qdb])
                self.tt("pool", v3(kdb[:, :]), kn, v3(Encb[:, :]), ALU.mult, [kab, Encb], [kdb])
                self.tt("dve", kuf[:, :], kt[:, n, :], Eef[:, :], ALU.mult, [ktb, Eef], [kuf])
                return dict(rb=rb, dec=dec, qdf=qdf, kdf=kdf, qdb=qdb, kdb=kdb, kuf=kuf)

            def stage2(n, c):
                rb, dec, qdf, kdf, qdb, kdb, kuf = c["rb"], c["dec"], c["qdf"], c["kdf"], c["qdb"], c["kdb"], c["kuf"]
                pA, pB = self.nb(), self.nb()
                for h in range(4):
                    hs = slice(h * 128, (h + 1) * 128)
                    self.mm(pA[:, hs], kdf[:, hs], qdf[:, hs], True, True, [kdf, qdf], [pA])
                for h in range(4):
                    hs = slice(h * 128, (h + 1) * 128)
                    self.mm(pB[:, hs], kdb[:, hs], qdb[:, hs], True, True, [kdb, qdb], [pB])
                atf, _ = self.nxt("glh")
                atb, _ = self.nxt("glh")
                self.tt("dve", v3(atf[:, :]), v3(pA[:, :]), MF.unsqueeze(1).broadcast_to([128, 4, 128]), ALU.mult,
                        [pA, gmb], [atf])
                self.tt("dve", v3(atb[:, :]), v3(pB[:, :]), MB_.unsqueeze(1).broadcast_to([128, 4, 128]), ALU.mult,
                        [pB, gmb], [atb])
                sfc = sfb[n % 2]
                sfn = sfb[(n + 1) % 2]
                po = [self.nb(), self.nb()]
                for h in range(4):
                    hs = slice(h * 128, (h + 1) * 128)
                    o_ = po[h // 2][:, (h % 2) * 256:(h % 2 + 1) * 256]
                    vv = va[:, n, h * 256:(h + 1) * 256]
                    self.mm(o_, atf[:, hs], vv, True, False, [atf, vab], [po[h // 2]])
                    self.mm(o_, atb[:, hs], vv, False, False, [atb, vab], [po[h // 2]])
                    self.mm(o_, qdf[:, hs], sf[:, n % 2, h, :], False, False, [qdf, sfc[h]], [po[h // 2]])
                    self.mm(o_, qdb[:, hs], sbst[:, n, h, :], False, True, [qdb, sbb[n]], [po[h // 2]])
                if n < NCH - 1:
                    for hp in range(2):
                        pu = self.nb()
                        for hh in range(2):
                            h = 2 * hp + hh
                            self.mm(pu[:, hh * 256:(hh + 1) * 256], kuf[:, h * 128:(h + 1) * 128],
                                    va[:, n, h * 256:(h + 1) * 256], True, True, [kuf, vab], [pu])
                        for hh in range(2):
                            h = 2 * hp + hh
                            self.stt(s32[:, h, :], s32[:, h, :], dec[:, h:h + 1],
                                     pu[:, hh * 256:(hh + 1) * 256], ALU.mult, ALU.add, [s32b[h], dec, pu], [s32b[h]])
                            self.cp("pool", sf[:, (n + 1) % 2, h, :], s32[:, h, :], [s32b[h]], [sfn[h]])
                ss, _ = self.nxt("glsm")
                junk, _ = self.nxt("glf")
                for h in range(4):
                    o_ = po[h // 2][:, (h % 2) * 256:(h % 2 + 1) * 256]
                    self.S.op("act", (lambda o_=o_, h=h, junk=junk, ss=ss: nc.scalar.activation(
                        junk[:, 0:256], o_, AF.Square, accum_out=ss[:, h:h + 1])),
                        reads=[po[h // 2]], writes=[junk, ss])
                self.act(ss[:, 4:8], ss[:, 0:4], AF.Sqrt, [ss, self.cb], [ss], scale=1.0 / 256, bias=self.eps_t[:, 0:1])
                self.recip(ss[:, 4:8], ss[:, 4:8], [ss], [ss])
                sr, _ = self.nxt("glf")
                sr2, _ = self.nxt("glf")
                self.act(sr[:, :], rb[:, 0:512], AF.Silu, [rb], [sr])
                self.act(sr2[:, :], rb[:, 512:1024], AF.Silu, [rb], [sr2])
                g2 = gn[:, :].unsqueeze(1).broadcast_to([128, 2, 256])
                self.tt("pool", v2(sr[:, :]), v2(sr[:, :]), g2, ALU.mult, [sr, gnb], [sr])
                self.tt("pool", v2(sr2[:, :]), v2(sr2[:, :]), g2, ALU.mult, [sr2, gnb], [sr2])
                ya = ya2[:, n % 2, :]
                yab = yab2[n % 2]
                for h in range(4):
                    o_ = po[h // 2][:, (h % 2) * 256:(h % 2 + 1) * 256]
                    gsrc = sr if h < 2 else sr2
                    self.stt(ya[:, h * 256:(h + 1) * 256], o_, ss[:, 4 + h:5 + h], gsrc[:, (h % 2) * 256:(h % 2 + 1) * 256],
                             ALU.mult, ALU.mult, [po[h // 2], ss, gsrc], [yab])

                def tail(n=n, ya=ya, yab=yab):
                    ysb = ystb[(n // 4) % 2]
                    ysv = yst[:, 0]
                    for half in range(2):
                        ptb = self.nb()
                        for f in range(4):
                            ft = half * 4 + f
                            self.tr(ptb[:, f * 128:(f + 1) * 128], ya[:, ft * 128:(ft + 1) * 128], self.ident[:],
                                    [yab, self.cb], [ptb])
                        self.cp("act" if half == 0 else "dve",
                                ysv[:, half * 4:half * 4 + 4, (n % 4) * 128:(n % 4 + 1) * 128],
                                ptb[:, :].rearrange("p (f t) -> p f t", f=4), [ptb], [ysb])
                    if n % 4 == 3:
                        t0 = c0 + (n // 4) * 512
                        S.dma("sp", YT[0:1024, t0:t0 + 512].rearrange("(f p) t -> p f t", p=128), ysv[:, :, :],
                              "glyst0", reads=[ysb])
                return tail

            gates = {0: stage1a(0), 1: stage1a(1)}
            ctxs = {0: stage1(0, gates.pop(0))}
            tails = {}
            for n in range(NCH):
                if n + 2 < NCH:
                    gates[n + 2] = stage1a(n + 2)
                if n + 1 < NCH:
                    ctxs[n + 1] = stage1(n + 1, gates.pop(n + 1))
                tails[n] = stage2(n, ctxs.pop(n))
                if n - 1 in tails:
                    tails.pop(n - 1)()
            tails.pop(NCH - 1)()

    def build(self, phases=("A0", "B0", "C0", "B1", "C1")):
        self.setup()
        with contextlib.ExitStack() as tes:
            if "A0" in phases:
                self.alloc_tok(tes)
                self.phase_A0()
        if "B0" in phases:
            self.phase_B0()
        if "C0" in phases:
            with contextlib.ExitStack() as tes:
                self.alloc_tok(tes)
                self.phase_C(0)
        if "B1" in phases:
            self.phase_B1()
        if "C1" in phases:
            with contextlib.ExitStack() as tes:
                self.alloc_tok(tes)
                self.phase_C(1)
        self.S.barrier()
        self.S.final_wait()
        return self.nc


MIX_CONST_SHAPES = {"dil_mask": [128, 256], "na_toe": [31, 4096], "na_colmask": [128, 64],
                    "gla_m": [6, 128, 128], "ones_row": [1, T]}


def _mix_consts():
    cs = {}
    k = np.arange(128)[:, None]
    q = np.arange(256)[None, :]
    cs["dil_mask"] = ((q >= k) & (q <= k + 128)).astype(np.float32)
    kc = np.arange(64)[:, None]
    c = np.arange(64)[None, :]
    dc = kc - c + 15
    toe = np.zeros((31, 64, 64), np.float32)
    for i in range(31):
        toe[i] = (dc == i)
    cs["na_toe"] = toe.reshape(31, 4096)
    c0 = np.clip(c - 8, 0, 48)
    ok = (kc >= c0) & (kc < c0 + 16)
    cm = np.where(ok, 0.0, NEG).astype(np.float32)
    cs["na_colmask"] = np.concatenate([cm, cm], 0)
    r = np.arange(128)[:, None]
    i = np.arange(128)[None, :]
    sc = -1.0 / 16.0
    gm = np.stack([(r <= i) * sc, (r >= i) * sc, (r > i) * sc, (r < i) * sc, (r <= i) * 1.0, (r > i) * 1.0]).astype(np.float32)
    cs["gla_m"] = gm
    cs["ones_row"] = np.ones((1, T), np.float32)
    return cs


def build_program(nseq, dbg=(), phases=("A0", "B0", "C0", "B1", "C1")):
    b1 = Builder(nseq, None, dbg)
    b1.build(phases)
    needed = b1.S.needed
    b1.es.close()
    b2 = Builder(nseq, needed, dbg)
    nc = b2.build(phases)
    return nc, b2


_PROG = {}


def kernel(x_prompt, x_sample, p_prompt, p_sample, norm_g, w_in_even, w_out_even, gla_wg_fwd, gla_bg_fwd,
           gla_wg_bwd, gla_bg_bwd, gla_norm_g, na_rpb, w_in_odd, w_out_odd, diff_lambda, diff_subln_g,
           w_ffn_gate, w_ffn_up, w_ffn_down, w_ple_proj, w_ple_gate):
    ncores, nseq = 8, 3
    f32 = lambda a: np.ascontiguousarray(np.asarray(a), dtype=np.float32)
    x_prompt, x_sample, p_prompt, p_sample = f32(x_prompt), f32(x_sample), f32(p_prompt), f32(p_sample)
    wts = {
        "norm_g": f32(norm_g), "w_in_even": f32(w_in_even)[0], "w_out_even": f32(w_out_even)[0],
        "gla_wg_fwd": f32(gla_wg_fwd)[0], "gla_bg_fwd": f32(gla_bg_fwd), "gla_wg_bwd": f32(gla_wg_bwd)[0],
        "gla_bg_bwd": f32(gla_bg_bwd), "gla_norm_g": f32(gla_norm_g), "na_rpb": f32(na_rpb).reshape(120, 31),
        "w_in_odd": f32(w_in_odd)[0], "w_out_odd": f32(w_out_odd)[0], "diff_lambda": f32(diff_lambda)[0],
        "diff_subln_g": f32(diff_subln_g), "w_ffn_gate": f32(w_ffn_gate), "w_ffn_up": f32(w_ffn_up),
        "w_ffn_down": f32(w_ffn_down), "w_ple_proj": f32(w_ple_proj), "w_ple_gate": f32(w_ple_gate),
    }
    for k, shp in WEIGHT_SHAPES.items():
        wts[k] = np.ascontiguousarray(wts[k].reshape(shp))
    consts = _consts()
    consts.update(_mix_consts())
    if "nc" not in _PROG:
        _PROG["nc"] = build_program(nseq)[0]
    nc = _PROG["nc"]
    in_maps = []
    for c in range(ncores):
        xs = np.concatenate([x_prompt[c], x_sample[2 * c], x_sample[2 * c + 1]], axis=0)
        ps = np.concatenate([p_prompt[:, c], p_sample[:, 2 * c], p_sample[:, 2 * c + 1]], axis=1)
        m = {"x": np.ascontiguousarray(xs), "p": np.ascontiguousarray(ps)}
        m.update(wts)
        m.update(consts)
        in_maps.append(m)
    res = run_bass_kernel_spmd(nc, in_maps, core_ids=list(range(ncores)))
    y_prompt = np.empty((8, T, D), np.float32)
    y_sample = np.empty((16, T, D), np.float32)
    for c in range(ncores):
        y = np.asarray(res.results[c]["y"], dtype=np.float32)
        y_prompt[c] = y[0:T]
        y_sample[2 * c] = y[T:2 * T]
        y_sample[2 * c + 1] = y[2 * T:3 * T]
    return (y_prompt, y_sample)
```

```python
import contextlib
import math
import numpy as np
import concourse.bass as bass
import concourse.mybir as mybir
from concourse.bass_utils import run_bass_kernel_spmd

F32 = mybir.dt.float32
BF16 = mybir.dt.bfloat16
AF = mybir.ActivationFunctionType
ALU = mybir.AluOpType

T = 2048
D = 2048
TT = 512
NTT = T // TT
KC = D // 128
DFF = 5632
FC = DFF // 128
PLE = 256
EPS = 1e-6
NEG = -30000.0
EVEN_W = 6176
ODD_W = 8192
ROPE_THETA = 500000.0


class Buf:
    __slots__ = ("ap", "w", "r", "name")

    def __init__(self, ap, name=""):
        self.ap = ap
        self.w = None
        self.r = {}
        self.name = name

    def __getitem__(self, idx):
        return self.ap[idx]


class Sched:
    ENGS = ("pe", "act", "dve", "pool", "sp")

    def __init__(self, nc, es, needed=None):
        self.nc = nc
        self.es = es
        self.dry = needed is None
        self.needed = needed if needed is not None else {n: set() for n in self.ENGS}
        self.eng = {"pe": nc.tensor, "act": nc.scalar, "dve": nc.vector, "pool": nc.gpsimd, "sp": nc.sync}
        self.sem = {n: es.enter_context(nc.semaphore("s_" + n)) for n in self.ENGS}
        self.cnt = {n: 0 for n in self.ENGS}
        self.inc = {n: 0 for n in self.ENGS}
        self.map = {n: {} for n in self.ENGS}
        self.seen = {n: {} for n in self.ENGS}
        self.dsem = {}
        self.dcnt = {}
        self.nwaits = 0

    def dkey(self, key):
        if key not in self.dsem:
            self.dsem[key] = self.es.enter_context(self.nc.semaphore("d_" + key))
            self.dcnt[key] = 0
        return key

    def _wait(self, en, ev):
        if ev[0] == "e":
            _, src, idx = ev
            if src == en and en in ("pe",):
                return
            key = ("e", src)
            if self.seen[en].get(key, 0) >= idx:
                return
            self.seen[en][key] = idx
            if self.dry:
                self.needed[src].add(idx)
            else:
                self.eng[en].wait_ge(self.sem[src], self.map[src][idx])
        else:
            _, k, val = ev
            val = self.dcnt[k]
            key = ("d", k)
            if self.seen[en].get(key, 0) >= val:
                return
            self.seen[en][key] = val
            if not self.dry:
                self.eng[en].wait_ge(self.dsem[k], val)
        self.nwaits += 1

    def _deps(self, en, reads, writes):
        for b in reads:
            if b.w is not None:
                self._wait(en, b.w)
        for b in writes:
            if b.w is not None:
                self._wait(en, b.w)
            for ev in b.r.values():
                self._wait(en, ev)

    def _mark(self, me, rk, reads, writes):
        for b in writes:
            b.w = me
            b.r = {}
        for b in reads:
            b.r[rk] = me

    def op(self, en, fn, reads=(), writes=()):
        self._deps(en, reads, writes)
        self.cnt[en] += 1
        idx = self.cnt[en]
        if not self.dry:
            ins = fn()
            if idx in self.needed[en]:
                self.inc[en] += 1
                self.map[en][idx] = self.inc[en]
                ins.then_inc(self.sem[en], 1)
        me = ("e", en, idx)
        self._mark(me, ("e", en), reads, writes)
        return me

    def dma(self, q, out, in_, key, reads=(), writes=(), **kw):
        self.dkey(key)
        self._deps(q, reads, writes)
        self.dcnt[key] += 16
        if not self.dry:
            self.eng[q].dma_start(out=out, in_=in_, **kw).then_inc(self.dsem[key], 16)
        me = ("d", key, self.dcnt[key])
        self._mark(me, ("d", key), reads, writes)
        return me

    def barrier(self):
        for en in self.ENGS:
            for src in self.ENGS:
                if src != en and self.cnt[src] > 0:
                    self._wait(en, ("e", src, self.cnt[src]))
            for k in self.dsem:
                if self.dcnt[k] > 0:
                    self._wait(en, ("d", k, self.dcnt[k]))

    def final_wait(self):
        for k in self.dsem:
            if self.dcnt[k] > 0:
                self._wait("sp", ("d", k, self.dcnt[k]))


def _rope_tables():
    def tab(rot, period):
        half = rot // 2
        inv = ROPE_THETA ** (-np.arange(half, dtype=np.float32) / half)
        ang = np.arange(T, dtype=np.float32)[None, :] * inv[:, None]
        c = np.ones((128, T), np.float32)
        s = np.zeros((128, T), np.float32)
        perm = np.zeros((128, 128), np.float32)
        for base in range(0, 128, period):
            c[base:base + half] = np.cos(ang)
            c[base + half:base + rot] = np.cos(ang)
            s[base:base + half] = -np.sin(ang)
            s[base + half:base + rot] = np.sin(ang)
            for i in range(half):
                perm[base + half + i, base + i] = 1.0
                perm[base + i, base + half + i] = 1.0
        return c, s, perm
    return tab(32, 128), tab(16, 64)


def _consts():
    cs = {}
    cs["ident"] = np.eye(128, dtype=np.float32)
    (c1, s1, p1), (c2, s2, p2) = _rope_tables()
    cs["rope_c1"], cs["rope_s1"], cs["rope_p1"] = c1, s1, p1
    cs["rope_c2"], cs["rope_s2"], cs["rope_p2"] = c2, s2, p2
    return cs


CONST_SHAPES = {
    "ident": [128, 128],
    "rope_c1": [128, T], "rope_s1": [128, T], "rope_p1": [128, 128],
    "rope_c2": [128, T], "rope_s2": [128, T], "rope_p2": [128, 128],
}

WEIGHT_SHAPES = {
    "norm_g": [2, 5, D], "w_in_even": [D, EVEN_W], "w_out_even": [D, D],
    "gla_wg_fwd": [16, 512], "gla_bg_fwd": [1, 512], "gla_wg_bwd": [16, 512], "gla_bg_bwd": [1, 512],
    "gla_norm_g": [1, 256], "na_rpb": [120, 31], "w_in_odd": [D, ODD_W], "w_out_odd": [D, D],
    "diff_lambda": [4, 64], "diff_subln_g": [1, 128],
    "w_ffn_gate": [2, D, DFF], "w_ffn_up": [2, D, DFF], "w_ffn_down": [2, DFF, D],
    "w_ple_proj": [2, PLE, D], "w_ple_gate": [2, D, D],
}


class Builder:
    def __init__(self, nseq, needed=None, dbg=(), stop_after=None):
        self.nseq = nseq
        self.ntok = nseq * T
        self.dbg = set(dbg)
        self.stop_after = stop_after
        self.nc = bass.Bass("TRN2", target_bir_lowering=False)
        self.es = contextlib.ExitStack()
        self.S = Sched(self.nc, self.es, needed)
        self.dram = {}
        self.uid = 0

    def din(self, name, shape, dt=F32):
        t = self.nc.dram_tensor(name, list(shape), dt, kind="ExternalInput").ap()
        self.dram[name] = t
        return t

    def dscr(self, name, shape, dt):
        kind = "ExternalOutput" if name in self.dbg else "Internal"
        t = self.nc.dram_tensor(name, list(shape), dt, kind=kind).ap()
        self.dram[name] = t
        return t

    def sb(self, es, name, shape, dt):
        self.uid += 1
        return es.enter_context(self.nc.sbuf_tensor("sb%d_%s" % (self.uid, name), list(shape), dt))

    def ps(self, es, name, shape, dt=F32):
        return es.enter_context(self.nc.psum_tensor("ps_" + name, list(shape), dt))

    def mm(self, out, lhsT, rhs, start, stop, reads, writes, **kw):
        nc = self.nc
        return self.S.op("pe", lambda: nc.tensor.matmul(out, lhsT, rhs, start=start, stop=stop, **kw),
                         reads=reads, writes=writes)

    def tr(self, out, in_, ident, reads, writes):
        nc = self.nc
        return self.S.op("pe", lambda: nc.tensor.transpose(out, in_, ident), reads=reads, writes=writes)

    def act(self, out, in_, func, reads, writes, scale=None, bias=None):
        nc = self.nc
        kw = {}
        if scale is not None:
            kw["scale"] = scale
        if bias is not None:
            kw["bias"] = bias
        return self.S.op("act", lambda: nc.scalar.activation(out, in_, func, **kw), reads=reads, writes=writes)

    def tt(self, en, out, in0, in1, op, reads, writes):
        e = self.nc.vector if en == "dve" else self.nc.gpsimd
        return self.S.op(en, lambda: e.tensor_tensor(out, in0, in1, op), reads=reads, writes=writes)

    def ts(self, en, out, in0, s1, op0, reads, writes, s2=None, op1=None):
        e = self.nc.vector if en == "dve" else self.nc.gpsimd
        if op1 is None:
            return self.S.op(en, lambda: e.tensor_scalar(out, in0, s1, None, op0), reads=reads, writes=writes)
        return self.S.op(en, lambda: e.tensor_scalar(out, in0, s1, s2, op0, op1), reads=reads, writes=writes)

    def stt(self, out, in0, scalar, in1, op0, op1, reads, writes):
        nc = self.nc
        return self.S.op("dve", lambda: nc.vector.scalar_tensor_tensor(out, in0, scalar, in1, op0, op1),
                         reads=reads, writes=writes)

    def cp(self, en, out, in_, reads, writes):
        nc = self.nc
        if en == "act":
            return self.S.op("act", lambda: nc.scalar.copy(out, in_), reads=reads, writes=writes)
        e = nc.vector if en == "dve" else nc.gpsimd
        return self.S.op(en, lambda: e.tensor_copy(out, in_), reads=reads, writes=writes)

    def memset(self, en, ap, val, writes):
        e = self.nc.vector if en == "dve" else self.nc.gpsimd
        return self.S.op(en, lambda: e.memset(ap, val), writes=writes)

    def recip(self, out, in_, reads, writes):
        nc = self.nc
        return self.S.op("dve", lambda: nc.vector.reciprocal(out, in_), reads=reads, writes=writes)

    def setup(self):
        nc, es, S = self.nc, self.es, self.S
        ntok = self.ntok
        self.x = self.din("x", [ntok, D])
        self.p = self.din("p", [2, ntok, PLE])
        self.W = {k: self.din(k, v) for k, v in WEIGHT_SHAPES.items()}
        self.C = {k: self.din(k, v) for k, v in CONST_SHAPES.items()}
        self.C.update({k: self.din(k, v) for k, v in MIX_CONST_SHAPES.items()})
        self.y = self.nc.dram_tensor("y", [ntok, D], F32, kind="ExternalOutput").ap()
        sc = {}
        sc["HS"] = self.dscr("HS", [D, ntok], F32)
        for l in range(2):
            sc["YT%d" % l] = self.dscr("YT%d" % l, [D, ntok], BF16)
        for nm, rows in (("QAT", 512), ("KAT", 512), ("QBT", 1024), ("KBT", 1024),
                         ("QCT", 3072), ("KCT", 1024), ("QDT", 1024), ("KDT", 1024)):
            sc[nm] = self.dscr(nm, [rows, ntok], BF16)
        for nm, cols in (("KA", 512), ("VA", 1024), ("RA", 1024), ("VB", 1024), ("VC", 1024), ("VD", 1024)):
            sc[nm] = self.dscr(nm, [ntok, cols], BF16)
        sc["LRT"] = self.dscr("LRT", [32, ntok], F32)
        self.sc = sc
        self.ident = self.sb(es, "ident", [128, 128], F32)
        self.ident_bf = self.sb(es, "ident_bf", [128, 128], BF16)
        self.ones_bf = self.sb(es, "ones_bf", [128, 128], BF16)
        self.eps_t = self.sb(es, "eps_t", [128, 1], F32)
        self.gT = self.sb(es, "gT", [128, 10, KC], F32)
        self.perm1 = self.sb(es, "perm1", [128, 128], BF16)
        self.perm2 = self.sb(es, "perm2", [128, 128], BF16)
        self.cb = Buf(None, "consts")
        ptmp = self.sb(es, "ptmp", [128, 256], F32)
        S.dma("sp", self.ident[:], self.C["ident"][:, :], "c0", writes=[self.cb])
        S.dma("sp", ptmp[:, 0:128], self.C["rope_p1"][:, :], "c0", writes=[self.cb])
        S.dma("sp", ptmp[:, 128:256], self.C["rope_p2"][:, :], "c0", writes=[self.cb])
        with nc.allow_non_contiguous_dma(reason="one-time gain vector gather"):
            for ln in range(10):
                S.dma("sp", self.gT[:, ln, :],
                      self.W["norm_g"][ln // 5, ln % 5, :].rearrange("(k p) -> p k", p=128), "c0",
                      writes=[self.cb])
        self.cp("dve", self.ident_bf[:], self.ident[:], [self.cb], [self.cb])
        self.cp("dve", self.perm1[:], ptmp[:, 0:128], [self.cb], [self.cb])
        self.cp("dve", self.perm2[:], ptmp[:, 128:256], [self.cb], [self.cb])
        self.memset("dve", self.ones_bf[:], 1.0, [self.cb])
        self.memset("dve", self.eps_t[:], EPS, [self.cb])
        self.pb = [Buf(self.ps(es, "pb%d" % i, [128, 512]), "pb%d" % i) for i in range(8)]
        self.mm_banks = self.pb[0:4]
        self.ss_bank = self.pb[4]
        self.tr_banks = self.pb[5:7]
        self.x_bank = self.pb[7]
        self.mm_i = 0
        self.tr_i = 0
        S.barrier()

    def next_mm(self):
        b = self.mm_banks[self.mm_i % len(self.mm_banks)]
        self.mm_i += 1
        return b

    def next_tr(self):
        b = self.tr_banks[self.tr_i % len(self.tr_banks)]
        self.tr_i += 1
        return b

    def alloc_tok(self, es):
        sbt = lambda n, s, d: self.sb(es, n, s, d)
        ht = sbt("hT", [128, KC, TT], F32)
        self.h = [Buf(ht[:, k, :], "h%d" % k) for k in range(KC)]
        xt = sbt("xT", [128, KC, TT], BF16)
        self.xt = [Buf(xt[:, k, :], "x%d" % k) for k in range(KC)]
        sct = sbt("scT", [128, KC, TT], F32)
        self.sct = [Buf(sct[:, k, :], "sc%d" % k) for k in range(KC)]
        self.xin = [Buf(sct[:, 4 * j:4 * j + 4, :].rearrange("p a b -> p (a b)"), "xin%d" % j) for j in range(4)]
        at = sbt("actT", [128, FC, TT], BF16)
        self.actt = [Buf(at[:, f, :], "a%d" % f) for f in range(FC)]
        self.ostg = []
        for j in range(2):
            v = at[:, 8 * j:8 * j + 8, :].rearrange("p a b -> p (a b)").bitcast(F32)
            self.ostg.append((Buf(v, "ostg%d" % j), self.actt[8 * j:8 * j + 8]))
        sq = sbt("sq", [128, 4, TT], BF16)
        self.sq = [Buf(sq[:, i, :], "sq%d" % i) for i in range(4)]
        self.sq_i = 0
        self.stats_pend = []
        tmp = sbt("tmp", [128, 6, TT], F32)
        self.tmp = [Buf(tmp[:, i, :], "tmp%d" % i) for i in range(6)]
        self.tmp_i = 0
        rs = sbt("rstd", [128, 3, TT], F32)
        self.rstd = [Buf(rs[:, i, :], "rstd%d" % i) for i in range(3)]
        rtm = sbt("rtm", [128, 2, 4], F32)
        self.rtm = [Buf(rtm[:, i, :], "rtm%d" % i) for i in range(2)]
        self.rtm_i = 0
        self.rstd_i = 0
        self.NW = 4
        wt = sbt("wslots", [128, self.NW, 4096], BF16)
        self.wslot = [Buf(wt[:, i, :], "w%d" % i) for i in range(self.NW)]
        stg = sbt("stg", [128, 6, TT], BF16)
        self.stg = [Buf(stg[:, i, :], "stg%d" % i) for i in range(6)]
        self.stg_i = 0
        qs_ = sbt("qs", [128, 4, TT], BF16)
        self.qs = [Buf(qs_[:, i, :], "qs%d" % i) for i in range(4)]
        self.qs_i = 0
        rp = sbt("rope", [128, 4, TT], F32)
        self.rope = [Buf(rp[:, i, :], "rope%d" % i) for i in range(4)]
        pt = sbt("pT", [128, 2, TT], BF16)
        self.pt = [Buf(pt[:, i, :], "pT%d" % i) for i in range(2)]
        pin = sbt("pin", [128, 4, PLE], F32)
        self.pin = Buf(pin, "pin")

    def nxt(self, name):
        lst = getattr(self, name)
        i = getattr(self, name + "_i")
        setattr(self, name + "_i", i + 1)
        return lst[i % len(lst)], i % len(lst)

    def wstream_begin(self, specs, ntiles, name):
        self.wspecs = specs
        self.w_n = len(specs)
        self.w_total = len(specs) * ntiles
        self.w_issued = 0
        self.w_used = 0
        self.wscr = self.dscr("WS_" + name, [len(specs), 128, 4096], BF16)
        self.wscr_b = [Buf(None, "ws%d" % i) for i in range(len(specs))]

    def wnext(self):
        S = self.S
        while self.w_issued < self.w_total and self.w_issued < self.w_used + self.NW - 1:
            bi = self.w_issued % self.w_n
            src, a, b = self.wspecs[bi]
            slot = self.wslot[self.w_issued % self.NW]
            key = "w%d" % (self.w_issued % self.NW)
            if self.w_issued < self.w_n:
                dst = slot[:, 0:a * b].rearrange("p (a b) -> p a b", b=b)
                S.dma("pool", dst, src, key, writes=[slot], max_dma_last_dim=4096)
                S.dma("sp", self.wscr[bi, :, 0:a * b], slot[:, 0:a * b], "wsb", reads=[slot], writes=[self.wscr_b[bi]])
            else:
                S.dma("pool", slot[:, 0:a * b], self.wscr[bi, :, 0:a * b], key, reads=[self.wscr_b[bi]], writes=[slot])
            self.w_issued += 1
        src, a, b = self.wspecs[self.w_used % self.w_n]
        slot = self.wslot[self.w_used % self.NW]
        self.w_used += 1
        return slot, slot[:, 0:a * b].rearrange("p (a b) -> p a b", b=b)

    def wspec_cols(self, w2d, c0, ncols):
        return (w2d[:, c0:c0 + ncols].rearrange("(k p) n -> p k n", p=128), w2d.shape[0] // 128, ncols)

    def stats_chunk(self, c, n, src_buf, src_ap, eng):
        sq, _ = self.nxt("sq")
        if eng == "act":
            self.act(sq[:, :], src_ap, AF.Square, [src_buf], [sq])
        else:
            self.tt(eng, sq[:, :], src_ap, src_ap, ALU.mult, [src_buf], [sq])
        self.stats_pend.append((sq, c == 0, c == n - 1))
        while len(self.stats_pend) > 2:
            self.stats_flush_one()

    def stats_flush_one(self):
        sq, first, last = self.stats_pend.pop(0)
        self.mm(self.ss_bank[:, :], self.ones_bf[:], sq[:, :], first, last, [sq, self.cb], [self.ss_bank])

    def stats_finish(self, nfeat):
        while self.stats_pend:
            self.stats_flush_one()
        r, _ = self.nxt("rstd")
        self.act(r[:, :], self.ss_bank[:, :], AF.Sqrt, [self.ss_bank, self.cb], [r],
                 scale=1.0 / nfeat, bias=self.eps_t[:, 0:1])
        self.recip(r[:, :], r[:, :], [r], [r])
        return r

    def g(self, l, n, k):
        return self.gT[:, l * 5 + n, k:k + 1]

    def win_plan(self, l):
        if l == 0:
            return [("QAT", 0, 512, "F", 128 ** -0.5, 0), ("KAT", 512, 512, "FT", 1.0, 0),
                    ("VA", 1024, 1024, "T", 1.0, 0), ("RA", 2048, 1024, "T", 1.0, 0),
                    ("LRT", 3072, 32, "L", 1.0, 0),
                    ("QBT", 3104, 1024, "F", 128 ** -0.5, 0), ("KBT", 4128, 1024, "F", 1.0, 0),
                    ("VB", 5152, 1024, "T", 1.0, 0)]
        return [("QCT", 0, 3072, "F", 128 ** -0.5, 1), ("KCT", 3072, 1024, "F", 1.0, 1),
                ("VC", 4096, 1024, "T", 1.0, 0),
                ("QDT", 5120, 1024, "F", 64 ** -0.5, 2), ("KDT", 6144, 1024, "F", 1.0, 2),
                ("VD", 7168, 1024, "T", 1.0, 0)]

    def win_wspecs(self, l):
        w = self.W["w_in_even" if l == 0 else "w_in_odd"]
        specs = []
        for (nm, c0, ncols, mode, scale, rope) in self.win_plan(l):
            if mode == "L":
                specs.append(self.wspec_cols(w, c0, 32))
            else:
                for b in range(ncols // 256):
                    specs.append(self.wspec_cols(w, c0 + 256 * b, 256))
        return specs

    def chain_wspecs(self, l):
        specs = []
        wo = self.W["w_out_even" if l == 0 else "w_out_odd"]
        for b in range(D // 256):
            specs.append(self.wspec_cols(wo, 256 * b, 256))
        wg, wu, wd = self.W["w_ffn_gate"][l], self.W["w_ffn_up"][l], self.W["w_ffn_down"][l]
        for b in range(DFF // 256):
            specs.append(self.wspec_cols(wg, 256 * b, 256))
            specs.append(self.wspec_cols(wu, 256 * b, 256))
        for c in range(KC):
            for hlf in range(2):
                src = wd[hlf * 2816:(hlf + 1) * 2816, c * 128:(c + 1) * 128].rearrange("(k p) n -> p k n", p=128)
                specs.append((src, 22, 128))
        wp = self.W["w_ple_proj"][l]
        for hlf in range(2):
            specs.append((wp[:, hlf * 1024:(hlf + 1) * 1024].rearrange("(k p) n -> p k n", p=128), 2, 1024))
        wpg = self.W["w_ple_gate"][l]
        for b in range(D // 256):
            specs.append(self.wspec_cols(wpg, 256 * b, 256))
        return specs

    def prenorm_chunk(self, l, n, c):
        self.stats_chunk(c, KC, self.h[c], self.h[c][:, :], "act")
        self.act(self.xt[c][:, :], self.h[c][:, :], AF.Identity, [self.h[c], self.cb], [self.xt[c]],
                 scale=self.g(l, n, c))

    def rstd_tokmajor(self, r):
        trb = self.next_tr()
        for j in range(4):
            self.tr(trb[:, j * 128:(j + 1) * 128], r[:, j * 128:(j + 1) * 128], self.ident[:], [r, self.cb], [trb])
        rt, _ = self.nxt("rtm")
        self.cp("dve", rt[:, 0:4], trb[:, :].rearrange("p (j t) -> p j t", j=4)[:, :, 0], [trb], [rt])
        return rt

    def win_stage(self, l, tok0, pos0):
        S = self.S
        sc = self.sc
        rr = {}

        def get_r():
            if "r" not in rr:
                rr["r"] = self.stats_finish(D)
                rr["rt"] = self.rstd_tokmajor(rr["r"])
            return rr["r"], rr["rt"]
        if l == 1:
            for i, nm in enumerate(("rope_c1", "rope_s1", "rope_c2", "rope_s2")):
                S.dma("sp", self.rope[i][:, :], self.C[nm][:, pos0:pos0 + TT], "rope", writes=[self.rope[i]])
        rope_pend = []
        for (nm, c0, ncols, mode, scale, rope) in self.win_plan(l):
            dst = sc[nm]
            if rope == 0:
                while rope_pend:
                    rope_pend.pop(0)()
            if mode == "L":
                wb, wv = self.wnext()
                pbk = self.next_mm()
                for k in range(KC):
                    self.mm(pbk[0:32, :], wv[:, k, 0:32], self.xt[k][:, :], k == 0, k == KC - 1,
                            [wb, self.xt[k]], [pbk])
                t, _ = self.nxt("tmp")
                r, rt = get_r()
                self.tt("dve", t[0:32, :], pbk[0:32, :], r[0:32, :], ALU.mult, [pbk, r], [t])
                S.dma("sp", dst[:, tok0:tok0 + TT], t[0:32, :], "tmp_st", reads=[t])
                continue
            for b in range(ncols // 256):
                wb, wv = self.wnext()
                if "F" in mode:
                    for sub in range(2):
                        f0 = 256 * b + 128 * sub
                        pbk = self.next_mm()
                        for k in range(KC):
                            self.mm(pbk[:, :], wv[:, k, sub * 128:(sub + 1) * 128], self.xt[k][:, :],
                                    k == 0, k == KC - 1, [wb, self.xt[k]], [pbk])
                        st, si = self.nxt("stg")
                        r, rt = get_r()
                        if rope == 0:
                            self.stt(st[:, :], pbk[:, :], scale, r[:, :], ALU.mult, ALU.mult, [pbk, r], [st])
                        else:
                            perm = self.perm1 if rope == 1 else self.perm2
                            rc, rs = (self.rope[0], self.rope[1]) if rope == 1 else (self.rope[2], self.rope[3])
                            qs, _ = self.nxt("qs")
                            self.stt(qs[:, :], pbk[:, :], scale, r[:, :], ALU.mult, ALU.mult, [pbk, r], [qs])

                            def fin(qs=qs, perm=perm, rc=rc, rs=rs, st=st, si=si, f0=f0, dst=dst):
                                trb = self.next_tr()
                                self.mm(trb[:, :], perm[:], qs[:, :], True, True, [qs, self.cb], [trb])
                                t2, _ = self.nxt("tmp")
                                self.tt("pool", t2[:, :], qs[:, :], rc[:, :], ALU.mult, [qs, rc], [t2])
                                t3, _ = self.nxt("tmp")
                                self.tt("dve", t3[:, :], trb[:, :], rs[:, :], ALU.mult, [trb, rs], [t3])
                                self.tt("pool", st[:, :], t2[:, :], t3[:, :], ALU.add, [t2, t3], [st])
                                S.dma("sp", dst[f0:f0 + 128, tok0:tok0 + TT], st[:, :], "stg%d" % si, reads=[st])
                            rope_pend.append(fin)
                            while len(rope_pend) > 1:
                                rope_pend.pop(0)()
                            continue
                        S.dma("sp", dst[f0:f0 + 128, tok0:tok0 + TT], st[:, :], "stg%d" % si, reads=[st])
                if "T" in mode:
                    dstT = sc["KA"] if mode == "FT" else dst
                    for jp in range(2):
                        pbk = self.next_mm()
                        for jj in range(2):
                            j = 2 * jp + jj
                            for k in range(KC):
                                self.mm(pbk[:, jj * 256:(jj + 1) * 256], self.xt[k][:, j * 128:(j + 1) * 128],
                                        wv[:, k, :], k == 0, k == KC - 1, [wb, self.xt[k]], [pbk])
                        st, si = self.nxt("stg")
                        r, rt = get_r()
                        assert scale == 1.0
                        for jj in range(2):
                            j = 2 * jp + jj
                            self.act(st[:, jj * 256:(jj + 1) * 256], pbk[:, jj * 256:(jj + 1) * 256], AF.Identity,
                                     [pbk, rt], [st], scale=rt[:, j:j + 1])
                        r0 = tok0 + jp * 256
                        S.dma("sp", dstT[r0:r0 + 256, 256 * b:256 * b + 256].rearrange("(j p) c -> p j c", p=128),
                              st[:, :].rearrange("p (j c) -> p j c", c=256), "stg%d" % si, reads=[st])

        while rope_pend:
            rope_pend.pop(0)()

    def phase_A0(self):
        S = self.S
        tiles = [(s, t) for s in range(self.nseq) for t in range(NTT)]
        self.wstream_begin(self.win_wspecs(0), len(tiles), "A0")
        def load_x(tok0):
            for j in range(4):
                S.dma("sp", self.xin[j][:, :], self.x[tok0 + j * 128:tok0 + (j + 1) * 128, :], "xin%d" % j,
                      writes=[self.xin[j]])

        load_x(0)
        for tix, (s, t) in enumerate(tiles):
            tok0 = s * T + t * TT
            for k in range(KC):
                trb = self.next_tr()
                for j in range(4):
                    self.tr(trb[:, j * 128:(j + 1) * 128], self.xin[j][:, k * 128:(k + 1) * 128], self.ident[:],
                            [self.xin[j], self.cb], [trb])
                self.cp("act" if k % 2 == 0 else "dve", self.h[k][:, :], trb[:, :], [trb], [self.h[k]])
                S.dma("sp", self.sc["HS"][k * 128:(k + 1) * 128, tok0:tok0 + TT], self.h[k][:, :], "hst",
                      reads=[self.h[k]])
                self.prenorm_chunk(0, 0, k)
            if tix + 1 < len(tiles):
                s2, t2 = tiles[tix + 1]
                load_x(s2 * T + t2 * TT)
            self.win_stage(0, tok0, t * TT)
        S.barrier()

    def proj_chunk(self, wb, wv, sub, pbk, xbufs=None, nk=KC, k0=0, first=True, last=True):
        xb = self.xt if xbufs is None else xbufs
        for k in range(nk):
            self.mm(pbk[:, :], wv[:, k, sub * 128:(sub + 1) * 128], xb[k0 + k][:, :],
                    first and k == 0, last and k == nk - 1, [wb, xb[k0 + k]], [pbk])

    def postnorm_residual(self, l, n, cast_xt=False, pre=None):
        r = self.stats_finish(D)
        for c in range(KC):
            t, _ = self.nxt("tmp")
            self.tt("pool", t[:, :], self.sct[c][:, :], r[:, :], ALU.mult, [self.sct[c], r], [t])
            self.stt(self.h[c][:, :], t[:, :], self.g(l, n, c), self.h[c][:, :], ALU.mult, ALU.add,
                     [t, self.h[c], self.cb], [self.h[c]])
            if cast_xt:
                self.cp("act", self.xt[c][:, :], self.h[c][:, :], [self.h[c]], [self.xt[c]])
            if pre is not None:
                self.prenorm_chunk(pre[0], pre[1], c)

    def phase_C(self, l):
        S = self.S
        last = (l == 1)
        tiles = [(s, ti) for s in range(self.nseq) for ti in range(NTT)]
        specs = self.chain_wspecs(l)
        if not last:
            specs += self.win_wspecs(l + 1)
        self.wstream_begin(specs, len(tiles), "C%d" % l)
        YT = self.sc["YT%d" % l]
        ybuf = self.actt[16:32]

        def load_h(tok0):
            for k in range(KC):
                S.dma("sp", self.h[k][:, :], self.sc["HS"][k * 128:(k + 1) * 128, tok0:tok0 + TT], "hld",
                      writes=[self.h[k]])

        def load_y(tok0):
            for k in range(KC):
                S.dma("sp", ybuf[k][:, :], YT[k * 128:(k + 1) * 128, tok0:tok0 + TT], "yld", writes=[ybuf[k]])

        def load_p(tok0):
            S.dma("sp", self.pin.ap[:, :, :],
                  self.p[l, tok0:tok0 + TT, :].rearrange("(j p) c -> p j c", p=128), "pld", writes=[self.pin])

        toks = [s_ * T + ti * TT for (s_, ti) in tiles]
        load_y(toks[0])
        load_h(toks[0])
        load_p(toks[0])
        for tix, (s, ti) in enumerate(tiles):
            tok0 = toks[tix]
            nxt_tok = toks[tix + 1] if tix + 1 < len(tiles) else None
            for b in range(D // 256):
                wb, wv = self.wnext()
                for sub in range(2):
                    c = 2 * b + sub
                    pbk = self.next_mm()
                    self.proj_chunk(wb, wv, sub, pbk, xbufs=ybuf)
                    self.cp("act", self.sct[c][:, :], pbk[:, :], [pbk], [self.sct[c]])
                    self.stats_chunk(c, KC, pbk, pbk[:, :], "act")
            self.postnorm_residual(l, 1, pre=(l, 2))
            r2 = None
            for b in range(DFF // 256):
                wbg, wvg = self.wnext()
                wbu, wvu = self.wnext()
                for sub in range(2):
                    f = 2 * b + sub
                    pg = self.next_mm()
                    self.proj_chunk(wbg, wvg, sub, pg)
                    pu = self.next_mm()
                    self.proj_chunk(wbu, wvu, sub, pu)
                    if r2 is None:
                        r2 = self.stats_finish(D)
                    t, _ = self.nxt("tmp")
                    t2, _ = self.nxt("tmp")
                    self.tt("dve", t[:, :], pg[:, :], r2[:, :], ALU.mult, [pg, r2], [t])
                    self.act(t[:, :], t[:, :], AF.Silu, [t], [t])
                    self.tt("dve", t2[:, :], pu[:, :], r2[:, :], ALU.mult, [pu, r2], [t2])
                    self.tt("dve", self.actt[f][:, :], t[:, :], t2[:, :], ALU.mult, [t, t2], [self.actt[f]])
            for c in range(KC):
                pbk = self.next_mm()
                for hlf in range(2):
                    wb, wv = self.wnext()
                    self.proj_chunk(wb, wv, 0, pbk, xbufs=self.actt, nk=22, k0=22 * hlf,
                                    first=(hlf == 0), last=(hlf == 1))
                self.cp("act", self.sct[c][:, :], pbk[:, :], [pbk], [self.sct[c]])
                self.stats_chunk(c, KC, pbk, pbk[:, :], "act")
            if nxt_tok is not None:
                load_y(nxt_tok)
            self.postnorm_residual(l, 3, cast_xt=True)
            for kk in range(2):
                trb = self.next_tr()
                for j in range(4):
                    self.tr(trb[:, j * 128:(j + 1) * 128], self.pin.ap[:, j, kk * 128:(kk + 1) * 128], self.ident[:],
                            [self.pin, self.cb], [trb])
                self.cp("dve", self.pt[kk][:, :], trb[:, :], [trb], [self.pt[kk]])
            if nxt_tok is not None:
                load_p(nxt_tok)
            for hlf in range(2):
                wb, wv = self.wnext()
                for sub in range(8):
                    c = 8 * hlf + sub
                    pbk = self.next_mm()
                    self.proj_chunk(wb, wv, sub, pbk, xbufs=self.pt, nk=2)
                    self.cp("act", self.sct[c][:, :], pbk[:, :], [pbk], [self.sct[c]])
                    self.stats_chunk(c, KC, pbk, pbk[:, :], "act")
            r4 = self.stats_finish(D)
            for b in range(D // 256):
                wb, wv = self.wnext()
                for sub in range(2):
                    c = 2 * b + sub
                    pbk = self.next_mm()
                    self.proj_chunk(wb, wv, sub, pbk)
                    sg, _ = self.nxt("tmp")
                    self.act(sg[:, :], pbk[:, :], AF.Sigmoid, [pbk], [sg])
                    t1, _ = self.nxt("tmp")
                    self.tt("pool", t1[:, :], self.sct[c][:, :], r4[:, :], ALU.mult, [self.sct[c], r4], [t1])
                    self.stt(t1[:, :], t1[:, :], self.g(l, 4, c), sg[:, :], ALU.mult, ALU.mult,
                             [t1, sg, self.cb], [t1])
                    self.tt("dve", self.h[c][:, :], self.h[c][:, :], t1[:, :], ALU.add, [self.h[c], t1], [self.h[c]])
                    if not last:
                        self.stats_chunk(c, KC, self.h[c], self.h[c][:, :], "act")
            if last:
                for j in range(4):
                    (ob, alias) = self.ostg[j % 2]
                    for k4 in range(4):
                        trb = self.next_tr()
                        for kk in range(4):
                            k = 4 * k4 + kk
                            self.tr(trb[:, kk * 128:(kk + 1) * 128], self.h[k][:, j * 128:(j + 1) * 128], self.ident[:],
                                    [self.h[k], self.cb], [trb])
                        self.cp("act" if k4 % 2 == 0 else "dve", ob[:, k4 * 512:(k4 + 1) * 512], trb[:, :],
                                [trb], [ob] + alias)
                    S.dma("sp", self.y[tok0 + j * 128:tok0 + (j + 1) * 128, :], ob[:, :], "ost%d" % (j % 2),
                          reads=[ob] + alias)
                if nxt_tok is not None:
                    load_h(nxt_tok)
            else:
                for k in range(KC):
                    S.dma("sp", self.sc["HS"][k * 128:(k + 1) * 128, tok0:tok0 + TT], self.h[k][:, :], "hst",
                          reads=[self.h[k]])
                for c in range(KC):
                    self.act(self.xt[c][:, :], self.h[c][:, :], AF.Identity, [self.h[c], self.cb], [self.xt[c]],
                             scale=self.g(l + 1, 0, c))
                if nxt_tok is not None:
                    load_h(nxt_tok)
                self.win_stage(l + 1, tok0, ti * TT)
        S.barrier()


    def phase_B1(self):
        with contextlib.ExitStack() as es:
            self.mix_common(es)
            self.dilated_all(es)
        self.S.barrier()
        with contextlib.ExitStack() as es:
            self.mix_common(es)
            self.diff_all(es)
        self.S.barrier()

    def mix_common(self, es):
        sbt = lambda n, s, d: self.sb(es, n, s, d)
        pt = sbt("mx_pt", [128, 6, 512], BF16)
        self.mpt = [Buf(pt[:, i, :], "mpt%d" % i) for i in range(6)]
        self.mpt_i = 0
        t32 = sbt("mx_t32", [128, 4, 512], F32)
        self.m32 = [Buf(t32[:, i, :], "m32_%d" % i) for i in range(4)]
        self.m32_i = 0
        st = sbt("mx_stg", [128, 4, 512], BF16)
        self.mstg = [Buf(st[:, i, :], "mstg%d" % i) for i in range(4)]
        self.mstg_i = 0


    def pipe(self, n, A, B, C, L=3):
        ctx = {}
        for i in range(n + L):
            if i < n:
                c = A(i)
                ctx[i] = B(i, c)
            if i >= L:
                C(i - L, ctx.pop(i - L))
            if self.later:
                due = [f for (d, f) in self.later if d <= i]
                self.later = [(d, f) for (d, f) in self.later if d > i]
                for f in due:
                    f()
        for (d, f) in self.later:
            f()
        self.later = []

    def diff_all(self, es):
        S, nc = self.S, self.nc
        sbt = lambda n, s, d: self.sb(es, n, s, d)
        lam_init = 0.8 - 0.6 * math.exp(-0.3 * 1)
        lam = sbt("df_lam", [128, 256], F32)
        lamb = Buf(lam, "lam")
        sc8 = sbt("df_sc", [128, 8], F32)
        scb = Buf(sc8, "dfsc")
        S.dma("sp", lam[:, :], self.W["diff_lambda"].rearrange("a b -> (a b)").partition_broadcast(128), "c0",
              writes=[lamb])
        with nc.allow_non_contiguous_dma(reason="tiny per-partition gain load"):
            S.dma("sp", sc8[:, 4:5], self.W["diff_subln_g"].rearrange("a p -> p a"), "c0", writes=[scb])
        prod = sbt("df_prod", [128, 128], F32)
        pb_ = Buf(prod, "prod")
        self.tt("dve", prod[:, 0:64], lam[:, 0:64], lam[:, 64:128], ALU.mult, [lamb], [pb_])
        self.tt("dve", prod[:, 64:128], lam[:, 128:192], lam[:, 192:256], ALU.mult, [lamb], [pb_])
        self.S.op("dve", lambda: nc.vector.tensor_reduce(sc8[:, 0:1], prod[:, 0:64], mybir.AxisListType.X, ALU.add),
                  reads=[pb_], writes=[scb])
        self.S.op("dve", lambda: nc.vector.tensor_reduce(sc8[:, 1:2], prod[:, 64:128], mybir.AxisListType.X, ALU.add),
                  reads=[pb_], writes=[scb])
        self.act(sc8[:, 0:2], sc8[:, 0:2], AF.Exp, [scb], [scb])
        self.tt("dve", sc8[:, 2:3], sc8[:, 1:2], sc8[:, 0:1], ALU.subtract, [scb], [scb])
        self.ts("dve", sc8[:, 3:4], sc8[:, 2:3], -lam_init, ALU.add, [scb], [scb])
        self.ts("dve", sc8[:, 5:6], sc8[:, 4:5], 1.0 - lam_init, ALU.mult, [scb], [scb])
        neglam = sc8[:, 3:4]
        gsub = sc8[:, 5:6]
        vt = sbt("df_v", [128, 16, 1024], BF16)
        vb = Buf(vt, "dfv")
        qk = sbt("df_qk", [128, 2, 2, T], BF16)
        qkb = [(Buf(qk[:, i, 0, :], "dfq%d" % i), Buf(qk[:, i, 1, :], "dfk%d" % i)) for i in range(2)]
        kz = sbt("df_kz", [128, 2, 2, T], BF16)
        kzb = [[Buf(kz[:, i, c, :], "dfkz%d_%d" % (i, c)) for c in range(2)] for i in range(2)]
        for i in range(2):
            for c in range(2):
                self.memset("pool", kz[:, i, c, :], 0.0, [kzb[i][c]])
        YT = self.sc["YT1"]
        self.later = []
        dsq = sbt("df_sq", [128, 2, 512], BF16)
        self.dfsq = [Buf(dsq[:, i, :], "dfsq%d" % i) for i in range(2)]
        self.dfsq_i = 0
        for s in range(self.nseq):
            S.dma("sp", vt[:, :, :], self.sc["VD"][s * T:(s + 1) * T, :].rearrange("(k p) c -> p k c", p=128), "dfv",
                  writes=[vb])
            steps = [(h, qblk, c, kt) for h in range(8) for qblk in range(4) for c in range(2) for kt in range(16)]
            sbanks = self.pb[4:7]
            state = {"res": {}}

            def load_head(h):
                qb, kb = qkb[h % 2]
                S.dma("sp", qb[:, :], self.sc["QDT"][h * 128:(h + 1) * 128, s * T:(s + 1) * T], "dfq%d" % (h % 2),
                      writes=[qb])
                S.dma("sp", kb[:, :], self.sc["KDT"][h * 128:(h + 1) * 128, s * T:(s + 1) * T], "dfk%d" % (h % 2),
                      writes=[kb])
                self.cp("pool", kz[0:64, h % 2, 0, :], kb[0:64, :], [kb], [kzb[h % 2][0]])
                self.cp("pool", kz[64:128, h % 2, 1, :], kb[64:128, :], [kb], [kzb[h % 2][1]])

            load_head(0)

            def A(i):
                h, qblk, c, kt = steps[i]
                if qblk == 0 and c == 0 and kt == 0 and h + 1 < 8:
                    load_head(h + 1)
                qb, kb = qkb[h % 2]
                sb_ = sbanks[i % 3]
                q0 = qblk * 512
                kzc = kzb[h % 2][c]
                self.mm(sb_[:, :], kzc[:, kt * 128:(kt + 1) * 128], qb[:, q0:q0 + 512], True, True, [kzc, qb], [sb_])
                return sb_

            def B(i, sb_):
                p_, _ = self.nxt("mpt")
                self.act(p_[:, :], sb_[:, :], AF.Exp, [sb_], [p_])
                return p_

            def C(i, p_):
                h, qblk, c, kt = steps[i]
                ob, zb = self.pb[c], self.pb[2 + c]
                self.mm(ob[:, :], vt[:, kt, h * 128:(h + 1) * 128], p_[:, :], kt == 0, kt == 15, [vb, p_], [ob])
                self.mm(zb[:, :], self.ones_bf[:], p_[:, :], kt == 0, kt == 15, [self.cb, p_], [zb])
                if kt == 15:
                    r, _ = self.nxt("m32")
                    self.recip(r[:, :], zb[:, :], [zb], [r])
                    self.tt("dve", r[:, :], ob[:, :], r[:, :], ALU.mult, [ob, r], [r])
                    state["res"][c] = r
                    if c == 1:
                        r0, r1 = state["res"][0], state["res"][1]
                        q0 = qblk * 512
                        self.stt(r0[:, :], r1[:, :], neglam, r0[:, :], ALU.mult, ALU.add, [r0, r1, scb], [r0])
                        sq, _ = self.nxt("dfsq")
                        self.tt("pool", sq[:, :], r0[:, :], r0[:, :], ALU.mult, [r0], [sq])

                        def fin(r0=r0, r1=r1, sq=sq, h=h, q0=q0):
                            ssb = self.x_bank_df
                            self.mm(ssb[:, :], self.ones_bf[:], sq[:, :], True, True, [self.cb, sq], [ssb])
                            self.act(r1[:, :], ssb[:, :], AF.Sqrt, [ssb, self.cb], [r1], scale=1.0 / 128,
                                     bias=self.eps_t[:, 0:1])
                            self.recip(r1[:, :], r1[:, :], [r1], [r1])
                            st, sti = self.nxt("mstg")
                            self.stt(st[:, :], r0[:, :], gsub, r1[:, :], ALU.mult, ALU.mult, [r0, r1, scb], [st])
                            S.dma("sp", YT[1024 + h * 128:1024 + (h + 1) * 128, s * T + q0:s * T + q0 + 512], st[:, :],
                                  "mstg%d" % sti, reads=[st])
                        self.later.append((i + 3 + 6, fin))

            self.x_bank_df = self.pb[7]
            self.pipe(len(steps), A, B, C, L=3)

    def dilated_all(self, es):
        S, nc = self.S, self.nc
        sbt = lambda n, s, d: self.sb(es, n, s, d)
        mk32 = sbt("dl_mk32", [128, 256], F32)
        mk = sbt("dl_mk", [128, 256], BF16)
        mkb = Buf(mk, "dlmask")
        S.dma("sp", mk32[:, :], self.C["dil_mask"][:, :], "c0", writes=[mkb])
        self.ts("dve", mk[:, :], mk32[:, :], -1.0, ALU.add, [mkb], [mkb], s2=-NEG, op1=ALU.mult)
        vts = [sbt("dl_v%d" % g, [128, 16, 1024], BF16) for g in range(3)]
        vbs = [Buf(vts[g], "dlv%d" % g) for g in range(3)]
        kt_ = sbt("dl_k", [128, 2, T], BF16)
        kbuf = [Buf(kt_[:, i, :], "dlk%d" % i) for i in range(2)]
        kp_ = sbt("dl_kp", [128, 2, 2, T], BF16)
        kpb = [[Buf(kp_[:, j, i, :], "dlkp%d_%d" % (j, i)) for i in range(2)] for j in range(2)]
        qt_ = sbt("dl_q", [128, 2, 3, T], BF16)
        qbuf = [[Buf(qt_[:, j, i, :], "dlq%d_%d" % (j, i)) for i in range(3)] for j in range(2)]
        qp_ = sbt("dl_qp", [128, 2, 2, T], BF16)
        qpb = [[Buf(qp_[:, j, i, :], "dlqp%d_%d" % (j, i)) for i in range(2)] for j in range(2)]
        acc = sbt("dl_acc", [128, 2, T], F32)
        acco, accz = Buf(acc[:, 0, :], "acco"), Buf(acc[:, 1, :], "accz")
        yst = sbt("dl_y", [128, T], BF16)
        ystb = Buf(yst, "dly")
        YT = self.sc["YT1"]
        DIL = (1, 4, 16)
        self.later = []
        for s in range(self.nseq):
            vsrc = self.sc["VC"][s * T:(s + 1) * T, :]
            S.dma("sp", vts[0][:, :, :], vsrc.rearrange("(k p) c -> p k c", p=128), "dlv0", writes=[vbs[0]])
            for b in range(4):
                S.dma("sp", vts[1][:, :, :].rearrange("p (r b) c -> p r b c", r=4)[:, :, b, :],
                      vsrc.rearrange("(b p r) c -> p r b c", p=128, r=4)[:, :, b, :], "dlv1", writes=[vbs[1]])
            S.dma("sp", vts[2][:, :, :], vsrc.rearrange("(p r) c -> p r c", r=16), "dlv2", writes=[vbs[2]])

            def prep_head(h):
                j = h % 2
                kb = kbuf[j]
                S.dma("sp", kb[:, :], self.sc["KCT"][h * 128:(h + 1) * 128, s * T:(s + 1) * T], "dlk%d" % j, writes=[kb])
                for g in range(3):
                    S.dma("sp", qbuf[j][g][:, :],
                          self.sc["QCT"][g * 1024 + h * 128:g * 1024 + (h + 1) * 128, s * T:(s + 1) * T],
                          "dlq%d_%d" % (j, g), writes=[qbuf[j][g]])
                for gi, d in ((0, 4), (1, 16)):
                    self.cp("pool", kpb[j][gi][:, :].rearrange("p (r i) -> p r i", r=d),
                            kb[:, :].rearrange("p (i r) -> p r i", r=d), [kb], [kpb[j][gi]])
                    self.cp("act", qpb[j][gi][:, :].rearrange("p (r i) -> p r i", r=d),
                            qbuf[j][gi + 1][:, :].rearrange("p (i r) -> p r i", r=d), [qbuf[j][gi + 1]], [qpb[j][gi]])

            steps = []
            blk = 0
            for h in range(8):
                for g in range(3):
                    d = DIL[g]
                    n = T // d
                    for rho in range(d):
                        for Q0 in range(0, n, 512):
                            Q1 = min(Q0 + 512, n)
                            b_lo = max(0, (Q0 - 64) // 128)
                            b_hi = min(n // 128 - 1, (Q1 - 1 + 64) // 128)
                            bl = []
                            for b in range(b_lo, b_hi + 1):
                                qs = max(Q0, 128 * b - 64)
                                qe = min(Q1, 128 * b + 192)
                                if qe - qs > 0:
                                    bl.append((b, qs, qe))
                            for bi, (b, qs, qe) in enumerate(bl):
                                steps.append(dict(h=h, g=g, d=d, n=n, rho=rho, Q0=Q0, Q1=Q1, b=b, qs=qs, qe=qe,
                                                  first=(bi == 0), last=(bi == len(bl) - 1), blk=blk,
                                                  hfirst=(g == 0 and Q0 == 0 and bi == 0),
                                                  hlast=(g == 2 and rho == d - 1 and bi == len(bl) - 1)))
                            blk += 1
            prep_head(0)
            sbanks = self.pb[4:8]

            def A(i):
                st = steps[i]
                h, g, n, rho, b = st["h"], st["g"], st["n"], st["rho"], st["b"]
                if st["hfirst"] and h + 1 < 8:
                    prep_head(h + 1)
                j = h % 2
                ksrc = kbuf[j] if g == 0 else kpb[j][g - 1]
                qsrc = qbuf[j][0] if g == 0 else qpb[j][g - 1]
                N = st["qe"] - st["qs"]
                sbk = sbanks[i % 4]
                kbase = rho * n + 128 * b
                qoff = st["qs"] - (128 * b - 64)
                self.mm(sbk[:, 0:N], self.ident_bf[:], mk[:, qoff:qoff + N], True, False, [self.cb, mkb], [sbk])
                self.mm(sbk[:, 0:N], ksrc[:, kbase:kbase + 128], qsrc[:, rho * n + st["qs"]:rho * n + st["qe"]],
                        False, True, [ksrc, qsrc], [sbk])
                return sbk

            def B(i, sbk):
                st = steps[i]
                N = st["qe"] - st["qs"]
                qoff = st["qs"] - (128 * st["b"] - 64)
                p_, _ = self.nxt("mpt")
                self.act(p_[:, 0:N], sbk[:, 0:N], AF.Exp, [sbk], [p_])
                return p_

            def C(i, p_):
                st = steps[i]
                h, g, d, n, rho, b, Q0, Q1 = st["h"], st["g"], st["d"], st["n"], st["rho"], st["b"], st["Q0"], st["Q1"]
                N = st["qe"] - st["qs"]
                ob = self.pb[st["blk"] % 2]
                zb = self.pb[2 + st["blk"] % 2]
                tile_i = b if g == 0 else (rho * 4 + b if g == 1 else rho)
                self.mm(ob[:, st["qs"] - Q0:st["qe"] - Q0], vts[g][:, tile_i, h * 128:(h + 1) * 128], p_[:, 0:N],
                        st["first"], False, [vbs[g], p_], [ob], skip_group_check=True)
                self.mm(zb[:, st["qs"] - Q0:st["qe"] - Q0], self.ones_bf[:], p_[:, 0:N],
                        st["first"], False, [self.cb, p_], [zb], skip_group_check=True)
                if st["last"]:
                    L_ = Q1 - Q0
                    ov = acco[:, :].rearrange("p (i r) -> p r i", r=d)[:, rho, Q0:Q1]
                    zv = accz[:, :].rearrange("p (i r) -> p r i", r=d)[:, rho, Q0:Q1]
                    if g == 0:
                        self.cp("act", ov, ob[:, 0:L_], [ob], [acco])
                        self.cp("dve", zv, zb[:, 0:L_], [zb], [accz])
                    else:
                        self.tt("dve", ov, ov, ob[:, 0:L_], ALU.add, [ob, acco], [acco])
                        self.tt("dve", zv, zv, zb[:, 0:L_], ALU.add, [zb, accz], [accz])
                if st["hlast"]:
                    self.recip(accz[:, :], accz[:, :], [accz], [accz])
                    self.tt("dve", yst[:, :], acco[:, :], accz[:, :], ALU.mult, [acco, accz], [ystb])
                    S.dma("sp", YT[h * 128:(h + 1) * 128, s * T:(s + 1) * T], yst[:, :], "dly", reads=[ystb])

            self.pipe(len(steps), A, B, C, L=3)

    def phase_B0(self):
        with contextlib.ExitStack() as es:
            self.mix_common(es)
            self.na_tables(es)
        self.S.barrier()
        with contextlib.ExitStack() as es:
            self.mix_common(es)
            self.na_all(es)
        self.S.barrier()
        with contextlib.ExitStack() as es:
            self.gla_all(es)
        self.S.barrier()

    @staticmethod
    def na_row(r):
        r0 = min(max(r - 4, 0), 24)
        return r0, r0 - r + 7

    def na_tables(self, es):
        S, nc = self.S, self.nc
        sbt = lambda n, s, d: self.sb(es, n, s, d)
        self.MB = self.dscr("MB", [8, 5, 128, 512], BF16)
        R1D = self.dscr("R1D", [120, 64, 64], F32)
        rp = sbt("na_rp", [128, 128], F32)
        rpb_ = Buf(rp, "rp")
        self.memset("dve", rp[:, :], 0.0, [rpb_])
        S.dma("sp", rp[0:120, 0:31], self.W["na_rpb"][:, :], "c0", writes=[rpb_])
        trb = self.next_tr()
        self.tr(trb[:, 0:128], rp[:, :], self.ident[:], [rpb_, self.cb], [trb])
        rpT = sbt("na_rpT", [32, 128], F32)
        rpTb = Buf(rpT, "rpT")
        self.cp("dve", rpT[0:32, 0:128], trb[0:32, 0:128], [trb], [rpTb])
        toe = sbt("na_toe", [32, 4096], F32)
        toeb = Buf(toe, "toe")
        self.memset("dve", toe[:, :], 0.0, [toeb])
        S.dma("sp", toe[0:31, :], self.C["na_toe"][:, :], "c0", writes=[toeb])
        r1 = sbt("na_r1", [128, 4096], F32)
        r1b = Buf(r1, "r1")
        for c in range(8):
            pbk = self.next_mm()
            self.mm(pbk[:, :], rpT[0:32, 0:128], toe[0:32, c * 512:(c + 1) * 512], True, True, [rpTb, toeb], [pbk])
            self.cp("act" if c % 2 else "dve", r1[0:120, c * 512:(c + 1) * 512], pbk[0:120, :], [pbk], [r1b])
        S.dma("sp", R1D.rearrange("a b c -> a (b c)"), r1[0:120, :], "na_r1", reads=[r1b])
        S.barrier()
        cm = sbt("na_cm", [128, 64], F32)
        cmb = Buf(cm, "cm")
        S.dma("sp", cm[:, :], self.C["na_colmask"][:, :], "c0", writes=[cmb])
        NTB = 8
        tb = sbt("na_tb", [128, NTB, 512], F32)
        tbb = [[Buf(tb[64 * (q % 2):64 * (q % 2) + 64, i, (q // 2) * 256:(q // 2 + 1) * 256], "natb%d_%d" % (i, q))
                for q in range(4)] for i in range(NTB)]
        tbo_ = sbt("na_tbo", [128, NTB, 512], BF16)
        tbo = [Buf(tbo_[:, i, :], "natbo%d" % i) for i in range(NTB)]
        PAIRS = ((0, 1), (2, 3), (4, 5), (28, 29), (30, 31))
        cnt = 0
        for h in range(8):
            for pc, (ra, rb) in enumerate(PAIRS):
                tq = tbb[cnt % NTB]
                t = tb[:, cnt % NTB, :]
                cnt += 1
                for slot, r in enumerate((ra, rb)):
                    r0, dr0 = self.na_row(r)
                    for par in range(2):
                        row0 = h * 15 + dr0 + par
                        src = R1D[row0:row0 + 7:2, :, :].rearrange("t k c -> k t c")
                        qb_ = tq[2 * slot + par]
                        dst = qb_[:, :].rearrange("k (t c) -> k t c", c=64)
                        S.dma(("sp", "pool")[(slot + par) % 2], dst, src, "natbq%d" % (2 * slot + par), writes=[qb_])
                to = tbo[cnt % NTB]
                self.tt("dve", to[:, :].rearrange("p (a c) -> p a c", c=64), t.rearrange("p (a c) -> p a c", c=64),
                        cm[:, :].unsqueeze(1).broadcast_to([128, 8, 64]), ALU.add, list(tq) + [cmb], [to])
                S.dma("sp", self.MB[h, pc, :, :], to[:, :], "natbo", reads=[to])

    def na_all(self, es):
        S, nc = self.S, self.nc
        sbt = lambda n, s, d: self.sb(es, n, s, d)
        v0 = sbt("na_v0", [128, 16, 1024], BF16)
        v1 = sbt("na_v1", [128, 15, 1024], BF16)
        v0b, v1b = Buf(v0, "nav0"), Buf(v1, "nav1")
        qk = sbt("na_qk", [128, 2, 2, T], BF16)
        qkb = [(Buf(qk[:, i, 0, :], "naq%d" % i), Buf(qk[:, i, 1, :], "nak%d" % i)) for i in range(2)]
        mb = sbt("na_mb", [128, 2, 5, 512], BF16)
        mbb = [Buf(mb[:, i], "namb%d" % i) for i in range(2)]
        YT = self.sc["YT0"]
        PCLS = {0: 0, 2: 1, 28: 3, 30: 4}
        self.later = []
        for s in range(self.nseq):
            vsrc = self.sc["VB"][s * T:(s + 1) * T, :]
            S.dma("sp", v0[:, :, :], vsrc.rearrange("(k p) c -> p k c", p=128), "nav0", writes=[v0b])
            S.dma("sp", v1[:, :, :], vsrc[64:64 + 15 * 128, :].rearrange("(k p) c -> p k c", p=128), "nav1", writes=[v1b])

            def load_head(h):
                qb, kb = qkb[h % 2]
                mt = mbb[h % 2]
                S.dma("sp", qb[:, :], self.sc["QBT"][h * 128:(h + 1) * 128, s * T:(s + 1) * T], "naq%d" % (h % 2), writes=[qb])
                S.dma("sp", kb[:, :], self.sc["KBT"][h * 128:(h + 1) * 128, s * T:(s + 1) * T], "nak%d" % (h % 2), writes=[kb])
                S.dma("sp", mt[:, :, :], self.MB[h].rearrange("a p c -> p a c"), "namb%d" % (h % 2), writes=[mt])

            steps = [(h, blk, pr) for h in range(8) for blk in range(4) for pr in range(4)]
            load_head(0)
            sbanks = self.pb[4:8]

            def A(i):
                h, blk, pr = steps[i]
                if blk == 0 and pr == 0 and h + 1 < 8:
                    load_head(h + 1)
                qb, kb = qkb[h % 2]
                ra = blk * 8 + pr * 2
                sbk = sbanks[i % 4]
                mt = mbb[h % 2]
                pc = PCLS.get(ra, 2)
                self.mm(sbk[:, :], self.ident_bf[:], mt[:, pc, :], True, False, [self.cb, mt], [sbk])
                for slot in range(2):
                    r = ra + slot
                    r0, _ = self.na_row(r)
                    for tp in range(4):
                        k0 = 64 * r0 + 128 * tp
                        self.mm(sbk[:, slot * 256 + tp * 64:slot * 256 + (tp + 1) * 64], kb[:, k0:k0 + 128],
                                qb[:, r * 64:(r + 1) * 64], False, slot == 1 and tp == 3, [kb, qb], [sbk],
                                skip_group_check=True)
                return sbk

            def B(i, sbk):
                h, blk, pr = steps[i]
                mt = mbb[h % 2]
                pc = PCLS.get(blk * 8 + pr * 2, 2)
                p_, _ = self.nxt("mpt")
                self.act(p_[:, :], sbk[:, :], AF.Exp, [sbk], [p_])
                return p_

            def C(i, p_):
                h, blk, pr = steps[i]
                ra = blk * 8 + pr * 2
                ob = self.pb[(h * 4 + blk) % 2]
                zb = self.pb[2 + (h * 4 + blk) % 2]
                for slot in range(2):
                    r = ra + slot
                    r0, _ = self.na_row(r)
                    c0 = (r - blk * 8) * 64
                    for tp in range(4):
                        k0 = 64 * r0 + 128 * tp
                        if k0 % 128 == 0:
                            vl, vbuf = v0[:, k0 // 128, h * 128:(h + 1) * 128], v0b
                        else:
                            vl, vbuf = v1[:, (k0 - 64) // 128, h * 128:(h + 1) * 128], v1b
                        pr_ = p_[:, slot * 256 + tp * 64:slot * 256 + (tp + 1) * 64]
                        self.mm(ob[:, c0:c0 + 64], vl, pr_, tp == 0, tp == 3, [vbuf, p_], [ob], skip_group_check=True)
                        self.mm(zb[:, c0:c0 + 64], self.ones_bf[:], pr_, tp == 0, tp == 3, [self.cb, p_], [zb],
                                skip_group_check=True)
                if pr == 3:
                    rz, _ = self.nxt("m32")
                    self.recip(rz[:, :], zb[:, :], [zb], [rz])
                    st, sti = self.nxt("mstg")
                    self.tt("dve", st[:, :], ob[:, :], rz[:, :], ALU.mult, [ob, rz], [st])
                    S.dma("sp", YT[1024 + h * 128:1024 + (h + 1) * 128, s * T + blk * 512:s * T + (blk + 1) * 512], st[:, :],
                          "mstg%d" % sti, reads=[st])

            self.pipe(len(steps), A, B, C, L=3)

    def nb(self):
        b = self.pb[self.nb_i % 8]
        self.nb_i += 1
        return b

    def gla_all(self, es):
        S, nc = self.S, self.nc
        sbt = lambda n, s, d: self.sb(es, n, s, d)
        self.nb_i = 0
        NCH = T // 128
        gm = sbt("gl_m", [128, 6, 128], F32)
        gmb = Buf(gm, "glm")
        S.dma("sp", gm[:, :, :], self.C["gla_m"].rearrange("a p c -> p a c"), "c0", writes=[gmb])
        M1, M2, M3, M4, MF, MB_ = [gm[:, i, :] for i in range(6)]
        wg = sbt("gl_wg", [32, 2, 512], F32)
        wgb = Buf(wg, "glwg")
        self.memset("dve", wg[:, :, :], 0.0, [wgb])
        for i, (wn, bn) in enumerate((("gla_wg_fwd", "gla_bg_fwd"), ("gla_wg_bwd", "gla_bg_bwd"))):
            S.dma("sp", wg[0:16, i, :], self.W[wn][:, :], "c0", writes=[wgb])
            S.dma("sp", wg[16:17, i, :], self.W[bn][:, :], "c0", writes=[wgb])
        lr = sbt("gl_lr", [32, 2, T], F32)
        lrb = Buf(lr, "gllr")
        self.memset("dve", lr[:, :, :], 0.0, [lrb])
        for i in range(2):
            S.dma("sp", lr[16:17, i, :], self.C["ones_row"][:, :], "c0", writes=[lrb])
        gn = sbt("gl_gn", [128, 256], F32)
        gnb = Buf(gn, "glgn")
        S.dma("sp", gn[:, :], self.W["gla_norm_g"].rearrange("a b -> (a b)").partition_broadcast(128), "c0", writes=[gnb])
        one_t = sbt("gl_one", [128, 1], F32)
        oneb = Buf(one_t, "glone")
        self.memset("dve", one_t[:, :], 1.0, [oneb])
        qa = sbt("gl_q", [128, 4, T], BF16)
        ka = sbt("gl_k", [128, 4, T], BF16)
        kt = sbt("gl_kt", [128, NCH, 512], BF16)
        va = sbt("gl_v", [128, NCH, 1024], BF16)
        qab, kab, ktb, vab = Buf(qa, "glq"), Buf(ka, "glk"), Buf(kt, "glkt"), Buf(va, "glv")
        ra = sbt("gl_ra", [128, 2, 1024], BF16)
        rab = [Buf(ra[:, i, :], "glra%d" % i) for i in range(2)]
        sbst = sbt("gl_sb", [128, NCH, 4, 256], BF16)
        sbb = [Buf(sbst[:, n], "glsb%d" % n) for n in range(NCH)]
        s32 = sbt("gl_s32", [128, 4, 256], F32)
        s32b = [Buf(s32[:, h, :], "gls32_%d" % h) for h in range(4)]
        sf = sbt("gl_sf", [128, 2, 4, 256], BF16)
        sfb = [[Buf(sf[:, i, h, :], "glsf%d_%d" % (i, h)) for h in range(4)] for i in range(2)]
        f32t = sbt("gl_f32", [128, 6, 512], F32)
        f32b = [Buf(f32t[:, i, :], "glf%d" % i) for i in range(6)]
        self.glf = f32b
        self.glf_i = 0
        g32t = sbt("gl_g32", [128, 6, 512], F32)
        self.glg = [Buf(g32t[:, i, :], "glg%d" % i) for i in range(6)]
        self.glg_i = 0
        b16t = sbt("gl_b16", [128, 16, 512], BF16)
        b16b = [Buf(b16t[:, i, :], "glh%d" % i) for i in range(16)]
        self.glh = b16b
        self.glh_i = 0
        sm = sbt("gl_sm", [128, 4, 8], F32)
        smb = [Buf(sm[:, i, :], "glsm%d" % i) for i in range(4)]
        self.glsm = smb
        self.glsm_i = 0
        ya2 = sbt("gl_ya", [128, 2, 1024], F32)
        yab2 = [Buf(ya2[:, i, :], "glya%d" % i) for i in range(2)]
        yst = sbt("gl_yst", [128, 1, 8, 512], BF16)
        ystb = [Buf(yst[:, 0], "glyst0")] * 2
        YT = self.sc["YT0"]

        def gate(n, di):
            pb = self.nb()
            self.mm(pb[:, :], lr[0:32, di, n * 128:(n + 1) * 128], wg[0:32, di, :], True, True, [lrb, wgb], [pb])
            e, _ = self.nxt("glg")
            self.act(e[:, :], pb[:, :], AF.Exp, [pb], [e], scale=-1.0)
            self.act(e[:, :], e[:, :], AF.Ln, [e, oneb], [e], bias=one_t[:, 0:1])
            return e

        for s in range(self.nseq):
            c0, c1 = s * T, (s + 1) * T
            S.dma("sp", qa[:, :, :], self.sc["QAT"][:, c0:c1].rearrange("(h p) t -> p h t", p=128), "glq", writes=[qab])
            S.dma("sp", ka[:, :, :], self.sc["KAT"][:, c0:c1].rearrange("(h p) t -> p h t", p=128), "glk", writes=[kab])
            S.dma("sp", kt[:, :, :], self.sc["KA"][c0:c1, :].rearrange("(c p) n -> p c n", p=128), "glkt", writes=[ktb])
            S.dma("sp", va[:, :, :], self.sc["VA"][c0:c1, :].rearrange("(c p) n -> p c n", p=128), "glv", writes=[vab])
            for i in range(2):
                S.dma("sp", lr[0:16, i, :], self.sc["LRT"][16 * i:16 * i + 16, c0:c1], "gllr", writes=[lrb])
            for h in range(4):
                self.memset("pool", s32[:, h, :], 0.0, [s32b[h]])
            self.memset("pool", sbst[:, NCH - 1], 0.0, [sbb[NCH - 1]])
            gnext = gate(NCH - 1, 1)
            for n in range(NCH - 1, 0, -1):
                Gb = gnext
                if n - 1 > 0:
                    gnext = gate(n - 1, 1)
                pe_ = self.nb()
                self.mm(pe_[:, :], M4, Gb[:, :], True, True, [gmb, Gb], [pe_])
                pd = self.nb()
                for h in range(4):
                    self.mm(pd[:, 2 * h:2 * h + 2], Gb[:, h * 128:(h + 1) * 128], M2[:, 0:2], True, True, [Gb, gmb], [pd])
                Ee, _ = self.nxt("glf")
                self.act(Ee[:, :], pe_[:, :], AF.Exp, [pe_], [Ee])
                dec, _ = self.nxt("glsm")
                self.act(dec[:, 0:8], pd[:, 0:8], AF.Exp, [pd], [dec])
                ku, _ = self.nxt("glh")
                self.tt("dve", ku[:, :], kt[:, n, :], Ee[:, :], ALU.mult, [ktb, Ee], [ku])
                for hp in range(2):
                    pu = self.nb()
                    for hh in range(2):
                        h = 2 * hp + hh
                        self.mm(pu[:, hh * 256:(hh + 1) * 256], ku[:, h * 128:(h + 1) * 128],
                                va[:, n, h * 256:(h + 1) * 256], True, True, [ku, vab], [pu])
                    for hh in range(2):
                        h = 2 * hp + hh
                        self.stt(s32[:, h, :], s32[:, h, :], dec[:, 2 * h:2 * h + 1], pu[:, hh * 256:(hh + 1) * 256],
                                 ALU.mult, ALU.add, [s32b[h], dec, pu], [s32b[h]])
                        self.cp("pool", sbst[:, n - 1, h, :], s32[:, h, :], [s32b[h]], [sbb[n - 1]])
            for h in range(4):
                self.memset("pool", s32[:, h, :], 0.0, [s32b[h]])
                self.memset("pool", sf[:, 0, h, :], 0.0, [sfb[0][h]])
            v3 = lambda ap: ap.rearrange("p (h t) -> p h t", h=4)
            v2 = lambda ap: ap.rearrange("p (h t) -> p h t", h=2)

            def stage1a(n):
                return gate(n, 0), gate(n, 1)

            def stage1(n, gg):
                rb = rab[n % 2]
                S.dma("sp", rb[:, :], self.sc["RA"][c0 + n * 128:c0 + (n + 1) * 128, :], "glra%d" % (n % 2), writes=[rb])
                Gf, Gb = gg
                pcf, pcb, pef = self.nb(), self.nb(), self.nb()
                for h in range(4):
                    self.mm(pcf[:, h * 128:(h + 1) * 128], Gf[:, h * 128:(h + 1) * 128], M1, True, True, [Gf, gmb], [pcf])
                for h in range(4):
                    self.mm(pcb[:, h * 128:(h + 1) * 128], Gb[:, h * 128:(h + 1) * 128], M2, True, True, [Gb, gmb], [pcb])
                self.mm(pef[:, :], M3, Gf[:, :], True, True, [gmb, Gf], [pef])
                Ecf, _ = self.nxt("glf")
                Encf, _ = self.nxt("glf")
                Ecb, _ = self.nxt("glf")
                Encb, _ = self.nxt("glf")
                Eef, _ = self.nxt("glf")
                self.act(Ecf[:, :], pcf[:, :], AF.Exp, [pcf], [Ecf])
                self.act(Encf[:, :], pcf[:, :], AF.Exp, [pcf], [Encf], scale=-1.0)
                self.act(Ecb[:, :], pcb[:, :], AF.Exp, [pcb], [Ecb])
                self.act(Encb[:, :], pcb[:, :], AF.Exp, [pcb], [Encb], scale=-1.0)
                self.act(Eef[:, :], pef[:, :], AF.Exp, [pef], [Eef])
                dec, _ = self.nxt("glsm")
                self.cp("pool", dec[:, 0:4], Ecf[:, :].rearrange("p (h t) -> p h t", h=4)[:, :, 127], [Ecf], [dec])
                qn = qa[:, :, n * 128:(n + 1) * 128]
                kn = ka[:, :, n * 128:(n + 1) * 128]
                qdf, _ = self.nxt("glh")
                kdf, _ = self.nxt("glh")
                qdb, _ = self.nxt("glh")
                kdb, _ = self.nxt("glh")
                kuf, _ = self.nxt("glh")
                self.tt("dve", v3(qdf[:, :]), qn, v3(Ecf[:, :]), ALU.mult, [qab, Ecf], [qdf])
                self.tt("pool", v3(kdf[:, :]), kn, v3(Encf[:, :]), ALU.mult, [kab, Encf], [kdf])
                self.tt("dve", v3(qdb[:, :]), qn, v3(Ecb[:, :]), ALU.mult, [qab, Ecb], [qdb])
                self.tt("pool", v3(kdb[:, :]), kn, v3(Encb[:, :]), ALU.mult, [kab, Encb], [kdb])
                self.tt("dve", kuf[:, :], kt[:, n, :], Eef[:, :], ALU.mult, [ktb, Eef], [kuf])
                return dict(rb=rb, dec=dec, qdf=qdf, kdf=kdf, qdb=qdb, kdb=kdb, kuf=kuf)

            def stage2(n, c):
                rb, dec, qdf, kdf, qdb, kdb, kuf = c["rb"], c["dec"], c["qdf"], c["kdf"], c["qdb"], c["kdb"], c["kuf"]
                pA, pB = self.nb(), self.nb()
                for h in range(4):
                    hs = slice(h * 128, (h + 1) * 128)
                    self.mm(pA[:, hs], kdf[:, hs], qdf[:, hs], True, True, [kdf, qdf], [pA])
                for h in range(4):
                    hs = slice(h * 128, (h + 1) * 128)
                    self.mm(pB[:, hs], kdb[:, hs], qdb[:, hs], True, True, [kdb, qdb], [pB])
                atf, _ = self.nxt("glh")
                atb, _ = self.nxt("glh")
                self.tt("dve", v3(atf[:, :]), v3(pA[:, :]), MF.unsqueeze(1).broadcast_to([128, 4, 128]), ALU.mult,
                        [pA, gmb], [atf])
                self.tt("dve", v3(atb[:, :]), v3(pB[:, :]), MB_.unsqueeze(1).broadcast_to([128, 4, 128]), ALU.mult,
                        [pB, gmb], [atb])
                sfc = sfb[n % 2]
                sfn = sfb[(n + 1) % 2]
                po = [self.nb(), self.nb()]
                for h in range(4):
                    hs = slice(h * 128, (h + 1) * 128)
                    o_ = po[h // 2][:, (h % 2) * 256:(h % 2 + 1) * 256]
                    vv = va[:, n, h * 256:(h + 1) * 256]
                    self.mm(o_, atf[:, hs], vv, True, False, [atf, vab], [po[h // 2]])
                    self.mm(o_, atb[:, hs], vv, False, False, [atb, vab], [po[h // 2]])
                    self.mm(o_, qdf[:, hs], sf[:, n % 2, h, :], False, False, [qdf, sfc[h]], [po[h // 2]])
                    self.mm(o_, qdb[:, hs], sbst[:, n, h, :], False, True, [qdb, sbb[n]], [po[h // 2]])
                if n < NCH - 1:
                    for hp in range(2):
                        pu = self.nb()
                        for hh in range(2):
                            h = 2 * hp + hh
                            self.mm(pu[:, hh * 256:(hh + 1) * 256], kuf[:, h * 128:(h + 1) * 128],
                                    va[:, n, h * 256:(h + 1) * 256], True, True, [kuf, vab], [pu])
                        for hh in range(2):
                            h = 2 * hp + hh
                            self.stt(s32[:, h, :], s32[:, h, :], dec[:, h:h + 1],
                                     pu[:, hh * 256:(hh + 1) * 256], ALU.mult, ALU.add, [s32b[h], dec, pu], [s32b[h]])
                            self.cp("pool", sf[:, (n + 1) % 2, h, :], s32[:, h, :], [s32b[h]], [sfn[h]])
                ss, _ = self.nxt("glsm")
                junk, _ = self.nxt("glf")
                for h in range(4):
                    o_ = po[h // 2][:, (h % 2) * 256:(h % 2 + 1) * 256]
                    self.S.op("act", (lambda o_=o_, h=h, junk=junk, ss=ss: nc.scalar.activation(
                        junk[:, 0:256], o_, AF.Square, accum_out=ss[:, h:h + 1])),
                        reads=[po[h // 2]], writes=[junk, ss])
                self.act(ss[:, 4:8], ss[:, 0:4], AF.Sqrt, [ss, self.cb], [ss], scale=1.0 / 256, bias=self.eps_t[:, 0:1])
                self.recip(ss[:, 4:8], ss[:, 4:8], [ss], [ss])
                sr, _ = self.nxt("glf")
                sr2, _ = self.nxt("glf")
                self.act(sr[:, :], rb[:, 0:512], AF.Silu, [rb], [sr])
                self.act(sr2[:, :], rb[:, 512:1024], AF.Silu, [rb], [sr2])
                g2 = gn[:, :].unsqueeze(1).broadcast_to([128, 2, 256])
                self.tt("pool", v2(sr[:, :]), v2(sr[:, :]), g2, ALU.mult, [sr, gnb], [sr])
                self.tt("pool", v2(sr2[:, :]), v2(sr2[:, :]), g2, ALU.mult, [sr2, gnb], [sr2])
                ya = ya2[:, n % 2, :]
                yab = yab2[n % 2]
                for h in range(4):
                    o_ = po[h // 2][:, (h % 2) * 256:(h % 2 + 1) * 256]
                    gsrc = sr if h < 2 else sr2
                    self.stt(ya[:, h * 256:(h + 1) * 256], o_, ss[:, 4 + h:5 + h], gsrc[:, (h % 2) * 256:(h % 2 + 1) * 256],
                             ALU.mult, ALU.mult, [po[h // 2], ss, gsrc], [yab])

                def tail(n=n, ya=ya, yab=yab):
                    ysb = ystb[(n // 4) % 2]
                    ysv = yst[:, 0]
                    for half in range(2):
                        ptb = self.nb()
                        for f in range(4):
                            ft = half * 4 + f
                            self.tr(ptb[:, f * 128:(f + 1) * 128], ya[:, ft * 128:(ft + 1) * 128], self.ident[:],
                                    [yab, self.cb], [ptb])
                        self.cp("act" if half == 0 else "dve",
                                ysv[:, half * 4:half * 4 + 4, (n % 4) * 128:(n % 4 + 1) * 128],
                                ptb[:, :].rearrange("p (f t) -> p f t", f=4), [ptb], [ysb])
                    if n % 4 == 3:
                        t0 = c0 + (n // 4) * 512
                        S.dma("sp", YT[0:1024, t0:t0 + 512].rearrange("(f p) t -> p f t", p=128), ysv[:, :, :],
                              "glyst0", reads=[ysb])
                return tail

            gates = {0: stage1a(0), 1: stage1a(1)}
            ctxs = {0: stage1(0, gates.pop(0))}
            tails = {}
            for n in range(NCH):
                if n + 2 < NCH:
                    gates[n + 2] = stage1a(n + 2)
                if n + 1 < NCH:
                    ctxs[n + 1] = stage1(n + 1, gates.pop(n + 1))
                tails[n] = stage2(n, ctxs.pop(n))
                if n - 1 in tails:
                    tails.pop(n - 1)()
            tails.pop(NCH - 1)()

    def build(self, phases=("A0", "B0", "C0", "B1", "C1")):
        self.setup()
        with contextlib.ExitStack() as tes:
            if "A0" in phases:
                self.alloc_tok(tes)
                self.phase_A0()
        if "B0" in phases:
            self.phase_B0()
        if "C0" in phases:
            with contextlib.ExitStack() as tes:
                self.alloc_tok(tes)
                self.phase_C(0)
        if "B1" in phases:
            self.phase_B1()
        if "C1" in phases:
            with contextlib.ExitStack() as tes:
                self.alloc_tok(tes)
                self.phase_C(1)
        self.S.barrier()
        self.S.final_wait()
        return self.nc


MIX_CONST_SHAPES = {"dil_mask": [128, 256], "na_toe": [31, 4096], "na_colmask": [128, 64],
                    "gla_m": [6, 128, 128], "ones_row": [1, T]}


def _mix_consts():
    cs = {}
    k = np.arange(128)[:, None]
    q = np.arange(256)[None, :]
    cs["dil_mask"] = ((q >= k) & (q <= k + 128)).astype(np.float32)
    kc = np.arange(64)[:, None]
    c = np.arange(64)[None, :]
    dc = kc - c + 15
    toe = np.zeros((31, 64, 64), np.float32)
    for i in range(31):
        toe[i] = (dc == i)
    cs["na_toe"] = toe.reshape(31, 4096)
    c0 = np.clip(c - 8, 0, 48)
    ok = (kc >= c0) & (kc < c0 + 16)
    cm = np.where(ok, 0.0, NEG).astype(np.float32)
    cs["na_colmask"] = np.concatenate([cm, cm], 0)
    r = np.arange(128)[:, None]
    i = np.arange(128)[None, :]
    sc = -1.0 / 16.0
    gm = np.stack([(r <= i) * sc, (r >= i) * sc, (r > i) * sc, (r < i) * sc, (r <= i) * 1.0, (r > i) * 1.0]).astype(np.float32)
    cs["gla_m"] = gm
    cs["ones_row"] = np.ones((1, T), np.float32)
    return cs


def build_program(nseq, dbg=(), phases=("A0", "B0", "C0", "B1", "C1")):
    b1 = Builder(nseq, None, dbg)
    b1.build(phases)
    needed = b1.S.needed
    b1.es.close()
    b2 = Builder(nseq, needed, dbg)
    nc = b2.build(phases)
    return nc, b2


_PROG = {}


def kernel(x_prompt, x_sample, p_prompt, p_sample, norm_g, w_in_even, w_out_even, gla_wg_fwd, gla_bg_fwd,
           gla_wg_bwd, gla_bg_bwd, gla_norm_g, na_rpb, w_in_odd, w_out_odd, diff_lambda, diff_subln_g,
           w_ffn_gate, w_ffn_up, w_ffn_down, w_ple_proj, w_ple_gate):
    ncores, nseq = 8, 3
    f32 = lambda a: np.ascontiguousarray(np.asarray(a), dtype=np.float32)
    x_prompt, x_sample, p_prompt, p_sample = f32(x_prompt), f32(x_sample), f32(p_prompt), f32(p_sample)
    wts = {
        "norm_g": f32(norm_g), "w_in_even": f32(w_in_even)[0], "w_out_even": f32(w_out_even)[0],
        "gla_wg_fwd": f32(gla_wg_fwd)[0], "gla_bg_fwd": f32(gla_bg_fwd), "gla_wg_bwd": f32(gla_wg_bwd)[0],
        "gla_bg_bwd": f32(gla_bg_bwd), "gla_norm_g": f32(gla_norm_g), "na_rpb": f32(na_rpb).reshape(120, 31),
        "w_in_odd": f32(w_in_odd)[0], "w_out_odd": f32(w_out_odd)[0], "diff_lambda": f32(diff_lambda)[0],
        "diff_subln_g": f32(diff_subln_g), "w_ffn_gate": f32(w_ffn_gate), "w_ffn_up": f32(w_ffn_up),
        "w_ffn_down": f32(w_ffn_down), "w_ple_proj": f32(w_ple_proj), "w_ple_gate": f32(w_ple_gate),
    }
    for k, shp in WEIGHT_SHAPES.items():
        wts[k] = np.ascontiguousarray(wts[k].reshape(shp))
    consts = _consts()
    consts.update(_mix_consts())
    if "nc" not in _PROG:
        _PROG["nc"] = build_program(nseq)[0]
    nc = _PROG["nc"]
    in_maps = []
    for c in range(ncores):
        xs = np.concatenate([x_prompt[c], x_sample[2 * c], x_sample[2 * c + 1]], axis=0)
        ps = np.concatenate([p_prompt[:, c], p_sample[:, 2 * c], p_sample[:, 2 * c + 1]], axis=1)
        m = {"x": np.ascontiguousarray(xs), "p": np.ascontiguousarray(ps)}
        m.update(wts)
        m.update(consts)
        in_maps.append(m)
    res = run_bass_kernel_spmd(nc, in_maps, core_ids=list(range(ncores)))
    y_prompt = np.empty((8, T, D), np.float32)
    y_sample = np.empty((16, T, D), np.float32)
    for c in range(ncores):
        y = np.asarray(res.results[c]["y"], dtype=np.float32)
        y_prompt[c] = y[0:T]
        y_sample[2 * c] = y[T:2 * T]
        y_sample[2 * c + 1] = y[2 * T:3 * T]
    return (y_prompt, y_sample)
```

```python
import contextlib
import math
import numpy as np
import concourse.bass as bass
import concourse.mybir as mybir
from concourse.bass_utils import run_bass_kernel_spmd

F32 = mybir.dt.float32
BF16 = mybir.dt.bfloat16
AF = mybir.ActivationFunctionType
ALU = mybir.AluOpType

T = 2048
D = 2048
TT = 512
NTT = T // TT
KC = D // 128
DFF = 5632
FC = DFF // 128
PLE = 256
EPS = 1e-6
NEG = -30000.0
EVEN_W = 6176
ODD_W = 8192
ROPE_THETA = 500000.0


class Buf:
    __slots__ = ("ap", "w", "r", "name")

    def __init__(self, ap, name=""):
        self.ap = ap
        self.w = None
        self.r = {}
        self.name = name

    def __getitem__(self, idx):
        return self.ap[idx]


class Sched:
    ENGS = ("pe", "act", "dve", "pool", "sp")

    def __init__(self, nc, es, needed=None):
        self.nc = nc
        self.es = es
        self.dry = needed is None
        self.needed = needed if needed is not None else {n: set() for n in self.ENGS}
        self.eng = {"pe": nc.tensor, "act": nc.scalar, "dve": nc.vector, "pool": nc.gpsimd, "sp": nc.sync}
        self.sem = {n: es.enter_context(nc.semaphore("s_" + n)) for n in self.ENGS}
        self.cnt = {n: 0 for n in self.ENGS}
        self.inc = {n: 0 for n in self.ENGS}
        self.map = {n: {} for n in self.ENGS}
        self.seen = {n: {} for n in self.ENGS}
        self.dsem = {}
        self.dcnt = {}
        self.nwaits = 0

    def dkey(self, key):
        if key not in self.dsem:
            self.dsem[key] = self.es.enter_context(self.nc.semaphore("d_" + key))
            self.dcnt[key] = 0
        return key

    def _wait(self, en, ev):
        if ev[0] == "e":
            _, src, idx = ev
            if src == en and en in ("pe",):
                return
            key = ("e", src)
            if self.seen[en].get(key, 0) >= idx:
                return
            self.seen[en][key] = idx
            if self.dry:
                self.needed[src].add(idx)
            else:
                self.eng[en].wait_ge(self.sem[src], self.map[src][idx])
        else:
            _, k, val = ev
            val = self.dcnt[k]
            key = ("d", k)
            if self.seen[en].get(key, 0) >= val:
                return
            self.seen[en][key] = val
            if not self.dry:
                self.eng[en].wait_ge(self.dsem[k], val)
        self.nwaits += 1

    def _deps(self, en, reads, writes):
        for b in reads:
            if b.w is not None:
                self._wait(en, b.w)
        for b in writes:
            if b.w is not None:
                self._wait(en, b.w)
            for ev in b.r.values():
                self._wait(en, ev)

    def _mark(self, me, rk, reads, writes):
        for b in writes:
            b.w = me
            b.r = {}
        for b in reads:
            b.r[rk] = me

    def op(self, en, fn, reads=(), writes=()):
        self._deps(en, reads, writes)
        self.cnt[en] += 1
        idx = self.cnt[en]
        if not self.dry:
            ins = fn()
            if idx in self.needed[en]:
                self.inc[en] += 1
                self.map[en][idx] = self.inc[en]
                ins.then_inc(self.sem[en], 1)
        me = ("e", en, idx)
        self._mark(me, ("e", en), reads, writes)
        return me

    def dma(self, q, out, in_, key, reads=(), writes=(), **kw):
        self.dkey(key)
        self._deps(q, reads, writes)
        self.dcnt[key] += 16
        if not self.dry:
            self.eng[q].dma_start(out=out, in_=in_, **kw).then_inc(self.dsem[key], 16)
        me = ("d", key, self.dcnt[key])
        self._mark(me, ("d", key), reads, writes)
        return me

    def barrier(self):
        for en in self.ENGS:
            for src in self.ENGS:
                if src != en and self.cnt[src] > 0:
                    self._wait(en, ("e", src, self.cnt[src]))
            for k in self.dsem:
                if self.dcnt[k] > 0:
                    self._wait(en, ("d", k, self.dcnt[k]))

    def final_wait(self):
        for k in self.dsem:
            if self.dcnt[k] > 0:
                self._wait("sp", ("d", k, self.dcnt[k]))


def _rope_tables():
    def tab(rot, period):
        half = rot // 2
        inv = ROPE_THETA ** (-np.arange(half, dtype=np.float32) / half)
        ang = np.arange(T, dtype=np.float32)[None, :] * inv[:, None]
        c = np.ones((128, T), np.float32)
        s = np.zeros((128, T), np.float32)
        perm = np.zeros((128, 128), np.float32)
        for base in range(0, 128, period):
            c[base:base + half] = np.cos(ang)
            c[base + half:base + rot] = np.cos(ang)
            s[base:base + half] = -np.sin(ang)
            s[base + half:base + rot] = np.sin(ang)
            for i in range(half):
                perm[base + half + i, base + i] = 1.0
                perm[base + i, base + half + i] = 1.0
        return c, s, perm
    return tab(32, 128), tab(16, 64)


def _consts():
    cs = {}
    cs["ident"] = np.eye(128, dtype=np.float32)
    (c1, s1, p1), (c2, s2, p2) = _rope_tables()
    cs["rope_c1"], cs["rope_s1"], cs["rope_p1"] = c1, s1, p1
    cs["rope_c2"], cs["rope_s2"], cs["rope_p2"] = c2, s2, p2
    return cs


CONST_SHAPES = {
    "ident": [128, 128],
    "rope_c1": [128, T], "rope_s1": [128, T], "rope_p1": [128, 128],
    "rope_c2": [128, T], "rope_s2": [128, T], "rope_p2": [128, 128],
}

WEIGHT_SHAPES = {
    "norm_g": [2, 5, D], "w_in_even": [D, EVEN_W], "w_out_even": [D, D],
    "gla_wg_fwd": [16, 512], "gla_bg_fwd": [1, 512], "gla_wg_bwd": [16, 512], "gla_bg_bwd": [1, 512],
    "gla_norm_g": [1, 256], "na_rpb": [120, 31], "w_in_odd": [D, ODD_W], "w_out_odd": [D, D],
    "diff_lambda": [4, 64], "diff_subln_g": [1, 128],
    "w_ffn_gate": [2, D, DFF], "w_ffn_up": [2, D, DFF], "w_ffn_down": [2, DFF, D],
    "w_ple_proj": [2, PLE, D], "w_ple_gate": [2, D, D],
}


class Builder:
    def __init__(self, nseq, needed=None, dbg=(), stop_after=None):
        self.nseq = nseq
        self.ntok = nseq * T
        self.dbg = set(dbg)
        self.stop_after = stop_after
        self.nc = bass.Bass("TRN2", target_bir_lowering=False)
        self.es = contextlib.ExitStack()
        self.S = Sched(self.nc, self.es, needed)
        self.dram = {}
        self.uid = 0

    def din(self, name, shape, dt=F32):
        t = self.nc.dram_tensor(name, list(shape), dt, kind="ExternalInput").ap()
        self.dram[name] = t
        return t

    def dscr(self, name, shape, dt):
        kind = "ExternalOutput" if name in self.dbg else "Internal"
        t = self.nc.dram_tensor(name, list(shape), dt, kind=kind).ap()
        self.dram[name] = t
        return t

    def sb(self, es, name, shape, dt):
        self.uid += 1
        return es.enter_context(self.nc.sbuf_tensor("sb%d_%s" % (self.uid, name), list(shape), dt))

    def ps(self, es, name, shape, dt=F32):
        return es.enter_context(self.nc.psum_tensor("ps_" + name, list(shape), dt))

    def mm(self, out, lhsT, rhs, start, stop, reads, writes, **kw):
        nc = self.nc
        return self.S.op("pe", lambda: nc.tensor.matmul(out, lhsT, rhs, start=start, stop=stop, **kw),
                         reads=reads, writes=writes)

    def tr(self, out, in_, ident, reads, writes):
        nc = self.nc
        return self.S.op("pe", lambda: nc.tensor.transpose(out, in_, ident), reads=reads, writes=writes)

    def act(self, out, in_, func, reads, writes, scale=None, bias=None):
        nc = self.nc
        kw = {}
        if scale is not None:
            kw["scale"] = scale
        if bias is not None:
            kw["bias"] = bias
        return self.S.op("act", lambda: nc.scalar.activation(out, in_, func, **kw), reads=reads, writes=writes)

    def tt(self, en, out, in0, in1, op, reads, writes):
        e = self.nc.vector if en == "dve" else self.nc.gpsimd
        return self.S.op(en, lambda: e.tensor_tensor(out, in0, in1, op), reads=reads, writes=writes)

    def ts(self, en, out, in0, s1, op0, reads, writes, s2=None, op1=None):
        e = self.nc.vector if en == "dve" else self.nc.gpsimd
        if op1 is None:
            return self.S.op(en, lambda: e.tensor_scalar(out, in0, s1, None, op0), reads=reads, writes=writes)
        return self.S.op(en, lambda: e.tensor_scalar(out, in0, s1, s2, op0, op1), reads=reads, writes=writes)

    def stt(self, out, in0, scalar, in1, op0, op1, reads, writes):
        nc = self.nc
        return self.S.op("dve", lambda: nc.vector.scalar_tensor_tensor(out, in0, scalar, in1, op0, op1),
                         reads=reads, writes=writes)

    def cp(self, en, out, in_, reads, writes):
        nc = self.nc
        if en == "act":
            return self.S.op("act", lambda: nc.scalar.copy(out, in_), reads=reads, writes=writes)
        e = nc.vector if en == "dve" else nc.gpsimd
        return self.S.op(en, lambda: e.tensor_copy(out, in_), reads=reads, writes=writes)

    def memset(self, en, ap, val, writes):
        e = self.nc.vector if en == "dve" else self.nc.gpsimd
        return self.S.op(en, lambda: e.memset(ap, val), writes=writes)

    def recip(self, out, in_, reads, writes):
        nc = self.nc
        return self.S.op("dve", lambda: nc.vector.reciprocal(out, in_), reads=reads, writes=writes)

    def setup(self):
        nc, es, S = self.nc, self.es, self.S
        ntok = self.ntok
        self.x = self.din("x", [ntok, D])
        self.p = self.din("p", [2, ntok, PLE])
        self.W = {k: self.din(k, v) for k, v in WEIGHT_SHAPES.items()}
        self.C = {k: self.din(k, v) for k, v in CONST_SHAPES.items()}
        self.C.update({k: self.din(k, v) for k, v in MIX_CONST_SHAPES.items()})
        self.y = self.nc.dram_tensor("y", [ntok, D], F32, kind="ExternalOutput").ap()
        sc = {}
        sc["HS"] = self.dscr("HS", [D, ntok], F32)
        for l in range(2):
            sc["YT%d" % l] = self.dscr("YT%d" % l, [D, ntok], BF16)
        for nm, rows in (("QAT", 512), ("KAT", 512), ("QBT", 1024), ("KBT", 1024),
                         ("QCT", 3072), ("KCT", 1024), ("QDT", 1024), ("KDT", 1024)):
            sc[nm] = self.dscr(nm, [rows, ntok], BF16)
        for nm, cols in (("KA", 512), ("VA", 1024), ("RA", 1024), ("VB", 1024), ("VC", 1024), ("VD", 1024)):
            sc[nm] = self.dscr(nm, [ntok, cols], BF16)
        sc["LRT"] = self.dscr("LRT", [32, ntok], F32)
        self.sc = sc
        self.ident = self.sb(es, "ident", [128, 128], F32)
        self.ident_bf = self.sb(es, "ident_bf", [128, 128], BF16)
        self.ones_bf = self.sb(es, "ones_bf", [128, 128], BF16)
        self.eps_t = self.sb(es, "eps_t", [128, 1], F32)
        self.gT = self.sb(es, "gT", [128, 10, KC], F32)
        self.perm1 = self.sb(es, "perm1", [128, 128], BF16)
        self.perm2 = self.sb(es, "perm2", [128, 128], BF16)
        self.cb = Buf(None, "consts")
        ptmp = self.sb(es, "ptmp", [128, 256], F32)
        S.dma("sp", self.ident[:], self.C["ident"][:, :], "c0", writes=[self.cb])
        S.dma("sp", ptmp[:, 0:128], self.C["rope_p1"][:, :], "c0", writes=[self.cb])
        S.dma("sp", ptmp[:, 128:256], self.C["rope_p2"][:, :], "c0", writes=[self.cb])
        with nc.allow_non_contiguous_dma(reason="one-time gain vector gather"):
            for ln in range(10):
                S.dma("sp", self.gT[:, ln, :],
                      self.W["norm_g"][ln // 5, ln % 5, :].rearrange("(k p) -> p k", p=128), "c0",
                      writes=[self.cb])
        self.cp("dve", self.ident_bf[:], self.ident[:], [self.cb], [self.cb])
        self.cp("dve", self.perm1[:], ptmp[:, 0:128], [self.cb], [self.cb])
        self.cp("dve", self.perm2[:], ptmp[:, 128:256], [self.cb], [self.cb])
        self.memset("dve", self.ones_bf[:], 1.0, [self.cb])
        self.memset("dve", self.eps_t[:], EPS, [self.cb])
        self.pb = [Buf(self.ps(es, "pb%d" % i, [128, 512]), "pb%d" % i) for i in range(8)]
        self.mm_banks = self.pb[0:4]
        self.ss_bank = self.pb[4]
        self.tr_banks = self.pb[5:7]
        self.x_bank = self.pb[7]
        self.mm_i = 0
        self.tr_i = 0
        S.barrier()

    def next_mm(self):
        b = self.mm_banks[self.mm_i % len(self.mm_banks)]
        self.mm_i += 1
        return b

    def next_tr(self):
        b = self.tr_banks[self.tr_i % len(self.tr_banks)]
        self.tr_i += 1
        return b

    def alloc_tok(self, es):
        sbt = lambda n, s, d: self.sb(es, n, s, d)
        ht = sbt("hT", [128, KC, TT], F32)
        self.h = [Buf(ht[:, k, :], "h%d" % k) for k in range(KC)]
        xt = sbt("xT", [128, KC, TT], BF16)
        self.xt = [Buf(xt[:, k, :], "x%d" % k) for k in range(KC)]
        sct = sbt("scT", [128, KC, TT], F32)
        self.sct = [Buf(sct[:, k, :], "sc%d" % k) for k in range(KC)]
        self.xin = [Buf(sct[:, 4 * j:4 * j + 4, :].rearrange("p a b -> p (a b)"), "xin%d" % j) for j in range(4)]
        at = sbt("actT", [128, FC, TT], BF16)
        self.actt = [Buf(at[:, f, :], "a%d" % f) for f in range(FC)]
        self.ostg = []
        for j in range(2):
            v = at[:, 8 * j:8 * j + 8, :].rearrange("p a b -> p (a b)").bitcast(F32)
            self.ostg.append((Buf(v, "ostg%d" % j), self.actt[8 * j:8 * j + 8]))
        sq = sbt("sq", [128, 4, TT], BF16)
        self.sq = [Buf(sq[:, i, :], "sq%d" % i) for i in range(4)]
        self.sq_i = 0
        self.stats_pend = []
        tmp = sbt("tmp", [128, 6, TT], F32)
        self.tmp = [Buf(tmp[:, i, :], "tmp%d" % i) for i in range(6)]
        self.tmp_i = 0
        rs = sbt("rstd", [128, 3, TT], F32)
        self.rstd = [Buf(rs[:, i, :], "rstd%d" % i) for i in range(3)]
        rtm = sbt("rtm", [128, 2, 4], F32)
        self.rtm = [Buf(rtm[:, i, :], "rtm%d" % i) for i in range(2)]
        self.rtm_i = 0
        self.rstd_i = 0
        self.NW = 4
        wt = sbt("wslots", [128, self.NW, 4096], BF16)
        self.wslot = [Buf(wt[:, i, :], "w%d" % i) for i in range(self.NW)]
        stg = sbt("stg", [128, 6, TT], BF16)
        self.stg = [Buf(stg[:, i, :], "stg%d" % i) for i in range(6)]
        self.stg_i = 0
        qs_ = sbt("qs", [128, 4, TT], BF16)
        self.qs = [Buf(qs_[:, i, :], "qs%d" % i) for i in range(4)]
        self.qs_i = 0
        rp = sbt("rope", [128, 4, TT], F32)
        self.rope = [Buf(rp[:, i, :], "rope%d" % i) for i in range(4)]
        pt = sbt("pT", [128, 2, TT], BF16)
        self.pt = [Buf(pt[:, i, :], "pT%d" % i) for i in range(2)]
        pin = sbt("pin", [128, 4, PLE], F32)
        self.pin = Buf(pin, "pin")

    def nxt(self, name):
        lst = getattr(self, name)
        i = getattr(self, name + "_i")
        setattr(self, name + "_i", i + 1)
        return lst[i % len(lst)], i % len(lst)

    def wstream_begin(self, specs, ntiles, name):
        self.wspecs = specs
        self.w_n = len(specs)
        self.w_total = len(specs) * ntiles
        self.w_issued = 0
        self.w_used = 0
        self.wscr = self.dscr("WS_" + name, [len(specs), 128, 4096], BF16)
        self.wscr_b = [Buf(None, "ws%d" % i) for i in range(len(specs))]

    def wnext(self):
        S = self.S
        while self.w_issued < self.w_total and self.w_issued < self.w_used + self.NW - 1:
            bi = self.w_issued % self.w_n
            src, a, b = self.wspecs[bi]
            slot = self.wslot[self.w_issued % self.NW]
            key = "w%d" % (self.w_issued % self.NW)
            if self.w_issued < self.w_n:
                dst = slot[:, 0:a * b].rearrange("p (a b) -> p a b", b=b)
                S.dma("pool", dst, src, key, writes=[slot], max_dma_last_dim=4096)
                S.dma("sp", self.wscr[bi, :, 0:a * b], slot[:, 0:a * b], "wsb", reads=[slot], writes=[self.wscr_b[bi]])
            else:
                S.dma("pool", slot[:, 0:a * b], self.wscr[bi, :, 0:a * b], key, reads=[self.wscr_b[bi]], writes=[slot])
            self.w_issued += 1
        src, a, b = self.wspecs[self.w_used % self.w_n]
        slot = self.wslot[self.w_used % self.NW]
        self.w_used += 1
        return slot, slot[:, 0:a * b].rearrange("p (a b) -> p a b", b=b)

    def wspec_cols(self, w2d, c0, ncols):
        return (w2d[:, c0:c0 + ncols].rearrange("(k p) n -> p k n", p=128), w2d.shape[0] // 128, ncols)

    def stats_chunk(self, c, n, src_buf, src_ap, eng):
        sq, _ = self.nxt("sq")
        if eng == "act":
            self.act(sq[:, :], src_ap, AF.Square, [src_buf], [sq])
        else:
            self.tt(eng, sq[:, :], src_ap, src_ap, ALU.mult, [src_buf], [sq])
        self.stats_pend.append((sq, c == 0, c == n - 1))
        while len(self.stats_pend) > 2:
            self.stats_flush_one()

    def stats_flush_one(self):
        sq, first, last = self.stats_pend.pop(0)
        self.mm(self.ss_bank[:, :], self.ones_bf[:], sq[:, :], first, last, [sq, self.cb], [self.ss_bank])

    def stats_finish(self, nfeat):
        while self.stats_pend:
            self.stats_flush_one()
        r, _ = self.nxt("rstd")
        self.act(r[:, :], self.ss_bank[:, :], AF.Sqrt, [self.ss_bank, self.cb], [r],
                 scale=1.0 / nfeat, bias=self.eps_t[:, 0:1])
        self.recip(r[:, :], r[:, :], [r], [r])
        return r

    def g(self, l, n, k):
        return self.gT[:, l * 5 + n, k:k + 1]

    def win_plan(self, l):
        if l == 0:
            return [("QAT", 0, 512, "F", 128 ** -0.5, 0), ("KAT", 512, 512, "FT", 1.0, 0),
                    ("VA", 1024, 1024, "T", 1.0, 0), ("RA", 2048, 1024, "T", 1.0, 0),
                    ("LRT", 3072, 32, "L", 1.0, 0),
                    ("QBT", 3104, 1024, "F", 128 ** -0.5, 0), ("KBT", 4128, 1024, "F", 1.0, 0),
                    ("VB", 5152, 1024, "T", 1.0, 0)]
        return [("QCT", 0, 3072, "F", 128 ** -0.5, 1), ("KCT", 3072, 1024, "F", 1.0, 1),
                ("VC", 4096, 1024, "T", 1.0, 0),
                ("QDT", 5120, 1024, "F", 64 ** -0.5, 2), ("KDT", 6144, 1024, "F", 1.0, 2),
                ("VD", 7168, 1024, "T", 1.0, 0)]

    def win_wspecs(self, l):
        w = self.W["w_in_even" if l == 0 else "w_in_odd"]
        specs = []
        for (nm, c0, ncols, mode, scale, rope) in self.win_plan(l):
            if mode == "L":
                specs.append(self.wspec_cols(w, c0, 32))
            else:
                for b in range(ncols // 256):
                    specs.append(self.wspec_cols(w, c0 + 256 * b, 256))
        return specs

    def chain_wspecs(self, l):
        specs = []
        wo = self.W["w_out_even" if l == 0 else "w_out_odd"]
        for b in range(D // 256):
            specs.append(self.wspec_cols(wo, 256 * b, 256))
        wg, wu, wd = self.W["w_ffn_gate"][l], self.W["w_ffn_up"][l], self.W["w_ffn_down"][l]
        for b in range(DFF // 256):
            specs.append(self.wspec_cols(wg, 256 * b, 256))
            specs.append(self.wspec_cols(wu, 256 * b, 256))
        for c in range(KC):
            for hlf in range(2):
                src = wd[hlf * 2816:(hlf + 1) * 2816, c * 128:(c + 1) * 128].rearrange("(k p) n -> p k n", p=128)
                specs.append((src, 22, 128))
        wp = self.W["w_ple_proj"][l]
        for hlf in range(2):
            specs.append((wp[:, hlf * 1024:(hlf + 1) * 1024].rearrange("(k p) n -> p k n", p=128), 2, 1024))
        wpg = self.W["w_ple_gate"][l]
        for b in range(D // 256):
            specs.append(self.wspec_cols(wpg, 256 * b, 256))
        return specs

    def prenorm_chunk(self, l, n, c):
        self.stats_chunk(c, KC, self.h[c], self.h[c][:, :], "act")
        self.act(self.xt[c][:, :], self.h[c][:, :], AF.Identity, [self.h[c], self.cb], [self.xt[c]],
                 scale=self.g(l, n, c))

    def rstd_tokmajor(self, r):
        trb = self.next_tr()
        for j in range(4):
            self.tr(trb[:, j * 128:(j + 1) * 128], r[:, j * 128:(j + 1) * 128], self.ident[:], [r, self.cb], [trb])
        rt, _ = self.nxt("rtm")
        self.cp("dve", rt[:, 0:4], trb[:, :].rearrange("p (j t) -> p j t", j=4)[:, :, 0], [trb], [rt])
        return rt

    def win_stage(self, l, tok0, pos0):
        S = self.S
        sc = self.sc
        rr = {}

        def get_r():
            if "r" not in rr:
                rr["r"] = self.stats_finish(D)
                rr["rt"] = self.rstd_tokmajor(rr["r"])
            return rr["r"], rr["rt"]
        if l == 1:
            for i, nm in enumerate(("rope_c1", "rope_s1", "rope_c2", "rope_s2")):
                S.dma("sp", self.rope[i][:, :], self.C[nm][:, pos0:pos0 + TT], "rope", writes=[self.rope[i]])
        rope_pend = []
        for (nm, c0, ncols, mode, scale, rope) in self.win_plan(l):
            dst = sc[nm]
            if rope == 0:
                while rope_pend:
                    rope_pend.pop(0)()
            if mode == "L":
                wb, wv = self.wnext()
                pbk = self.next_mm()
                for k in range(KC):
                    self.mm(pbk[0:32, :], wv[:, k, 0:32], self.xt[k][:, :], k == 0, k == KC - 1,
                            [wb, self.xt[k]], [pbk])
                t, _ = self.nxt("tmp")
                r, rt = get_r()
                self.tt("dve", t[0:32, :], pbk[0:32, :], r[0:32, :], ALU.mult, [pbk, r], [t])
                S.dma("sp", dst[:, tok0:tok0 + TT], t[0:32, :], "tmp_st", reads=[t])
                continue
            for b in range(ncols // 256):
                wb, wv = self.wnext()
                if "F" in mode:
                    for sub in range(2):
                        f0 = 256 * b + 128 * sub
                        pbk = self.next_mm()
                        for k in range(KC):
                            self.mm(pbk[:, :], wv[:, k, sub * 128:(sub + 1) * 128], self.xt[k][:, :],
                                    k == 0, k == KC - 1, [wb, self.xt[k]], [pbk])
                        st, si = self.nxt("stg")
                        r, rt = get_r()
                        if rope == 0:
                            self.stt(st[:, :], pbk[:, :], scale, r[:, :], ALU.mult, ALU.mult, [pbk, r], [st])
                        else:
                            perm = self.perm1 if rope == 1 else self.perm2
                            rc, rs = (self.rope[0], self.rope[1]) if rope == 1 else (self.rope[2], self.rope[3])
                            qs, _ = self.nxt("qs")
                            self.stt(qs[:, :], pbk[:, :], scale, r[:, :], ALU.mult, ALU.mult, [pbk, r], [qs])

                            def fin(qs=qs, perm=perm, rc=rc, rs=rs, st=st, si=si, f0=f0, dst=dst):
                                trb = self.next_tr()
                                self.mm(trb[:, :], perm[:], qs[:, :], True, True, [qs, self.cb], [trb])
                                t2, _ = self.nxt("tmp")
                                self.tt("pool", t2[:, :], qs[:, :], rc[:, :], ALU.mult, [qs, rc], [t2])
                                t3, _ = self.nxt("tmp")
                                self.tt("dve", t3[:, :], trb[:, :], rs[:, :], ALU.mult, [trb, rs], [t3])
                                self.tt("pool", st[:, :], t2[:, :], t3[:, :], ALU.add, [t2, t3], [st])
                                S.dma("sp", dst[f0:f0 + 128, tok0:tok0 + TT], st[:, :], "stg%d" % si, reads=[st])
                            rope_pend.append(fin)
                            while len(rope_pend) > 1:
                                rope_pend.pop(0)()
                            continue
                        S.dma("sp", dst[f0:f0 + 128, tok0:tok0 + TT], st[:, :], "stg%d" % si, reads=[st])
                if "T" in mode:
                    dstT = sc["KA"] if mode == "FT" else dst
                    for jp in range(2):
                        pbk = self.next_mm()
                        for jj in range(2):
                            j = 2 * jp + jj
                            for k in range(KC):
                                self.mm(pbk[:, jj * 256:(jj + 1) * 256], self.xt[k][:, j * 128:(j + 1) * 128],
                                        wv[:, k, :], k == 0, k == KC - 1, [wb, self.xt[k]], [pbk])
                        st, si = self.nxt("stg")
                        r, rt = get_r()
                        assert scale == 1.0
                        for jj in range(2):
                            j = 2 * jp + jj
                            self.act(st[:, jj * 256:(jj + 1) * 256], pbk[:, jj * 256:(jj + 1) * 256], AF.Identity,
                                     [pbk, rt], [st], scale=rt[:, j:j + 1])
                        r0 = tok0 + jp * 256
                        S.dma("sp", dstT[r0:r0 + 256, 256 * b:256 * b + 256].rearrange("(j p) c -> p j c", p=128),
                              st[:, :].rearrange("p (j c) -> p j c", c=256), "stg%d" % si, reads=[st])

        while rope_pend:
            rope_pend.pop(0)()

    def phase_A0(self):
        S = self.S
        tiles = [(s, t) for s in range(self.nseq) for t in range(NTT)]
        self.wstream_begin(self.win_wspecs(0), len(tiles), "A0")
        def load_x(tok0):
            for j in range(4):
                S.dma("sp", self.xin[j][:, :], self.x[tok0 + j * 128:tok0 + (j + 1) * 128, :], "xin%d" % j,
                      writes=[self.xin[j]])

        load_x(0)
        for tix, (s, t) in enumerate(tiles):
            tok0 = s * T + t * TT
            for k in range(KC):
                trb = self.next_tr()
                for j in range(4):
                    self.tr(trb[:, j * 128:(j + 1) * 128], self.xin[j][:, k * 128:(k + 1) * 128], self.ident[:],
                            [self.xin[j], self.cb], [trb])
                self.cp("act" if k % 2 == 0 else "dve", self.h[k][:, :], trb[:, :], [trb], [self.h[k]])
                S.dma("sp", self.sc["HS"][k * 128:(k + 1) * 128, tok0:tok0 + TT], self.h[k][:, :], "hst",
                      reads=[self.h[k]])
                self.prenorm_chunk(0, 0, k)
            if tix + 1 < len(tiles):
                s2, t2 = tiles[tix + 1]
                load_x(s2 * T + t2 * TT)
            self.win_stage(0, tok0, t * TT)
        S.barrier()

    def proj_chunk(self, wb, wv, sub, pbk, xbufs=None, nk=KC, k0=0, first=True, last=True):
        xb = self.xt if xbufs is None else xbufs
        for k in range(nk):
            self.mm(pbk[:, :], wv[:, k, sub * 128:(sub + 1) * 128], xb[k0 + k][:, :],
                    first and k == 0, last and k == nk - 1, [wb, xb[k0 + k]], [pbk])

    def postnorm_residual(self, l, n, cast_xt=False, pre=None):
        r = self.stats_finish(D)
        for c in range(KC):
            t, _ = self.nxt("tmp")
            self.tt("pool" if c % 2 == 0 else "dve", t[:, :], self.sct[c][:, :], r[:, :], ALU.mult, [self.sct[c], r], [t])
            self.stt(self.h[c][:, :], t[:, :], self.g(l, n, c), self.h[c][:, :], ALU.mult, ALU.add,
                     [t, self.h[c], self.cb], [self.h[c]])
            if cast_xt:
                self.cp("act", self.xt[c][:, :], self.h[c][:, :], [self.h[c]], [self.xt[c]])
            if pre is not None:
                self.prenorm_chunk(pre[0], pre[1], c)

    def phase_C(self, l):
        S = self.S
        last = (l == 1)
        tiles = [(s, ti) for s in range(self.nseq) for ti in range(NTT)]
        specs = self.chain_wspecs(l)
        if not last:
            specs += self.win_wspecs(l + 1)
        self.wstream_begin(specs, len(tiles), "C%d" % l)
        YT = self.sc["YT%d" % l]
        ybuf = self.actt[16:32]

        def load_h(tok0):
            for k in range(KC):
                S.dma("sp", self.h[k][:, :], self.sc["HS"][k * 128:(k + 1) * 128, tok0:tok0 + TT], "hld",
                      writes=[self.h[k]])

        def load_y(tok0):
            for k in range(KC):
                S.dma("sp", ybuf[k][:, :], YT[k * 128:(k + 1) * 128, tok0:tok0 + TT], "yld", writes=[ybuf[k]])

        def load_p(tok0):
            S.dma("sp", self.pin.ap[:, :, :],
                  self.p[l, tok0:tok0 + TT, :].rearrange("(j p) c -> p j c", p=128), "pld", writes=[self.pin])

        toks = [s_ * T + ti * TT for (s_, ti) in tiles]
        load_y(toks[0])
        load_h(toks[0])
        load_p(toks[0])
        for tix, (s, ti) in enumerate(tiles):
            tok0 = toks[tix]
            nxt_tok = toks[tix + 1] if tix + 1 < len(tiles) else None
            for b in range(D // 256):
                wb, wv = self.wnext()
                for sub in range(2):
                    c = 2 * b + sub
                    pbk = self.next_mm()
                    self.proj_chunk(wb, wv, sub, pbk, xbufs=ybuf)
                    self.cp("act", self.sct[c][:, :], pbk[:, :], [pbk], [self.sct[c]])
                    self.stats_chunk(c, KC, pbk, pbk[:, :], "act")
            self.postnorm_residual(l, 1, pre=(l, 2))
            r2 = None
            for b in range(DFF // 256):
                wbg, wvg = self.wnext()
                wbu, wvu = self.wnext()
                for sub in range(2):
                    f = 2 * b + sub
                    pg = self.next_mm()
                    self.proj_chunk(wbg, wvg, sub, pg)
                    pu = self.next_mm()
                    self.proj_chunk(wbu, wvu, sub, pu)
                    if r2 is None:
                        r2 = self.stats_finish(D)
                    t, _ = self.nxt("tmp")
                    t2, _ = self.nxt("tmp")
                    self.tt("dve", t[:, :], pg[:, :], r2[:, :], ALU.mult, [pg, r2], [t])
                    self.act(t[:, :], t[:, :], AF.Silu, [t], [t])
                    self.tt("dve", t2[:, :], pu[:, :], r2[:, :], ALU.mult, [pu, r2], [t2])
                    self.tt("dve", self.actt[f][:, :], t[:, :], t2[:, :], ALU.mult, [t, t2], [self.actt[f]])
            for c in range(KC):
                pbk = self.next_mm()
                for hlf in range(2):
                    wb, wv = self.wnext()
                    self.proj_chunk(wb, wv, 0, pbk, xbufs=self.actt, nk=22, k0=22 * hlf,
                                    first=(hlf == 0), last=(hlf == 1))
                self.cp("act", self.sct[c][:, :], pbk[:, :], [pbk], [self.sct[c]])
                self.stats_chunk(c, KC, pbk, pbk[:, :], "act")
            if nxt_tok is not None:
                load_y(nxt_tok)
            self.postnorm_residual(l, 3, cast_xt=True)
            for kk in range(2):
                trb = self.next_tr()
                for j in range(4):
                    self.tr(trb[:, j * 128:(j + 1) * 128], self.pin.ap[:, j, kk * 128:(kk + 1) * 128], self.ident[:],
                            [self.pin, self.cb], [trb])
                self.cp("dve", self.pt[kk][:, :], trb[:, :], [trb], [self.pt[kk]])
            if nxt_tok is not None:
                load_p(nxt_tok)
            for hlf in range(2):
                wb, wv = self.wnext()
                for sub in range(8):
                    c = 8 * hlf + sub
                    pbk = self.next_mm()
                    self.proj_chunk(wb, wv, sub, pbk, xbufs=self.pt, nk=2)
                    self.cp("act", self.sct[c][:, :], pbk[:, :], [pbk], [self.sct[c]])
                    self.stats_chunk(c, KC, pbk, pbk[:, :], "act")
            r4 = self.stats_finish(D)
            for b in range(D // 256):
                wb, wv = self.wnext()
                for sub in range(2):
                    c = 2 * b + sub
                    pbk = self.next_mm()
                    self.proj_chunk(wb, wv, sub, pbk)
                    sg, _ = self.nxt("tmp")
                    self.act(sg[:, :], pbk[:, :], AF.Sigmoid, [pbk], [sg])
                    t1, _ = self.nxt("tmp")
                    self.tt("pool", t1[:, :], self.sct[c][:, :], r4[:, :], ALU.mult, [self.sct[c], r4], [t1])
                    self.stt(t1[:, :], t1[:, :], self.g(l, 4, c), sg[:, :], ALU.mult, ALU.mult,
                             [t1, sg, self.cb], [t1])
                    self.tt("dve", self.h[c][:, :], self.h[c][:, :], t1[:, :], ALU.add, [self.h[c], t1], [self.h[c]])
                    if not last:
                        self.stats_chunk(c, KC, self.h[c], self.h[c][:, :], "act")
            if last:
                for j in range(4):
                    (ob, alias) = self.ostg[j % 2]
                    for k4 in range(4):
                        trb = self.next_tr()
                        for kk in range(4):
                            k = 4 * k4 + kk
                            self.tr(trb[:, kk * 128:(kk + 1) * 128], self.h[k][:, j * 128:(j + 1) * 128], self.ident[:],
                                    [self.h[k], self.cb], [trb])
                        self.cp("act" if k4 % 2 == 0 else "dve", ob[:, k4 * 512:(k4 + 1) * 512], trb[:, :],
                                [trb], [ob] + alias)
                    S.dma("sp", self.y[tok0 + j * 128:tok0 + (j + 1) * 128, :], ob[:, :], "ost%d" % (j % 2),
                          reads=[ob] + alias)
                if nxt_tok is not None:
                    load_h(nxt_tok)
            else:
                for k in range(KC):
                    S.dma("sp", self.sc["HS"][k * 128:(k + 1) * 128, tok0:tok0 + TT], self.h[k][:, :], "hst",
                          reads=[self.h[k]])
                for c in range(KC):
                    self.act(self.xt[c][:, :], self.h[c][:, :], AF.Identity, [self.h[c], self.cb], [self.xt[c]],
                             scale=self.g(l + 1, 0, c))
                if nxt_tok is not None:
                    load_h(nxt_tok)
                self.win_stage(l + 1, tok0, ti * TT)
        S.barrier()


    def phase_B1(self):
        with contextlib.ExitStack() as es:
            self.mix_common(es)
            self.dilated_all(es)
        self.S.barrier()
        with contextlib.ExitStack() as es:
            self.mix_common(es)
            self.diff_all(es)
        self.S.barrier()

    def mix_common(self, es):
        sbt = lambda n, s, d: self.sb(es, n, s, d)
        pt = sbt("mx_pt", [128, 6, 512], BF16)
        self.mpt = [Buf(pt[:, i, :], "mpt%d" % i) for i in range(6)]
        self.mpt_i = 0
        t32 = sbt("mx_t32", [128, 4, 512], F32)
        self.m32 = [Buf(t32[:, i, :], "m32_%d" % i) for i in range(4)]
        self.m32_i = 0
        st = sbt("mx_stg", [128, 4, 512], BF16)
        self.mstg = [Buf(st[:, i, :], "mstg%d" % i) for i in range(4)]
        self.mstg_i = 0


    def pipe(self, n, A, B, C, L=3):
        ctx = {}
        for i in range(n + L):
            if i < n:
                c = A(i)
                ctx[i] = B(i, c)
            if i >= L:
                C(i - L, ctx.pop(i - L))
            if self.later:
                due = [f for (d, f) in self.later if d <= i]
                self.later = [(d, f) for (d, f) in self.later if d > i]
                for f in due:
                    f()
        for (d, f) in self.later:
            f()
        self.later = []

    def diff_all(self, es):
        S, nc = self.S, self.nc
        sbt = lambda n, s, d: self.sb(es, n, s, d)
        lam_init = 0.8 - 0.6 * math.exp(-0.3 * 1)
        lam = sbt("df_lam", [128, 256], F32)
        lamb = Buf(lam, "lam")
        sc8 = sbt("df_sc", [128, 8], F32)
        scb = Buf(sc8, "dfsc")
        S.dma("sp", lam[:, :], self.W["diff_lambda"].rearrange("a b -> (a b)").partition_broadcast(128), "c0",
              writes=[lamb])
        with nc.allow_non_contiguous_dma(reason="tiny per-partition gain load"):
            S.dma("sp", sc8[:, 4:5], self.W["diff_subln_g"].rearrange("a p -> p a"), "c0", writes=[scb])
        prod = sbt("df_prod", [128, 128], F32)
        pb_ = Buf(prod, "prod")
        self.tt("dve", prod[:, 0:64], lam[:, 0:64], lam[:, 64:128], ALU.mult, [lamb], [pb_])
        self.tt("dve", prod[:, 64:128], lam[:, 128:192], lam[:, 192:256], ALU.mult, [lamb], [pb_])
        self.S.op("dve", lambda: nc.vector.tensor_reduce(sc8[:, 0:1], prod[:, 0:64], mybir.AxisListType.X, ALU.add),
                  reads=[pb_], writes=[scb])
        self.S.op("dve", lambda: nc.vector.tensor_reduce(sc8[:, 1:2], prod[:, 64:128], mybir.AxisListType.X, ALU.add),
                  reads=[pb_], writes=[scb])
        self.act(sc8[:, 0:2], sc8[:, 0:2], AF.Exp, [scb], [scb])
        self.tt("dve", sc8[:, 2:3], sc8[:, 1:2], sc8[:, 0:1], ALU.subtract, [scb], [scb])
        self.ts("dve", sc8[:, 3:4], sc8[:, 2:3], -lam_init, ALU.add, [scb], [scb])
        self.ts("dve", sc8[:, 5:6], sc8[:, 4:5], 1.0 - lam_init, ALU.mult, [scb], [scb])
        neglam = sc8[:, 3:4]
        gsub = sc8[:, 5:6]
        vt = sbt("df_v", [128, 16, 1024], BF16)
        vb = Buf(vt, "dfv")
        qk = sbt("df_qk", [128, 2, 2, T], BF16)
        qkb = [(Buf(qk[:, i, 0, :], "dfq%d" % i), Buf(qk[:, i, 1, :], "dfk%d" % i)) for i in range(2)]
        kz = sbt("df_kz", [128, 2, 2, T], BF16)
        kzb = [[Buf(kz[:, i, c, :], "dfkz%d_%d" % (i, c)) for c in range(2)] for i in range(2)]
        for i in range(2):
            for c in range(2):
                self.memset("pool", kz[:, i, c, :], 0.0, [kzb[i][c]])
        YT = self.sc["YT1"]
        self.later = []
        dsq = sbt("df_sq", [128, 2, 512], BF16)
        self.dfsq = [Buf(dsq[:, i, :], "dfsq%d" % i) for i in range(2)]
        self.dfsq_i = 0
        for s in range(self.nseq):
            S.dma("sp", vt[:, :, :], self.sc["VD"][s * T:(s + 1) * T, :].rearrange("(k p) c -> p k c", p=128), "dfv",
                  writes=[vb])
            steps = [(h, qblk, c, kt) for h in range(8) for qblk in range(4) for c in range(2) for kt in range(16)]
            sbanks = self.pb[4:7]
            state = {"res": {}}

            def load_head(h):
                qb, kb = qkb[h % 2]
                S.dma("sp", qb[:, :], self.sc["QDT"][h * 128:(h + 1) * 128, s * T:(s + 1) * T], "dfq%d" % (h % 2),
                      writes=[qb])
                S.dma("sp", kb[:, :], self.sc["KDT"][h * 128:(h + 1) * 128, s * T:(s + 1) * T], "dfk%d" % (h % 2),
                      writes=[kb])
                self.cp("pool", kz[0:64, h % 2, 0, :], kb[0:64, :], [kb], [kzb[h % 2][0]])
                self.cp("pool", kz[64:128, h % 2, 1, :], kb[64:128, :], [kb], [kzb[h % 2][1]])

            load_head(0)

            def A(i):
                h, qblk, c, kt = steps[i]
                if qblk == 0 and c == 0 and kt == 0 and h + 1 < 8:
                    load_head(h + 1)
                qb, kb = qkb[h % 2]
                sb_ = sbanks[i % 3]
                q0 = qblk * 512
                kzc = kzb[h % 2][c]
                self.mm(sb_[:, :], kzc[:, kt * 128:(kt + 1) * 128], qb[:, q0:q0 + 512], True, True, [kzc, qb], [sb_])
                return sb_

            def B(i, sb_):
                p_, _ = self.nxt("mpt")
                self.act(p_[:, :], sb_[:, :], AF.Exp, [sb_], [p_])
                return p_

            def C(i, p_):
                h, qblk, c, kt = steps[i]
                ob, zb = self.pb[c], self.pb[2 + c]
                self.mm(ob[:, :], vt[:, kt, h * 128:(h + 1) * 128], p_[:, :], kt == 0, kt == 15, [vb, p_], [ob])
                self.mm(zb[:, :], self.ones_bf[:], p_[:, :], kt == 0, kt == 15, [self.cb, p_], [zb])
                if kt == 15:
                    r, _ = self.nxt("m32")
                    self.recip(r[:, :], zb[:, :], [zb], [r])
                    self.tt("dve", r[:, :], ob[:, :], r[:, :], ALU.mult, [ob, r], [r])
                    state["res"][c] = r
                    if c == 1:
                        r0, r1 = state["res"][0], state["res"][1]
                        q0 = qblk * 512
                        self.stt(r0[:, :], r1[:, :], neglam, r0[:, :], ALU.mult, ALU.add, [r0, r1, scb], [r0])
                        sq, _ = self.nxt("dfsq")
                        self.tt("pool", sq[:, :], r0[:, :], r0[:, :], ALU.mult, [r0], [sq])

                        def fin(r0=r0, r1=r1, sq=sq, h=h, q0=q0):
                            ssb = self.x_bank_df
                            self.mm(ssb[:, :], self.ones_bf[:], sq[:, :], True, True, [self.cb, sq], [ssb])
                            self.act(r1[:, :], ssb[:, :], AF.Sqrt, [ssb, self.cb], [r1], scale=1.0 / 128,
                                     bias=self.eps_t[:, 0:1])
                            self.recip(r1[:, :], r1[:, :], [r1], [r1])
                            st, sti = self.nxt("mstg")
                            self.stt(st[:, :], r0[:, :], gsub, r1[:, :], ALU.mult, ALU.mult, [r0, r1, scb], [st])
                            S.dma("sp", YT[1024 + h * 128:1024 + (h + 1) * 128, s * T + q0:s * T + q0 + 512], st[:, :],
                                  "mstg%d" % sti, reads=[st])
                        self.later.append((i + 3 + 6, fin))

            self.x_bank_df = self.pb[7]
            self.pipe(len(steps), A, B, C, L=3)

    def dilated_all(self, es):
        S, nc = self.S, self.nc
        sbt = lambda n, s, d: self.sb(es, n, s, d)
        mk32 = sbt("dl_mk32", [128, 256], F32)
        mk = sbt("dl_mk", [128, 256], BF16)
        mkb = Buf(mk, "dlmask")
        S.dma("sp", mk32[:, :], self.C["dil_mask"][:, :], "c0", writes=[mkb])
        self.ts("dve", mk[:, :], mk32[:, :], -1.0, ALU.add, [mkb], [mkb], s2=-NEG, op1=ALU.mult)
        vts = [sbt("dl_v%d" % g, [128, 16, 1024], BF16) for g in range(3)]
        vbs = [Buf(vts[g], "dlv%d" % g) for g in range(3)]
        kt_ = sbt("dl_k", [128, 2, T], BF16)
        kbuf = [Buf(kt_[:, i, :], "dlk%d" % i) for i in range(2)]
        kp_ = sbt("dl_kp", [128, 2, 2, T], BF16)
        kpb = [[Buf(kp_[:, j, i, :], "dlkp%d_%d" % (j, i)) for i in range(2)] for j in range(2)]
        qt_ = sbt("dl_q", [128, 2, 3, T], BF16)
        qbuf = [[Buf(qt_[:, j, i, :], "dlq%d_%d" % (j, i)) for i in range(3)] for j in range(2)]
        qp_ = sbt("dl_qp", [128, 2, 2, T], BF16)
        qpb = [[Buf(qp_[:, j, i, :], "dlqp%d_%d" % (j, i)) for i in range(2)] for j in range(2)]
        acc = sbt("dl_acc", [128, 2, T], F32)
        acco, accz = Buf(acc[:, 0, :], "acco"), Buf(acc[:, 1, :], "accz")
        yst = sbt("dl_y", [128, T], BF16)
        ystb = Buf(yst, "dly")
        YT = self.sc["YT1"]
        DIL = (1, 4, 16)
        self.later = []
        for s in range(self.nseq):
            vsrc = self.sc["VC"][s * T:(s + 1) * T, :]
            S.dma("sp", vts[0][:, :, :], vsrc.rearrange("(k p) c -> p k c", p=128), "dlv0", writes=[vbs[0]])
            for b in range(4):
                S.dma("sp", vts[1][:, :, :].rearrange("p (r b) c -> p r b c", r=4)[:, :, b, :],
                      vsrc.rearrange("(b p r) c -> p r b c", p=128, r=4)[:, :, b, :], "dlv1", writes=[vbs[1]])
            S.dma("sp", vts[2][:, :, :], vsrc.rearrange("(p r) c -> p r c", r=16), "dlv2", writes=[vbs[2]])

            def prep_head(h):
                j = h % 2
                kb = kbuf[j]
                S.dma("sp", kb[:, :], self.sc["KCT"][h * 128:(h + 1) * 128, s * T:(s + 1) * T], "dlk%d" % j, writes=[kb])
                for g in range(3):
                    S.dma("sp", qbuf[j][g][:, :],
                          self.sc["QCT"][g * 1024 + h * 128:g * 1024 + (h + 1) * 128, s * T:(s + 1) * T],
                          "dlq%d_%d" % (j, g), writes=[qbuf[j][g]])
                for gi, d in ((0, 4), (1, 16)):
                    self.cp("pool", kpb[j][gi][:, :].rearrange("p (r i) -> p r i", r=d),
                            kb[:, :].rearrange("p (i r) -> p r i", r=d), [kb], [kpb[j][gi]])
                    self.cp("act", qpb[j][gi][:, :].rearrange("p (r i) -> p r i", r=d),
                            qbuf[j][gi + 1][:, :].rearrange("p (i r) -> p r i", r=d), [qbuf[j][gi + 1]], [qpb[j][gi]])

            steps = []
            blk = 0
            for h in range(8):
                for g in range(3):
                    d = DIL[g]
                    n = T // d
                    for rho in range(d):
                        for Q0 in range(0, n, 512):
                            Q1 = min(Q0 + 512, n)
                            b_lo = max(0, (Q0 - 64) // 128)
                            b_hi = min(n // 128 - 1, (Q1 - 1 + 64) // 128)
                            bl = []
                            for b in range(b_lo, b_hi + 1):
                                qs = max(Q0, 128 * b - 64)
                                qe = min(Q1, 128 * b + 192)
                                if qe - qs > 0:
                                    bl.append((b, qs, qe))
                            for bi, (b, qs, qe) in enumerate(bl):
                                steps.append(dict(h=h, g=g, d=d, n=n, rho=rho, Q0=Q0, Q1=Q1, b=b, qs=qs, qe=qe,
                                                  first=(bi == 0), last=(bi == len(bl) - 1), blk=blk,
                                                  hfirst=(g == 0 and Q0 == 0 and bi == 0),
                                                  hlast=(g == 2 and rho == d - 1 and bi == len(bl) - 1)))
                            blk += 1
            prep_head(0)
            sbanks = self.pb[4:8]

            def A(i):
                st = steps[i]
                h, g, n, rho, b = st["h"], st["g"], st["n"], st["rho"], st["b"]
                if st["hfirst"] and h + 1 < 8:
                    prep_head(h + 1)
                j = h % 2
                ksrc = kbuf[j] if g == 0 else kpb[j][g - 1]
                qsrc = qbuf[j][0] if g == 0 else qpb[j][g - 1]
                N = st["qe"] - st["qs"]
                sbk = sbanks[i % 4]
                kbase = rho * n + 128 * b
                qoff = st["qs"] - (128 * b - 64)
                self.mm(sbk[:, 0:N], self.ident_bf[:], mk[:, qoff:qoff + N], True, False, [self.cb, mkb], [sbk])
                self.mm(sbk[:, 0:N], ksrc[:, kbase:kbase + 128], qsrc[:, rho * n + st["qs"]:rho * n + st["qe"]],
                        False, True, [ksrc, qsrc], [sbk])
                return sbk

            def B(i, sbk):
                st = steps[i]
                N = st["qe"] - st["qs"]
                qoff = st["qs"] - (128 * st["b"] - 64)
                p_, _ = self.nxt("mpt")
                self.act(p_[:, 0:N], sbk[:, 0:N], AF.Exp, [sbk], [p_])
                return p_

            def C(i, p_):
                st = steps[i]
                h, g, d, n, rho, b, Q0, Q1 = st["h"], st["g"], st["d"], st["n"], st["rho"], st["b"], st["Q0"], st["Q1"]
                N = st["qe"] - st["qs"]
                ob = self.pb[st["blk"] % 2]
                zb = self.pb[2 + st["blk"] % 2]
                tile_i = b if g == 0 else (rho * 4 + b if g == 1 else rho)
                self.mm(ob[:, st["qs"] - Q0:st["qe"] - Q0], vts[g][:, tile_i, h * 128:(h + 1) * 128], p_[:, 0:N],
                        st["first"], False, [vbs[g], p_], [ob], skip_group_check=True)
                self.mm(zb[:, st["qs"] - Q0:st["qe"] - Q0], self.ones_bf[:], p_[:, 0:N],
                        st["first"], False, [self.cb, p_], [zb], skip_group_check=True)
                if st["last"]:
                    L_ = Q1 - Q0
                    ov = acco[:, :].rearrange("p (i r) -> p r i", r=d)[:, rho, Q0:Q1]
                    zv = accz[:, :].rearrange("p (i r) -> p r i", r=d)[:, rho, Q0:Q1]
                    if g == 0:
                        self.cp("act", ov, ob[:, 0:L_], [ob], [acco])
                        self.cp("dve", zv, zb[:, 0:L_], [zb], [accz])
                    else:
                        self.tt("dve", ov, ov, ob[:, 0:L_], ALU.add, [ob, acco], [acco])
                        self.tt("dve", zv, zv, zb[:, 0:L_], ALU.add, [zb, accz], [accz])
                if st["hlast"]:
                    self.recip(accz[:, :], accz[:, :], [accz], [accz])
                    self.tt("dve", yst[:, :], acco[:, :], accz[:, :], ALU.mult, [acco, accz], [ystb])
                    S.dma("sp", YT[h * 128:(h + 1) * 128, s * T:(s + 1) * T], yst[:, :], "dly", reads=[ystb])

            self.pipe(len(steps), A, B, C, L=3)

    def phase_B0(self):
        with contextlib.ExitStack() as es:
            self.mix_common(es)
            self.na_tables(es)
        self.S.barrier()
        with contextlib.ExitStack() as es:
            self.mix_common(es)
            self.na_all(es)
        self.S.barrier()
        with contextlib.ExitStack() as es:
            self.gla_all(es)
        self.S.barrier()

    @staticmethod
    def na_row(r):
        r0 = min(max(r - 4, 0), 24)
        return r0, r0 - r + 7

    def na_tables(self, es):
        S, nc = self.S, self.nc
        sbt = lambda n, s, d: self.sb(es, n, s, d)
        self.MB = self.dscr("MB", [8, 5, 128, 512], BF16)
        R1D = self.dscr("R1D", [120, 64, 64], F32)
        rp = sbt("na_rp", [128, 128], F32)
        rpb_ = Buf(rp, "rp")
        self.memset("dve", rp[:, :], 0.0, [rpb_])
        S.dma("sp", rp[0:120, 0:31], self.W["na_rpb"][:, :], "c0", writes=[rpb_])
        trb = self.next_tr()
        self.tr(trb[:, 0:128], rp[:, :], self.ident[:], [rpb_, self.cb], [trb])
        rpT = sbt("na_rpT", [32, 128], F32)
        rpTb = Buf(rpT, "rpT")
        self.cp("dve", rpT[0:32, 0:128], trb[0:32, 0:128], [trb], [rpTb])
        toe = sbt("na_toe", [32, 4096], F32)
        toeb = Buf(toe, "toe")
        self.memset("dve", toe[:, :], 0.0, [toeb])
        S.dma("sp", toe[0:31, :], self.C["na_toe"][:, :], "c0", writes=[toeb])
        r1 = sbt("na_r1", [128, 4096], F32)
        r1b = Buf(r1, "r1")
        for c in range(8):
            pbk = self.next_mm()
            self.mm(pbk[:, :], rpT[0:32, 0:128], toe[0:32, c * 512:(c + 1) * 512], True, True, [rpTb, toeb], [pbk])
            self.cp("act" if c % 2 else "dve", r1[0:120, c * 512:(c + 1) * 512], pbk[0:120, :], [pbk], [r1b])
        S.dma("sp", R1D.rearrange("a b c -> a (b c)"), r1[0:120, :], "na_r1", reads=[r1b])
        S.barrier()
        cm = sbt("na_cm", [128, 64], F32)
        cmb = Buf(cm, "cm")
        S.dma("sp", cm[:, :], self.C["na_colmask"][:, :], "c0", writes=[cmb])
        NTB = 8
        tb = sbt("na_tb", [128, NTB, 512], F32)
        tbb = [[Buf(tb[64 * (q % 2):64 * (q % 2) + 64, i, (q // 2) * 256:(q // 2 + 1) * 256], "natb%d_%d" % (i, q))
                for q in range(4)] for i in range(NTB)]
        tbo_ = sbt("na_tbo", [128, NTB, 512], BF16)
        tbo = [Buf(tbo_[:, i, :], "natbo%d" % i) for i in range(NTB)]
        PAIRS = ((0, 1), (2, 3), (4, 5), (28, 29), (30, 31))
        cnt = 0
        for h in range(8):
            for pc, (ra, rb) in enumerate(PAIRS):
                tq = tbb[cnt % NTB]
                t = tb[:, cnt % NTB, :]
                cnt += 1
                for slot, r in enumerate((ra, rb)):
                    r0, dr0 = self.na_row(r)
                    for par in range(2):
                        row0 = h * 15 + dr0 + par
                        src = R1D[row0:row0 + 7:2, :, :].rearrange("t k c -> k t c")
                        qb_ = tq[2 * slot + par]
                        dst = qb_[:, :].rearrange("k (t c) -> k t c", c=64)
                        S.dma(("sp", "pool")[(slot + par) % 2], dst, src, "natbq%d" % (2 * slot + par), writes=[qb_])
                to = tbo[cnt % NTB]
                self.tt("dve", to[:, :].rearrange("p (a c) -> p a c", c=64), t.rearrange("p (a c) -> p a c", c=64),
                        cm[:, :].unsqueeze(1).broadcast_to([128, 8, 64]), ALU.add, list(tq) + [cmb], [to])
                S.dma("sp", self.MB[h, pc, :, :], to[:, :], "natbo", reads=[to])

    def na_all(self, es):
        S, nc = self.S, self.nc
        sbt = lambda n, s, d: self.sb(es, n, s, d)
        v0 = sbt("na_v0", [128, 16, 1024], BF16)
        v1 = sbt("na_v1", [128, 15, 1024], BF16)
        v0b, v1b = Buf(v0, "nav0"), Buf(v1, "nav1")
        qk = sbt("na_qk", [128, 2, 2, T], BF16)
        qkb = [(Buf(qk[:, i, 0, :], "naq%d" % i), Buf(qk[:, i, 1, :], "nak%d" % i)) for i in range(2)]
        mb = sbt("na_mb", [128, 2, 5, 512], BF16)
        mbb = [Buf(mb[:, i], "namb%d" % i) for i in range(2)]
        YT = self.sc["YT0"]
        PCLS = {0: 0, 2: 1, 28: 3, 30: 4}
        self.later = []
        for s in range(self.nseq):
            vsrc = self.sc["VB"][s * T:(s + 1) * T, :]
            S.dma("sp", v0[:, :, :], vsrc.rearrange("(k p) c -> p k c", p=128), "nav0", writes=[v0b])
            S.dma("sp", v1[:, :, :], vsrc[64:64 + 15 * 128, :].rearrange("(k p) c -> p k c", p=128), "nav1", writes=[v1b])

            def load_head(h):
                qb, kb = qkb[h % 2]
                mt = mbb[h % 2]
                S.dma("sp", qb[:, :], self.sc["QBT"][h * 128:(h + 1) * 128, s * T:(s + 1) * T], "naq%d" % (h % 2), writes=[qb])
                S.dma("sp", kb[:, :], self.sc["KBT"][h * 128:(h + 1) * 128, s * T:(s + 1) * T], "nak%d" % (h % 2), writes=[kb])
                S.dma("sp", mt[:, :, :], self.MB[h].rearrange("a p c -> p a c"), "namb%d" % (h % 2), writes=[mt])

            steps = [(h, blk, pr) for h in range(8) for blk in range(4) for pr in range(4)]
            load_head(0)
            sbanks = self.pb[4:8]

            def A(i):
                h, blk, pr = steps[i]
                if blk == 0 and pr == 0 and h + 1 < 8:
                    load_head(h + 1)
                qb, kb = qkb[h % 2]
                ra = blk * 8 + pr * 2
                sbk = sbanks[i % 4]
                mt = mbb[h % 2]
                pc = PCLS.get(ra, 2)
                self.mm(sbk[:, :], self.ident_bf[:], mt[:, pc, :], True, False, [self.cb, mt], [sbk])
                for slot in range(2):
                    r = ra + slot
                    r0, _ = self.na_row(r)
                    for tp in range(4):
                        k0 = 64 * r0 + 128 * tp
                        self.mm(sbk[:, slot * 256 + tp * 64:slot * 256 + (tp + 1) * 64], kb[:, k0:k0 + 128],
                                qb[:, r * 64:(r + 1) * 64], False, slot == 1 and tp == 3, [kb, qb], [sbk],
                                skip_group_check=True)
                return sbk

            def B(i, sbk):
                h, blk, pr = steps[i]
                mt = mbb[h % 2]
                pc = PCLS.get(blk * 8 + pr * 2, 2)
                p_, _ = self.nxt("mpt")
                self.act(p_[:, :], sbk[:, :], AF.Exp, [sbk], [p_])
                return p_

            def C(i, p_):
                h, blk, pr = steps[i]
                ra = blk * 8 + pr * 2
                ob = self.pb[(h * 4 + blk) % 2]
                zb = self.pb[2 + (h * 4 + blk) % 2]
                for slot in range(2):
                    r = ra + slot
                    r0, _ = self.na_row(r)
                    c0 = (r - blk * 8) * 64
                    for tp in range(4):
                        k0 = 64 * r0 + 128 * tp
                        if k0 % 128 == 0:
                            vl, vbuf = v0[:, k0 // 128, h * 128:(h + 1) * 128], v0b
                        else:
                            vl, vbuf = v1[:, (k0 - 64) // 128, h * 128:(h + 1) * 128], v1b
                        pr_ = p_[:, slot * 256 + tp * 64:slot * 256 + (tp + 1) * 64]
                        self.mm(ob[:, c0:c0 + 64], vl, pr_, tp == 0, tp == 3, [vbuf, p_], [ob], skip_group_check=True)
                        self.mm(zb[:, c0:c0 + 64], self.ones_bf[:], pr_, tp == 0, tp == 3, [self.cb, p_], [zb],
                                skip_group_check=True)
                if pr == 3:
                    rz, _ = self.nxt("m32")
                    self.recip(rz[:, :], zb[:, :], [zb], [rz])
                    st, sti = self.nxt("mstg")
                    self.tt("dve", st[:, :], ob[:, :], rz[:, :], ALU.mult, [ob, rz], [st])
                    S.dma("sp", YT[1024 + h * 128:1024 + (h + 1) * 128, s * T + blk * 512:s * T + (blk + 1) * 512], st[:, :],
                          "mstg%d" % sti, reads=[st])

            self.pipe(len(steps), A, B, C, L=3)

    def nb(self):
        b = self.pb[self.nb_i % 8]
        self.nb_i += 1
        return b

    def gla_all(self, es):
        S, nc = self.S, self.nc
        sbt = lambda n, s, d: self.sb(es, n, s, d)
        self.nb_i = 0
        NCH = T // 128
        gm = sbt("gl_m", [128, 6, 128], F32)
        gmb = Buf(gm, "glm")
        S.dma("sp", gm[:, :, :], self.C["gla_m"].rearrange("a p c -> p a c"), "c0", writes=[gmb])
        M1, M2, M3, M4, MF, MB_ = [gm[:, i, :] for i in range(6)]
        wg = sbt("gl_wg", [32, 2, 512], F32)
        wgb = Buf(wg, "glwg")
        self.memset("dve", wg[:, :, :], 0.0, [wgb])
        for i, (wn, bn) in enumerate((("gla_wg_fwd", "gla_bg_fwd"), ("gla_wg_bwd", "gla_bg_bwd"))):
            S.dma("sp", wg[0:16, i, :], self.W[wn][:, :], "c0", writes=[wgb])
            S.dma("sp", wg[16:17, i, :], self.W[bn][:, :], "c0", writes=[wgb])
        lr = sbt("gl_lr", [32, 2, T], F32)
        lrb = Buf(lr, "gllr")
        self.memset("dve", lr[:, :, :], 0.0, [lrb])
        for i in range(2):
            S.dma("sp", lr[16:17, i, :], self.C["ones_row"][:, :], "c0", writes=[lrb])
        gn = sbt("gl_gn", [128, 256], F32)
        gnb = Buf(gn, "glgn")
        S.dma("sp", gn[:, :], self.W["gla_norm_g"].rearrange("a b -> (a b)").partition_broadcast(128), "c0", writes=[gnb])
        one_t = sbt("gl_one", [128, 1], F32)
        oneb = Buf(one_t, "glone")
        self.memset("dve", one_t[:, :], 1.0, [oneb])
        qa = sbt("gl_q", [128, 4, T], BF16)
        ka = sbt("gl_k", [128, 4, T], BF16)
        kt = sbt("gl_kt", [128, NCH, 512], BF16)
        va = sbt("gl_v", [128, NCH, 1024], BF16)
        qab, kab, ktb, vab = Buf(qa, "glq"), Buf(ka, "glk"), Buf(kt, "glkt"), Buf(va, "glv")
        ra = sbt("gl_ra", [128, 2, 1024], BF16)
        rab = [Buf(ra[:, i, :], "glra%d" % i) for i in range(2)]
        sbst = sbt("gl_sb", [128, NCH, 4, 256], BF16)
        sbb = [Buf(sbst[:, n], "glsb%d" % n) for n in range(NCH)]
        s32 = sbt("gl_s32", [128, 4, 256], F32)
        s32b = [Buf(s32[:, h, :], "gls32_%d" % h) for h in range(4)]
        sf = sbt("gl_sf", [128, 2, 4, 256], BF16)
        sfb = [[Buf(sf[:, i, h, :], "glsf%d_%d" % (i, h)) for h in range(4)] for i in range(2)]
        f32t = sbt("gl_f32", [128, 6, 512], F32)
        f32b = [Buf(f32t[:, i, :], "glf%d" % i) for i in range(6)]
        self.glf = f32b
        self.glf_i = 0
        g32t = sbt("gl_g32", [128, 6, 512], F32)
        self.glg = [Buf(g32t[:, i, :], "glg%d" % i) for i in range(6)]
        self.glg_i = 0
        b16t = sbt("gl_b16", [128, 16, 512], BF16)
        b16b = [Buf(b16t[:, i, :], "glh%d" % i) for i in range(16)]
        self.glh = b16b
        self.glh_i = 0
        sm = sbt("gl_sm", [128, 4, 8], F32)
        smb = [Buf(sm[:, i, :], "glsm%d" % i) for i in range(4)]
        self.glsm = smb
        self.glsm_i = 0
        ya2 = sbt("gl_ya", [128, 2, 1024], F32)
        yab2 = [Buf(ya2[:, i, :], "glya%d" % i) for i in range(2)]
        yst = sbt("gl_yst", [128, 1, 8, 512], BF16)
        ystb = [Buf(yst[:, 0], "glyst0")] * 2
        YT = self.sc["YT0"]

        def gate(n, di):
            pb = self.nb()
            self.mm(pb[:, :], lr[0:32, di, n * 128:(n + 1) * 128], wg[0:32, di, :], True, True, [lrb, wgb], [pb])
            e, _ = self.nxt("glg")
            self.act(e[:, :], pb[:, :], AF.Exp, [pb], [e], scale=-1.0)
            self.act(e[:, :], e[:, :], AF.Ln, [e, oneb], [e], bias=one_t[:, 0:1])
            return e

        for s in range(self.nseq):
            c0, c1 = s * T, (s + 1) * T
            S.dma("sp", qa[:, :, :], self.sc["QAT"][:, c0:c1].rearrange("(h p) t -> p h t", p=128), "glq", writes=[qab])
            S.dma("sp", ka[:, :, :], self.sc["KAT"][:, c0:c1].rearrange("(h p) t -> p h t", p=128), "glk", writes=[kab])
            S.dma("sp", kt[:, :, :], self.sc["KA"][c0:c1, :].rearrange("(c p) n -> p c n", p=128), "glkt", writes=[ktb])
            S.dma("sp", va[:, :, :], self.sc["VA"][c0:c1, :].rearrange("(c p) n -> p c n", p=128), "glv", writes=[vab])
            for i in range(2):
                S.dma("sp", lr[0:16, i, :], self.sc["LRT"][16 * i:16 * i + 16, c0:c1], "gllr", writes=[lrb])
            for h in range(4):
                self.memset("pool", s32[:, h, :], 0.0, [s32b[h]])
            self.memset("pool", sbst[:, NCH - 1], 0.0, [sbb[NCH - 1]])
            gnext = gate(NCH - 1, 1)
            for n in range(NCH - 1, 0, -1):
                Gb = gnext
                if n - 1 > 0:
                    gnext = gate(n - 1, 1)
                pe_ = self.nb()
                self.mm(pe_[:, :], M4, Gb[:, :], True, True, [gmb, Gb], [pe_])
                pd = self.nb()
                for h in range(4):
                    self.mm(pd[:, 2 * h:2 * h + 2], Gb[:, h * 128:(h + 1) * 128], M2[:, 0:2], True, True, [Gb, gmb], [pd])
                Ee, _ = self.nxt("glf")
                self.act(Ee[:, :], pe_[:, :], AF.Exp, [pe_], [Ee])
                dec, _ = self.nxt("glsm")
                self.act(dec[:, 0:8], pd[:, 0:8], AF.Exp, [pd], [dec])
                ku, _ = self.nxt("glh")
                self.tt("dve", ku[:, :], kt[:, n, :], Ee[:, :], ALU.mult, [ktb, Ee], [ku])
                for hp in range(2):
                    pu = self.nb()
                    for hh in range(2):
                        h = 2 * hp + hh
                        self.mm(pu[:, hh * 256:(hh + 1) * 256], ku[:, h * 128:(h + 1) * 128],
                                va[:, n, h * 256:(h + 1) * 256], True, True, [ku, vab], [pu])
                    for hh in range(2):
                        h = 2 * hp + hh
                        self.stt(s32[:, h, :], s32[:, h, :], dec[:, 2 * h:2 * h + 1], pu[:, hh * 256:(hh + 1) * 256],
                                 ALU.mult, ALU.add, [s32b[h], dec, pu], [s32b[h]])
                        self.cp("pool", sbst[:, n - 1, h, :], s32[:, h, :], [s32b[h]], [sbb[n - 1]])
            for h in range(4):
                self.memset("pool", s32[:, h, :], 0.0, [s32b[h]])
                self.memset("pool", sf[:, 0, h, :], 0.0, [sfb[0][h]])
            v3 = lambda ap: ap.rearrange("p (h t) -> p h t", h=4)
            v2 = lambda ap: ap.rearrange("p (h t) -> p h t", h=2)

            def stage1a(n):
                return gate(n, 0), gate(n, 1)

            def stage1(n, gg):
                rb = rab[n % 2]
                S.dma("sp", rb[:, :], self.sc["RA"][c0 + n * 128:c0 + (n + 1) * 128, :], "glra%d" % (n % 2), writes=[rb])
                Gf, Gb = gg
                pcf, pcb, pef = self.nb(), self.nb(), self.nb()
                for h in range(4):
                    self.mm(pcf[:, h * 128:(h + 1) * 128], Gf[:, h * 128:(h + 1) * 128], M1, True, True, [Gf, gmb], [pcf])
                for h in range(4):
                    self.mm(pcb[:, h * 128:(h + 1) * 128], Gb[:, h * 128:(h + 1) * 128], M2, True, True, [Gb, gmb], [pcb])
                self.mm(pef[:, :], M3, Gf[:, :], True, True, [gmb, Gf], [pef])
                Ecf, _ = self.nxt("glf")
                Encf, _ = self.nxt("glf")
                Ecb, _ = self.nxt("glf")
                Encb, _ = self.nxt("glf")
                Eef, _ = self.nxt("glf")
                self.act(Ecf[:, :], pcf[:, :], AF.Exp, [pcf], [Ecf])
                self.act(Encf[:, :], pcf[:, :], AF.Exp, [pcf], [Encf], scale=-1.0)
                self.act(Ecb[:, :], pcb[:, :], AF.Exp, [pcb], [Ecb])
                self.act(Encb[:, :], pcb[:, :], AF.Exp, [pcb], [Encb], scale=-1.0)
                self.act(Eef[:, :], pef[:, :], AF.Exp, [pef], [Eef])
                dec, _ = self.nxt("glsm")
                self.cp("pool", dec[:, 0:4], Ecf[:, :].rearrange("p (h t) -> p h t", h=4)[:, :, 127], [Ecf], [dec])
                qn = qa[:, :, n * 128:(n + 1) * 128]
                kn = ka[:, :, n * 128:(n + 1) * 128]
                qdf, _ = self.nxt("glh")
                kdf, _ = self.nxt("glh")
                qdb, _ = self.nxt("glh")
                kdb, _ = self.nxt("glh")
                kuf, _ = self.nxt("glh")
                self.tt("dve", v3(qdf[:, :]), qn, v3(Ecf[:, :]), ALU.mult, [qab, Ecf], [qdf])
                self.tt("pool", v3(kdf[:, :]), kn, v3(Encf[:, :]), ALU.mult, [kab, Encf], [kdf])
                self.tt("dve", v3(qdb[:, :]), qn, v3(Ecb[:, :]), ALU.mult, [qab, Ecb], [qdb])
                self.tt("pool", v3(kdb[:, :]), kn, v3(Encb[:, :]), ALU.mult, [kab, Encb], [kdb])
                self.tt("dve", kuf[:, :], kt[:, n, :], Eef[:, :], ALU.mult, [ktb, Eef], [kuf])
                return dict(rb=rb, dec=dec, qdf=qdf, kdf=kdf, qdb=qdb, kdb=kdb, kuf=kuf)

            def stage2(n, c):
                rb, dec, qdf, kdf, qdb, kdb, kuf = c["rb"], c["dec"], c["qdf"], c["kdf"], c["qdb"], c["kdb"], c["kuf"]
                pA, pB = self.nb(), self.nb()
                for h in range(4):
                    hs = slice(h * 128, (h + 1) * 128)
                    self.mm(pA[:, hs], kdf[:, hs], qdf[:, hs], True, True, [kdf, qdf], [pA])
                for h in range(4):
                    hs = slice(h * 128, (h + 1) * 128)
                    self.mm(pB[:, hs], kdb[:, hs], qdb[:, hs], True, True, [kdb, qdb], [pB])
                atf, _ = self.nxt("glh")
                atb, _ = self.nxt("glh")
                self.tt("dve", v3(atf[:, :]), v3(pA[:, :]), MF.unsqueeze(1).broadcast_to([128, 4, 128]), ALU.mult,
                        [pA, gmb], [atf])
                self.tt("dve", v3(atb[:, :]), v3(pB[:, :]), MB_.unsqueeze(1).broadcast_to([128, 4, 128]), ALU.mult,
                        [pB, gmb], [atb])
                sfc = sfb[n % 2]
                sfn = sfb[(n + 1) % 2]
                po = [self.nb(), self.nb()]
                for h in range(4):
                    hs = slice(h * 128, (h + 1) * 128)
                    o_ = po[h // 2][:, (h % 2) * 256:(h % 2 + 1) * 256]
                    vv = va[:, n, h * 256:(h + 1) * 256]
                    self.mm(o_, atf[:, hs], vv, True, False, [atf, vab], [po[h // 2]])
                    self.mm(o_, atb[:, hs], vv, False, False, [atb, vab], [po[h // 2]])
                    self.mm(o_, qdf[:, hs], sf[:, n % 2, h, :], False, False, [qdf, sfc[h]], [po[h // 2]])
                    self.mm(o_, qdb[:, hs], sbst[:, n, h, :], False, True, [qdb, sbb[n]], [po[h // 2]])
                if n < NCH - 1:
                    for hp in range(2):
                        pu = self.nb()
                        for hh in range(2):
                            h = 2 * hp + hh
                            self.mm(pu[:, hh * 256:(hh + 1) * 256], kuf[:, h * 128:(h + 1) * 128],
                                    va[:, n, h * 256:(h + 1) * 256], True, True, [kuf, vab], [pu])
                        for hh in range(2):
                            h = 2 * hp + hh
                            self.stt(s32[:, h, :], s32[:, h, :], dec[:, h:h + 1],
                                     pu[:, hh * 256:(hh + 1) * 256], ALU.mult, ALU.add, [s32b[h], dec, pu], [s32b[h]])
                            self.cp("pool", sf[:, (n + 1) % 2, h, :], s32[:, h, :], [s32b[h]], [sfn[h]])
                ss, _ = self.nxt("glsm")
                junk, _ = self.nxt("glf")
                for h in range(4):
                    o_ = po[h // 2][:, (h % 2) * 256:(h % 2 + 1) * 256]
                    self.S.op("act", (lambda o_=o_, h=h, junk=junk, ss=ss: nc.scalar.activation(
                        junk[:, 0:256], o_, AF.Square, accum_out=ss[:, h:h + 1])),
                        reads=[po[h // 2]], writes=[junk, ss])
                self.act(ss[:, 4:8], ss[:, 0:4], AF.Sqrt, [ss, self.cb], [ss], scale=1.0 / 256, bias=self.eps_t[:, 0:1])
                self.recip(ss[:, 4:8], ss[:, 4:8], [ss], [ss])
                sr, _ = self.nxt("glf")
                sr2, _ = self.nxt("glf")
                self.act(sr[:, :], rb[:, 0:512], AF.Silu, [rb], [sr])
                self.act(sr2[:, :], rb[:, 512:1024], AF.Silu, [rb], [sr2])
                g2 = gn[:, :].unsqueeze(1).broadcast_to([128, 2, 256])
                self.tt("pool", v2(sr[:, :]), v2(sr[:, :]), g2, ALU.mult, [sr, gnb], [sr])
                self.tt("pool", v2(sr2[:, :]), v2(sr2[:, :]), g2, ALU.mult, [sr2, gnb], [sr2])
                ya = ya2[:, n % 2, :]
                yab = yab2[n % 2]
                for h in range(4):
                    o_ = po[h // 2][:, (h % 2) * 256:(h % 2 + 1) * 256]
                    gsrc = sr if h < 2 else sr2
                    self.stt(ya[:, h * 256:(h + 1) * 256], o_, ss[:, 4 + h:5 + h], gsrc[:, (h % 2) * 256:(h % 2 + 1) * 256],
                             ALU.mult, ALU.mult, [po[h // 2], ss, gsrc], [yab])

                def tail(n=n, ya=ya, yab=yab):
                    ysb = ystb[(n // 4) % 2]
                    ysv = yst[:, 0]
                    for half in range(2):
                        ptb = self.nb()
                        for f in range(4):
                            ft = half * 4 + f
                            self.tr(ptb[:, f * 128:(f + 1) * 128], ya[:, ft * 128:(ft + 1) * 128], self.ident[:],
                                    [yab, self.cb], [ptb])
                        self.cp("act" if half == 0 else "dve",
                                ysv[:, half * 4:half * 4 + 4, (n % 4) * 128:(n % 4 + 1) * 128],
                                ptb[:, :].rearrange("p (f t) -> p f t", f=4), [ptb], [ysb])
                    if n % 4 == 3:
                        t0 = c0 + (n // 4) * 512
                        S.dma("sp", YT[0:1024, t0:t0 + 512].rearrange("(f p) t -> p f t", p=128), ysv[:, :, :],
                              "glyst0", reads=[ysb])
                return tail

            gates = {0: stage1a(0), 1: stage1a(1)}
            ctxs = {0: stage1(0, gates.pop(0))}
            tails = {}
            for n in range(NCH):
                if n + 2 < NCH:
                    gates[n + 2] = stage1a(n + 2)
                if n + 1 < NCH:
                    ctxs[n + 1] = stage1(n + 1, gates.pop(n + 1))
                tails[n] = stage2(n, ctxs.pop(n))
                if n - 1 in tails:
                    tails.pop(n - 1)()
            tails.pop(NCH - 1)()

    def build(self, phases=("A0", "B0", "C0", "B1", "C1")):
        self.setup()
        with contextlib.ExitStack() as tes:
            if "A0" in phases:
                self.alloc_tok(tes)
                self.phase_A0()
        if "B0" in phases:
            self.phase_B0()
        if "C0" in phases:
            with contextlib.ExitStack() as tes:
                self.alloc_tok(tes)
                self.phase_C(0)
        if "B1" in phases:
            self.phase_B1()
        if "C1" in phases:
            with contextlib.ExitStack() as tes:
                self.alloc_tok(tes)
                self.phase_C(1)
        self.S.barrier()
        self.S.final_wait()
        return self.nc


MIX_CONST_SHAPES = {"dil_mask": [128, 256], "na_toe": [31, 4096], "na_colmask": [128, 64],
                    "gla_m": [6, 128, 128], "ones_row": [1, T]}


def _mix_consts():
    cs = {}
    k = np.arange(128)[:, None]
    q = np.arange(256)[None, :]
    cs["dil_mask"] = ((q >= k) & (q <= k + 128)).astype(np.float32)
    kc = np.arange(64)[:, None]
    c = np.arange(64)[None, :]
    dc = kc - c + 15
    toe = np.zeros((31, 64, 64), np.float32)
    for i in range(31):
        toe[i] = (dc == i)
    cs["na_toe"] = toe.reshape(31, 4096)
    c0 = np.clip(c - 8, 0, 48)
    ok = (kc >= c0) & (kc < c0 + 16)
    cm = np.where(ok, 0.0, NEG).astype(np.float32)
    cs["na_colmask"] = np.concatenate([cm, cm], 0)
    r = np.arange(128)[:, None]
    i = np.arange(128)[None, :]
    sc = -1.0 / 16.0
    gm = np.stack([(r <= i) * sc, (r >= i) * sc, (r > i) * sc, (r < i) * sc, (r <= i) * 1.0, (r > i) * 1.0]).astype(np.float32)
    cs["gla_m"] = gm
    cs["ones_row"] = np.ones((1, T), np.float32)
    return cs


def build_program(nseq, dbg=(), phases=("A0", "B0", "C0", "B1", "C1")):
    b1 = Builder(nseq, None, dbg)
    b1.build(phases)
    needed = b1.S.needed
    b1.es.close()
    b2 = Builder(nseq, needed, dbg)
    nc = b2.build(phases)
    return nc, b2


_PROG = {}


def kernel(x_prompt, x_sample, p_prompt, p_sample, norm_g, w_in_even, w_out_even, gla_wg_fwd, gla_bg_fwd,
           gla_wg_bwd, gla_bg_bwd, gla_norm_g, na_rpb, w_in_odd, w_out_odd, diff_lambda, diff_subln_g,
           w_ffn_gate, w_ffn_up, w_ffn_down, w_ple_proj, w_ple_gate):
    ncores, nseq = 8, 3
    f32 = lambda a: np.ascontiguousarray(np.asarray(a), dtype=np.float32)
    x_prompt, x_sample, p_prompt, p_sample = f32(x_prompt), f32(x_sample), f32(p_prompt), f32(p_sample)
    wts = {
        "norm_g": f32(norm_g), "w_in_even": f32(w_in_even)[0], "w_out_even": f32(w_out_even)[0],
        "gla_wg_fwd": f32(gla_wg_fwd)[0], "gla_bg_fwd": f32(gla_bg_fwd), "gla_wg_bwd": f32(gla_wg_bwd)[0],
        "gla_bg_bwd": f32(gla_bg_bwd), "gla_norm_g": f32(gla_norm_g), "na_rpb": f32(na_rpb).reshape(120, 31),
        "w_in_odd": f32(w_in_odd)[0], "w_out_odd": f32(w_out_odd)[0], "diff_lambda": f32(diff_lambda)[0],
        "diff_subln_g": f32(diff_subln_g), "w_ffn_gate": f32(w_ffn_gate), "w_ffn_up": f32(w_ffn_up),
        "w_ffn_down": f32(w_ffn_down), "w_ple_proj": f32(w_ple_proj), "w_ple_gate": f32(w_ple_gate),
    }
    for k, shp in WEIGHT_SHAPES.items():
        wts[k] = np.ascontiguousarray(wts[k].reshape(shp))
    consts = _consts()
    consts.update(_mix_consts())
    if "nc" not in _PROG:
        _PROG["nc"] = build_program(nseq)[0]
    nc = _PROG["nc"]
    in_maps = []
    for c in range(ncores):
        xs = np.concatenate([x_prompt[c], x_sample[2 * c], x_sample[2 * c + 1]], axis=0)
        ps = np.concatenate([p_prompt[:, c], p_sample[:, 2 * c], p_sample[:, 2 * c + 1]], axis=1)
        m = {"x": np.ascontiguousarray(xs), "p": np.ascontiguousarray(ps)}
        m.update(wts)
        m.update(consts)
        in_maps.append(m)
    res = run_bass_kernel_spmd(nc, in_maps, core_ids=list(range(ncores)))
    y_prompt = np.empty((8, T, D), np.float32)
    y_sample = np.empty((16, T, D), np.float32)
    for c in range(ncores):
        y = np.asarray(res.results[c]["y"], dtype=np.float32)
        y_prompt[c] = y[0:T]
        y_sample[2 * c] = y[T:2 * T]
        y_sample[2 * c + 1] = y[2 * T:3 * T]
    return (y_prompt, y_sample)
```
